# Optimizing a Trainium2 kernel written in Bass

```python
import jax, jax.numpy as jnp
from jax import lax
import numpy as np

D_MODEL = 1024
BATCH = 8
SEQ = 2048
DEPTH = 4
DEC_BATCH = 32
DEC_SEQ = 1
PAST_LEN = 8192
PAGE_SIZE = 128

N_HEADS_A = 8
HEAD_DIM = 64
A_WIDTH = N_HEADS_A * HEAD_DIM
DILATED_BRANCHES = ((128, 1), (512, 4), (2048, 16))
MAX_WINDOW = 2048
ROPE_THETA = 10000.0
Q_BLOCK = 128
B_WIDTH = D_MODEL // 2
N_GROUPS_B = 8
B_GROUP_DIM = B_WIDTH // N_GROUPS_B
CHUNK = 128
D_CONV = D_MODEL
CONV_WIDTH = 31
D_FF = 4 * D_MODEL
EPS = 1e-6
N_AB_LAYERS = (DEPTH + 1) // 2
N_C_LAYERS = DEPTH // 2
IN_AB_WIDTH = 3 * A_WIDTH + 2 * B_WIDTH

kernel_name = "hybrid_dilated_gmlp_conformer_step"


def rms_norm(x, g):
    xf = x.astype(jnp.float32)
    y = xf * lax.rsqrt(jnp.mean(xf * xf, axis=-1, keepdims=True) + EPS)
    return (y * g.astype(jnp.float32)).astype(x.dtype)


def layer_norm(x, g, b):
    xf = x.astype(jnp.float32)
    mu = jnp.mean(xf, axis=-1, keepdims=True)
    xc = xf - mu
    y = xc * lax.rsqrt(jnp.mean(xc * xc, axis=-1, keepdims=True) + EPS)
    return (y * g.astype(jnp.float32) + b.astype(jnp.float32)).astype(x.dtype)


def rope(x, pos):
    inv = ROPE_THETA ** (-jnp.arange(0, HEAD_DIM, 2, dtype=jnp.float32) / HEAD_DIM)
    ang = pos.astype(jnp.float32)[:, None] * inv[None, :]
    cos = jnp.cos(ang)[None, :, None, :]
    sin = jnp.sin(ang)[None, :, None, :]
    xf = x.astype(jnp.float32)
    x1, x2 = xf[..., :HEAD_DIM // 2], xf[..., HEAD_DIM // 2:]
    out = jnp.concatenate([x1 * cos - x2 * sin, x2 * cos + x1 * sin], axis=-1)
    return out.astype(x.dtype)


def masked_softmax(s, valid):
    s = jnp.where(valid, s, -jnp.inf)
    m = jnp.max(s, axis=-1, keepdims=True)
    p = jnp.exp(s - m)
    den = jnp.sum(p, axis=-1, keepdims=True)
    return p / den, (m + jnp.log(den))[..., 0]


def ab_project(h, w_in, q_g, k_g, vb_g, vb_b, pos):
    z = h @ w_in
    n, l, _ = z.shape
    q, k, v, u_b, v_b = jnp.split(z, [A_WIDTH, 2 * A_WIDTH, 3 * A_WIDTH, 3 * A_WIDTH + B_WIDTH], axis=-1)
    q = rope(rms_norm(q.reshape(n, l, N_HEADS_A, HEAD_DIM), q_g), pos)
    k = rope(rms_norm(k.reshape(n, l, N_HEADS_A, HEAD_DIM), k_g), pos)
    v = v.reshape(n, l, N_HEADS_A, HEAD_DIM)
    u_b = jax.nn.gelu(u_b)
    v_b = layer_norm(jax.nn.gelu(v_b), vb_g, vb_b)
    return q, k, v, u_b, v_b


def banded_attention(q, k, v, n_back):
    n, l, h, dh = q.shape
    nb = -(-l // Q_BLOCK)
    lp = nb * Q_BLOCK
    pad = ((0, 0), (0, lp - l), (0, 0), (0, 0))
    q, k, v = jnp.pad(q, pad), jnp.pad(k, pad), jnp.pad(v, pad)
    qb = q.reshape(n, nb, Q_BLOCK, h, dh)

    def two_blocks(t):
        cur = t.reshape(n, nb, Q_BLOCK, h, dh)
        prev = jnp.pad(t, ((0, 0), (Q_BLOCK, 0), (0, 0), (0, 0)))[:, :lp].reshape(n, nb, Q_BLOCK, h, dh)
        return jnp.concatenate([prev, cur], axis=2)

    kw, vw = two_blocks(k), two_blocks(v)
    s = jnp.einsum('nbqhd,nbkhd->nbhqk', qb, kw).astype(jnp.float32) * (dh ** -0.5)
    blk = jnp.arange(nb)[:, None, None] * Q_BLOCK
    qpos = blk + jnp.arange(Q_BLOCK)[None, :, None]
    kpos = blk - Q_BLOCK + jnp.arange(2 * Q_BLOCK)[None, None, :]
    dist = qpos - kpos
    valid = (dist >= 0) & (dist <= n_back) & (kpos >= 0)
    p, lse = masked_softmax(s, valid[None, :, None])
    o = jnp.einsum('nbhqk,nbkhd->nbqhd', p.astype(v.dtype), vw).reshape(n, lp, h, dh)[:, :l]
    lse = lse.transpose(0, 1, 3, 2).reshape(n, lp, h)[:, :l]
    return o, lse


def combine_branches(outs, lses):
    w = jax.nn.softmax(jnp.stack(lses, axis=0), axis=0)
    o = jnp.sum(w[..., None] * jnp.stack(outs, axis=0).astype(jnp.float32), axis=0)
    return o.astype(outs[0].dtype)


def dilated_attention_prompt(q, k, v):
    b, t, h, dh = q.shape
    outs, lses = [], []
    for window, dil in DILATED_BRANCHES:
        l = t // dil

        def to_sub(x):
            return x.reshape(b, l, dil, h, dh).transpose(0, 2, 1, 3, 4).reshape(b * dil, l, h, dh)

        o, lse = banded_attention(to_sub(q), to_sub(k), to_sub(v), window // dil)
        outs.append(o.reshape(b, dil, l, h, dh).transpose(0, 2, 1, 3, 4).reshape(b, t, h, dh))
        lses.append(lse.reshape(b, dil, l, h).transpose(0, 2, 1, 3).reshape(b, t, h))
    return combine_branches(outs, lses)


def dilated_attention_sample(q, k_all, v_all, q_pos, buf_start):
    outs, lses = [], []
    for window, dil in DILATED_BRANCHES:
        n_back = window // dil
        key_pos = q_pos[:, None] - dil * jnp.arange(n_back + 1)[None, :]
        valid = key_pos >= 0
        idx = jnp.clip(key_pos - buf_start, 0, k_all.shape[1] - 1)
        kg = jnp.take(k_all, idx, axis=1)
        vg = jnp.take(v_all, idx, axis=1)
        s = jnp.einsum('nshd,nskhd->nhsk', q, kg).astype(jnp.float32) * (HEAD_DIM ** -0.5)
        p, lse = masked_softmax(s, valid[None, None])
        outs.append(jnp.einsum('nhsk,nskhd->nshd', p.astype(vg.dtype), vg))
        lses.append(lse.transpose(0, 2, 1))
    return combine_branches(outs, lses)


def spatial_gate(u, v, w_s, b_s):
    n, l, _ = v.shape
    nc = -(-l // CHUNK)
    lp = nc * CHUNK
    vp = jnp.pad(v, ((0, 0), (0, lp - l), (0, 0))).reshape(n, nc, CHUNK, N_GROUPS_B, B_GROUP_DIM)
    mask = jnp.tril(jnp.ones((CHUNK, CHUNK), dtype=bool))
    w = jnp.where(mask[None], w_s, jnp.zeros_like(w_s))
    mixed = jnp.einsum('gij,ncjgd->ncigd', w, vp) + b_s.T[None, None, :, :, None]
    mixed = mixed.reshape(n, lp, B_WIDTH)[:, :l]
    return u * mixed


def conv_module(h, buf, w_in, w_dw, b_dw, g, b, w_out):
    z = h @ w_in
    a, gate = jnp.split(z, 2, axis=-1)
    x = a * jax.nn.sigmoid(gate)
    xc = jnp.concatenate([buf.astype(x.dtype), x], axis=1)
    y = lax.conv_general_dilated(xc, w_dw[:, None, :], window_strides=(1,), padding='VALID',
                                 dimension_numbers=('NWC', 'WIO', 'NWC'), feature_group_count=D_CONV) + b_dw
    y = jax.nn.silu(layer_norm(y, g, b))
    return y @ w_out, xc[:, -(CONV_WIDTH - 1):]


def ffn(h, w_up, w_down):
    return jnp.square(jax.nn.relu(h @ w_up)) @ w_down


def setup_inputs(seed: int = 0) -> dict:
    key = jax.random.key(seed)
    ks = jax.random.split(key, 24)
    win_buf = min(MAX_WINDOW, PAST_LEN)

    def nrm(k, shape, scale):
        return jax.random.normal(k, shape, jnp.float32) * scale

    def gain(k, shape):
        return 1.0 + 0.05 * jax.random.normal(k, shape, jnp.float32)

    return {
        "x_prompt": nrm(ks[0], (BATCH, SEQ, D_MODEL), 1.0),
        "x_sample": nrm(ks[1], (DEC_BATCH, DEC_SEQ, D_MODEL), 1.0),
        "cache_a_k": nrm(ks[2], (N_AB_LAYERS, DEC_BATCH, win_buf, N_HEADS_A, HEAD_DIM), 1.0),
        "cache_a_v": nrm(ks[3], (N_AB_LAYERS, DEC_BATCH, win_buf, N_HEADS_A, HEAD_DIM), 1.0),
        "state_c_conv": nrm(ks[4], (N_C_LAYERS, DEC_BATCH, CONV_WIDTH - 1, D_CONV), 0.5),
        "norm_mix_g": gain(ks[5], (DEPTH, D_MODEL)),
        "norm_ffn_g": gain(ks[6], (DEPTH, D_MODEL)),
        "w_ffn_up": nrm(ks[7], (DEPTH, D_MODEL, D_FF), D_MODEL ** -0.5),
        "w_ffn_down": nrm(ks[8], (DEPTH, D_FF, D_MODEL), D_FF ** -0.5),
        "w_in_ab": nrm(ks[9], (N_AB_LAYERS, D_MODEL, IN_AB_WIDTH), D_MODEL ** -0.5),
        "q_norm_g": gain(ks[10], (N_AB_LAYERS, HEAD_DIM)),
        "k_norm_g": gain(ks[11], (N_AB_LAYERS, HEAD_DIM)),
        "vb_norm_g": gain(ks[12], (N_AB_LAYERS, B_WIDTH)),
        "vb_norm_b": nrm(ks[13], (N_AB_LAYERS, B_WIDTH), 0.02),
        "w_spatial": nrm(ks[14], (N_AB_LAYERS, N_GROUPS_B, CHUNK, CHUNK), CHUNK ** -0.5),
        "b_spatial": gain(ks[15], (N_AB_LAYERS, N_GROUPS_B, CHUNK)),
        "w_out_ab": nrm(ks[16], (N_AB_LAYERS, A_WIDTH + B_WIDTH, D_MODEL), (A_WIDTH + B_WIDTH) ** -0.5),
        "w_c_in": nrm(ks[17], (N_C_LAYERS, D_MODEL, 2 * D_CONV), D_MODEL ** -0.5),
        "w_c_dw": nrm(ks[18], (N_C_LAYERS, CONV_WIDTH, D_CONV), CONV_WIDTH ** -0.5),
        "b_c_dw": nrm(ks[19], (N_C_LAYERS, D_CONV), 0.02),
        "c_norm_g": gain(ks[20], (N_C_LAYERS, D_CONV)),
        "c_norm_b": nrm(ks[21], (N_C_LAYERS, D_CONV), 0.02),
        "w_c_out": nrm(ks[22], (N_C_LAYERS, D_CONV, D_MODEL), D_CONV ** -0.5),
    }


def reference(x_prompt, x_sample, cache_a_k, cache_a_v, state_c_conv, norm_mix_g, norm_ffn_g, w_ffn_up, w_ffn_down,
              w_in_ab, q_norm_g, k_norm_g, vb_norm_g, vb_norm_b, w_spatial, b_spatial, w_out_ab,
              w_c_in, w_c_dw, b_c_dw, c_norm_g, c_norm_b, w_c_out):
    pos_p = jnp.arange(SEQ, dtype=jnp.int32)
    pos_s = PAST_LEN + jnp.arange(DEC_SEQ, dtype=jnp.int32)
    win_buf = cache_a_k.shape[2]
    buf_start = PAST_LEN - win_buf
    prompt_buf = min(MAX_WINDOW, SEQ)
    last_chunk_start = ((SEQ - 1) // CHUNK) * CHUNK

    hp, hs = x_prompt, x_sample
    ak_p, av_p, ak_s, av_s, bv_p, bv_s, cc_p, cc_s = [], [], [], [], [], [], [], []
    for layer in range(DEPTH):
        j = layer // 2
        n_p = rms_norm(hp, norm_mix_g[layer])
        n_s = rms_norm(hs, norm_mix_g[layer])
        if layer % 2 == 0:
            qp, kp, vp, up, vbp = ab_project(n_p, w_in_ab[j], q_norm_g[j], k_norm_g[j], vb_norm_g[j], vb_norm_b[j], pos_p)
            qs, kss, vss, us, vbs = ab_project(n_s, w_in_ab[j], q_norm_g[j], k_norm_g[j], vb_norm_g[j], vb_norm_b[j], pos_s)
            att_p = dilated_attention_prompt(qp, kp, vp).reshape(BATCH, SEQ, A_WIDTH)
            k_all = jnp.concatenate([cache_a_k[j].astype(kss.dtype), kss], axis=1)
            v_all = jnp.concatenate([cache_a_v[j].astype(vss.dtype), vss], axis=1)
            att_s = dilated_attention_sample(qs, k_all, v_all, pos_s, buf_start).reshape(DEC_BATCH, DEC_SEQ, A_WIDTH)
            gate_p = spatial_gate(up, vbp, w_spatial[j], b_spatial[j])
            gate_s = spatial_gate(us, vbs, w_spatial[j], b_spatial[j])
            mix_p = jnp.concatenate([att_p, gate_p], axis=-1) @ w_out_ab[j]
            mix_s = jnp.concatenate([att_s, gate_s], axis=-1) @ w_out_ab[j]
            ak_p.append(kp[:, SEQ - prompt_buf:])
            av_p.append(vp[:, SEQ - prompt_buf:])
            ak_s.append(kss)
            av_s.append(vss)
            bv_p.append(vbp[:, last_chunk_start:])
            bv_s.append(vbs)
        else:
            zero_buf = jnp.zeros((BATCH, CONV_WIDTH - 1, D_CONV), dtype=n_p.dtype)
            mix_p, new_cp = conv_module(n_p, zero_buf, w_c_in[j], w_c_dw[j], b_c_dw[j], c_norm_g[j], c_norm_b[j], w_c_out[j])
            mix_s, new_cs = conv_module(n_s, state_c_conv[j], w_c_in[j], w_c_dw[j], b_c_dw[j], c_norm_g[j], c_norm_b[j], w_c_out[j])
            cc_p.append(new_cp)
            cc_s.append(new_cs)
        hp = hp + mix_p
        hs = hs + mix_s
        hp = hp + ffn(rms_norm(hp, norm_ffn_g[layer]), w_ffn_up[layer], w_ffn_down[layer])
        hs = hs + ffn(rms_norm(hs, norm_ffn_g[layer]), w_ffn_up[layer], w_ffn_down[layer])

    return (hp, hs, jnp.stack(ak_p), jnp.stack(av_p), jnp.stack(ak_s), jnp.stack(av_s),
            jnp.stack(bv_p), jnp.stack(bv_s), jnp.stack(cc_p), jnp.stack(cc_s))
```

```python
from contextlib import ExitStack
import numpy as np
import concourse.bass as bass
import concourse.mybir as mybir
from concourse.bass_utils import run_bass_kernel_spmd

F32 = mybir.dt.float32
BF16 = mybir.dt.bfloat16
AF = mybir.ActivationFunctionType
ALU = mybir.AluOpType
AX = mybir.AxisListType

NCORES = 8
LAYERS = 4
STRICT = True
SKIP = set()
D = 1024
SEQ = 2048
NS = 4
T = SEQ + NS
TILES = [(0, 512), (512, 512), (1024, 512), (1536, 512), (2048, NS)]
EPS = 1e-6
PAST = 8192
WINBUF = 2048
DILS = (1, 4, 16)

C_ID, C_ROT, C_ONE, C_PM = 0, 128, 256, 384
C_PK = 392
PK_MIXG, PK_FFNG, PK_QG, PK_KG, PK_BDW, PK_CNG, PK_CNB, PK_WDW, PK_WS00, PK_BS0 = 0, 32, 64, 66, 68, 84, 100, 116, 612, 620
NPK = 628
NCF = C_PK + NPK
B_ID, B_HM, B_ODM, B_ONE64, B_ZER, B_MASK = 0, 128, 256, 384, 448, 576
B_EH = B_MASK + 6 * 512
B_MULT = B_EH + 256
NCB = B_MULT + 512
ARN = 33600


def ss(start, n, step=1):
    return slice(start, start + step * (n - 1) + 1, step)


class Op:
    __slots__ = ("eng", "fn", "deps", "marked", "val", "isdma", "sem", "prev")

    def __init__(self, eng, fn):
        self.eng = eng
        self.fn = fn
        self.deps = []
        self.marked = False
        self.val = 0
        self.isdma = False
        self.sem = None
        self.prev = 0


class Prog:
    ENGS = ("pe", "act", "dve", "pool", "sp")

    def __init__(self, nc, esems, dsems):
        self.nc = nc
        self.esem = esems
        self.dsems = dsems
        self.dcnt = {q: [0] * len(v) for q, v in dsems.items()}
        self.drr = {q: 0 for q in dsems}
        self.dlast = {q: [None] * len(v) for q, v in dsems.items()}
        self.ops = {e: [] for e in self.ENGS}
        self.last_w = {}
        self.readers = {}
        self.bar = []
        self.bar_seen = {e: True for e in self.ENGS}
        self.nops = 0

    def _deps(self, op, reads, writes):
        deps = []
        for r in reads:
            w = self.last_w.get(r)
            if w is not None:
                deps.append(w)
        for w in writes:
            lw = self.last_w.get(w)
            if lw is not None and (STRICT or lw.isdma or op.isdma or lw.eng != op.eng):
                deps.append(lw)
            for rd in self.readers.get(w, ()):
                if STRICT or rd.isdma or op.isdma or rd.eng != op.eng:
                    deps.append(rd)
        if not self.bar_seen[op.eng]:
            deps.extend(self.bar)
            self.bar_seen[op.eng] = True
        seen = set()
        for d in deps:
            if d is op or id(d) in seen:
                continue
            seen.add(id(d))
            if (not d.isdma) and d.eng == "pe" and op.eng == "pe" and not op.isdma:
                continue
            op.deps.append(d)
            d.marked = True
        for r in reads:
            self.readers.setdefault(r, []).append(op)
        for w in writes:
            self.last_w[w] = op
            self.readers[w] = []

    def op(self, eng, fn, reads=(), writes=()):
        o = Op(eng, fn)
        self._deps(o, reads, writes)
        self.ops[eng].append(o)
        self.nops += 1
        return o

    def dma(self, q, out, in_, reads=(), writes=()):
        o = Op(q, lambda e, out=out, in_=in_: e.dma_start(out=out, in_=in_))
        o.isdma = True
        i = self.drr[q]
        self.drr[q] = (i + 1) % len(self.dsems[q])
        o.sem = self.dsems[q][i]
        o.prev = self.dcnt[q][i]
        self.dcnt[q][i] += 16
        o.val = self.dcnt[q][i]
        self.dlast[q][i] = o
        self._deps(o, reads, writes)
        self.ops[q].append(o)
        self.nops += 1
        return o

    def barrier(self):
        b = []
        for e in self.ENGS:
            for o in reversed(self.ops[e]):
                if not o.isdma:
                    b.append(o)
                    o.marked = True
                    break
        for q in self.dlast:
            for o in self.dlast[q]:
                if o is not None:
                    b.append(o)
        self.bar = b
        self.bar_seen = {e: False for e in self.ENGS}

    def emit(self, block):
        for e in self.ENGS:
            c = 0
            for o in self.ops[e]:
                if not o.isdma and o.marked:
                    c += 1
                    o.val = c
                    o.sem = self.esem[e]
        binder = {"pe": block.tensor, "act": block.scalar, "dve": block.vector, "pool": block.gpsimd, "sp": block.sync}
        for e in self.ENGS:
            ops = self.ops[e]
            dsems = self.dsems
            dcnt = self.dcnt

            def body(eng, ops=ops, e=e):
                waited = {}
                for o in ops:
                    for d in o.deps:
                        k = id(d.sem)
                        if waited.get(k, 0) >= d.val:
                            continue
                        eng.wait_ge(d.sem, d.val)
                        waited[k] = d.val
                    if o.isdma:
                        k = id(o.sem)
                        if o.prev > 0 and waited.get(k, 0) < o.prev:
                            eng.wait_ge(o.sem, o.prev)
                            waited[k] = o.prev
                        o.fn(eng).then_inc(o.sem, 16)
                    else:
                        ins = o.fn(eng)
                        if o.marked:
                            ins.then_inc(o.sem, 1)
                if e in ("sp", "pool"):
                    for q in dsems:
                        for i, s in enumerate(dsems[q]):
                            if dcnt[q][i] > 0:
                                eng.wait_ge(s, dcnt[q][i])

            binder[e](body)


def build_program():
    nc = bass.Bass("TRN2", target_bir_lowering=False)

    def din(name, shape):
        return nc.dram_tensor(name, list(shape), F32, kind="ExternalInput").ap()

    def dout(name, shape):
        return nc.dram_tensor(name, list(shape), F32, kind="ExternalOutput").ap()

    xp = din("xp", [SEQ, D])
    xs = din("xs", [NS, D])
    ck = din("ck", [2, NS, WINBUF, 512])
    cv = din("cv", [2, NS, WINBUF, 512])
    stc = din("stc", [2, NS, 30, D])
    w_up = din("w_up", [4, D, 4 * D])
    w_dn = din("w_dn", [4, 4 * D, D])
    w_in = din("w_in", [2, D, 2560])
    w_out = din("w_out", [2, D, D])
    w_ci = din("w_ci", [2, D, 2 * D])
    w_co = din("w_co", [2, D, D])
    cf_d = din("cf", [128, NCF])
    cb_d = din("cb", [128, NCB])
    cos_d = din("cosT", [128, T])
    sin_d = din("sinT", [128, T])
    wsT_d = din("wsT", [2, 128, 8 * 128])
    bT_d = din("bT", [2, 128, 4 * 128])
    vbg_d = din("vbg", [2, 128, 512])
    vbb_d = din("vbb", [2, 128, 512])

    y_p = dout("y_p", [SEQ, D])
    y_s = dout("y_s", [NS, D])
    ak_p = dout("ak_p", [2, SEQ, 512])
    av_p = dout("av_p", [2, SEQ, 512])
    ak_s = dout("ak_s", [2, NS, 512])
    av_s = dout("av_s", [2, NS, 512])
    bv_p = dout("bv_p", [2, 128, 512])
    bv_s = dout("bv_s", [2, NS, 512])
    cc_p = dout("cc_p", [2, 30, D])
    cc_s = dout("cc_s", [2, NS, 30, D])

    with ExitStack() as es:
        def sb(name, shape, dt):
            return es.enter_context(nc.sbuf_tensor(name, list(shape), dt))

        XT = sb("XT", [128, 8, T], F32)
        XN = sb("XN", [128, 8, T], BF16)
        RING = sb("RING", [128, 4, 4096], BF16)
        CFS = sb("CFS", [128, NCF], F32)
        CBS = sb("CBS", [128, NCB], BF16)
        AR = sb("AR", [128, ARN], BF16)
        PS = [es.enter_context(nc.psum_tensor(f"ps{i}", [128, 512], F32)) for i in range(8)]
        esems = {e: es.enter_context(nc.semaphore(f"se_{e}")) for e in Prog.ENGS}
        dsems = {"sp": [es.enter_context(nc.semaphore(f"sd_sp{i}")) for i in range(12)],
                 "pool": [es.enter_context(nc.semaphore(f"sd_pl{i}")) for i in range(8)]}
        block = es.enter_context(nc.Block())
        P = Prog(nc, esems, dsems)

        IDF = CFS[:, C_ID:C_ID + 128]
        ROT = CFS[:, C_ROT:C_ROT + 128]
        ONEF = CFS[:, C_ONE:C_ONE + 128]
        PMASK = CFS[:, C_PM:C_PM + 8]
        IDB = CBS[:, B_ID:B_ID + 128]
        HM = CBS[:, B_HM:B_HM + 128]
        ODM = CBS[:, B_ODM:B_ODM + 128]
        ONE64 = CBS[:, B_ONE64:B_ONE64 + 64]
        ZER = CBS[:, B_ZER:B_ZER + 128]

        def EH(h):
            return CBS[:, B_EH + 128 * h:B_EH + 128 * (h + 1)]

        def MASK(i):
            return CBS[:, B_MASK + 512 * i:B_MASK + 512 * (i + 1)]

        def pk(col, n=1):
            return CFS[:, C_PK + col:C_PK + col + n]

        arpos = [0]

        def ar_reset(pos=0):
            arpos[0] = pos

        def ar(n_el, dt=BF16):
            nb = n_el * (2 if dt == F32 else 1)
            nb = (nb + 7) // 8 * 8
            a = arpos[0]
            assert a + nb <= ARN, ("arena overflow", a, nb)
            arpos[0] = a + nb
            v = AR[:, a:a + nb]
            if dt == F32:
                v = v.bitcast(F32)
            return v[:, 0:n_el]

        bank_rot = {"list": list(range(8)), "i": 0}

        def bank():
            l = bank_rot["list"]
            i = l[bank_rot["i"] % len(l)]
            bank_rot["i"] += 1
            return PS[i], f"ps{i}"

        def set_banks(l):
            bank_rot["list"] = list(l)
            bank_rot["i"] = 0

        def mm(out, lhsT, rhs, start, stop, reads, writes):
            return P.op("pe", lambda e: e.matmul(out, lhsT=lhsT, rhs=rhs, start=start, stop=stop), reads, writes)

        def tr(out, in_, ident, reads, writes):
            return P.op("pe", lambda e: e.transpose(out=out, in_=in_, identity=ident), reads, writes)

        def act(out, in_, func, reads, writes, scale=None, bias=None):
            kw = {}
            if scale is not None:
                kw["scale"] = scale
            if bias is not None:
                kw["bias"] = bias
            return P.op("act", lambda e: e.activation(out=out, in_=in_, func=func, **kw), reads, writes)

        def tt(eng, out, in0, in1, op, reads, writes):
            return P.op(eng, lambda e: e.tensor_tensor(out=out, in0=in0, in1=in1, op=op), reads, writes)

        def stt(out, in0, scalar, in1, op0, op1, reads, writes):
            return P.op("dve", lambda e: e.scalar_tensor_tensor(out=out, in0=in0, scalar=scalar, in1=in1, op0=op0, op1=op1), reads, writes)

        def tsc(eng, out, in0, s1, s2, op0, op1, reads, writes):
            if op1 is None:
                return P.op(eng, lambda e: e.tensor_scalar(out=out, in0=in0, scalar1=s1, scalar2=None, op0=op0), reads, writes)
            return P.op(eng, lambda e: e.tensor_scalar(out=out, in0=in0, scalar1=s1, scalar2=s2, op0=op0, op1=op1), reads, writes)

        def cp(eng, out, in_, reads, writes):
            if eng == "act":
                return act(out, in_, AF.Copy, reads, writes)
            return P.op(eng, lambda e: e.tensor_copy(out=out, in_=in_), reads, writes)

        def memset(eng, ap, val, writes):
            return P.op(eng, lambda e: e.memset(ap, val), (), writes)

        def rstd_from(ps_ap, out_ap, reads, wkey):
            act(out_ap, ps_ap, AF.Ln, reads, [wkey], bias=EPS)
            act(out_ap, out_ap, AF.Exp, [wkey], [wkey], scale=-0.5)

        def wslot(i):
            return RING[:, i, :]

        P.dma("sp", CFS[:, :], cf_d[:, :], (), ["CFS"])
        for i0 in range(0, NCB, 1472):
            P.dma("pool", CBS[:, i0:i0 + 1472], cb_d[:, i0:i0 + 1472], (), ["CBS"])
        ar_reset()
        XSI = ar(D, F32)
        XIN = [ar(D, F32) for _ in range(15)]
        for b in range(16):
            xin = XIN[b % 15]
            P.dma("sp", xin[:, :], xp[b * 128:(b + 1) * 128, :], (), [f"XIN{b % 15}"])
            for half in range(2):
                ps, pk_ = bank()
                for cc in range(4):
                    c = half * 4 + cc
                    tr(ps[:, cc * 128:(cc + 1) * 128], xin[:, c * 128:(c + 1) * 128], IDF, [f"XIN{b % 15}", "CFS"], [pk_])
                cp("act" if half == 0 else "dve", XT[:, half * 4:half * 4 + 4, b * 128:(b + 1) * 128],
                   ps[:, :].rearrange("p (c n) -> p c n", c=4), [pk_], [f"XT{b // 4}_{half * 4 + q}" for q in range(4)])
        P.dma("sp", XSI[0:NS, :], xs[:, :], (), ["XSI"])
        ps, pk_ = bank()
        for c in range(8):
            tr(ps[:, c * NS:(c + 1) * NS], XSI[0:NS, c * 128:(c + 1) * 128], IDF[0:NS, 0:NS], ["XSI", "CFS"], [pk_])
        cp("act", XT[:, :, SEQ:T], ps[:, 0:8 * NS].rearrange("p (c n) -> p c n", c=8), [pk_], [f"XT4_{q}" for q in range(8)])

        TILES_F = [(i * 342, 342) for i in range(6)]

        def rmsnorm(gbase, SQ, RS, tiles=TILES, xk="XT", nk="XN", extra=None):
            for _ in rmsnorm_gen(gbase, SQ, RS, tiles, xk, nk, extra):
                pass

        def rmsnorm_gen(gbase, SQ, RS, tiles=TILES, xk="XT", nk="XN", extra=None):
            for ti, (t0, n) in enumerate(tiles):
                ex = extra(ti) if extra is not None else (lambda c: [])
                act(SQ[:, :, 0:n], XT[:, :, t0:t0 + n], AF.Square,
                    [f"{xk}{ti}_{c}" for c in range(8)] + [k_ for c in range(8) for k_ in ex(c)], ["SQ"])
                ps, pk_ = bank()
                for c in range(8):
                    mm(ps[:, 0:n], ODM, SQ[:, c, 0:n], c == 0, c == 7, ["SQ", "CBS"], [pk_])
                rstd_from(ps[:, 0:n], RS[:, 0:n], [pk_], "RS")
                for c in range(8):
                    stt(XN[:, c, t0:t0 + n], XT[:, c, t0:t0 + n], pk(gbase + c), RS[:, 0:n], ALU.mult, ALU.mult,
                        [f"{xk}{ti}_{c}", "RS", "CFS"] + ex(c), [f"{nk}{ti}"])
                yield

        def load_w(slot_ap, dram_ap, key):
            P.dma("pool", slot_ap, dram_ap, (), [key])

        def ffn_loadG(layer, G, first):
            wu = w_up[layer].rearrange("(k p) f -> p k f", p=128)
            wd = w_dn[layer].rearrange("(f p) j -> p f j", p=128)
            su, sd = (first + 2 * G) % 4, (first + 2 * G + 1) % 4
            load_w(wslot(su).rearrange("p (k f) -> p k f", k=8), wu[:, :, G * 512:(G + 1) * 512], f"W{su}")
            load_w(wslot(sd).rearrange("p (f j) -> p f j", f=4), wd[:, G * 4:(G + 1) * 4, :], f"W{sd}")

        def ffn(layer, first, abase=None):
            def ov(ti):
                a_, n_ = TILES_F[ti]
                return [t for t, (b_, m_) in enumerate(TILES) if b_ < a_ + n_ and a_ < b_ + m_]
            if abase is None:
                P.barrier()
                ar_reset()
                extra = None
                xkeys = lambda ti, j: []
            else:
                ar_reset(abase)
                extra = lambda ti: (lambda c: [f"XT{t}_{c}" for t in ov(ti)])
                xkeys = lambda ti, j: [f"XT{t}_{j}" for t in ov(ti)]
            SQ = ar(8 * 512).rearrange("p (c n) -> p c n", c=8)
            RS = ar(512, F32)
            R = [ar(512, F32), ar(512, F32)]
            H = [ar(4 * 512).rearrange("p (f n) -> p f n", f=4) for _ in range(2)]
            set_banks(range(8))
            ng = rmsnorm_gen(PK_FFNG + 8 * layer, SQ, RS, TILES_F, "XF", "XNF", extra)
            next(ng)
            hb = 0
            prevd = None
            for G in range(8):
                su, sd = (first + 2 * G) % 4, (first + 2 * G + 1) % 4
                WU = wslot(su).rearrange("p (k f) -> p k f", k=8)
                WD = wslot(sd).rearrange("p (f j) -> p f j", f=4)

                def up_part(ti, t0, n, Hc, hk, WU=WU, su=su):
                    for f in range(4):
                        ps, pk_ = bank()
                        for k in range(8):
                            mm(ps[:, 0:n], WU[:, k, f * 128:(f + 1) * 128], XN[:, k, t0:t0 + n], k == 0, k == 7,
                               [f"W{su}", f"XNF{ti}"], [pk_])
                        r = R[f % 2]
                        act(r[:, 0:n], ps[:, 0:n], AF.Relu, [pk_], [f"R{f % 2}"])
                        act(Hc[:, f, 0:n], r[:, 0:n], AF.Square, [f"R{f % 2}"], [f"{hk}_{f}"])

                def down_part(ti, t0, n, Hc, hk, WD=WD, sd=sd):
                    for j in range(8):
                        ps, pk_ = bank()
                        for f in range(4):
                            mm(ps[:, 0:n], WD[:, f, j * 128:(j + 1) * 128], Hc[:, f, 0:n], f == 0, f == 3,
                               [f"W{sd}", f"{hk}_{f}"], [pk_])
                        tt("dve", XT[:, j, t0:t0 + n], XT[:, j, t0:t0 + n], ps[:, 0:n], ALU.add, [pk_, f"XF{ti}_{j}"] + xkeys(ti, j), [f"XF{ti}_{j}"])

                for ti, (t0, n) in enumerate(TILES_F):
                    Hc, hk = H[hb], f"H{hb}"
                    hb ^= 1
                    up_part(ti, t0, n, Hc, hk)
                    if G == 0:
                        next(ng, None)
                    if prevd is not None:
                        prevd[0](*prevd[1])
                    prevd = (down_part, (ti, t0, n, Hc, hk))
                    if ti == 0 and G + 1 < 8:
                        ffn_loadG(layer, G + 1, first)
            prevd[0](*prevd[1])

        AB_BASE = [0]

        def ab_layer(layer):
            jl = layer // 2
            P.barrier()
            ar_reset()
            COS = ar(T, F32)
            SIN = ar(T, F32)
            ATT = ar(4 * T).rearrange("p (c n) -> p c n", c=4)
            RS = ar(512, F32)
            base = arpos[0]
            AB_BASE[0] = base
            SQ = RING[:, 3, :].rearrange("p (c n) -> p c n", c=8)
            set_banks(range(8))
            P.dma("sp", COS[:, :], cos_d[:, :], (), ["COS"])
            P.dma("sp", SIN[:, :], sin_d[:, :], (), ["SIN"])
            win = w_in[jl].rearrange("(k p) f -> p k f", p=128)
            wo = w_out[jl].rearrange("(k p) j -> p k j", p=128)
            WU = wslot(0).rearrange("p (k f) -> p k f", k=8)
            WVB = wslot(1).rearrange("p (k f) -> p k f", k=8)
            WOG = wslot(2).rearrange("p (k j) -> p k j", k=4)
            load_w(WU, win[:, :, 1536:2048], "W0")
            load_w(WVB, win[:, :, 2048:2560], "W1")
            load_w(WOG, wo[:, 4:8, :], "W2")
            ngab = rmsnorm_gen(PK_MIXG + 8 * layer, SQ, RS)
            next(ngab)
            ar_reset(base)
            UT = ar(4 * 512).rearrange("p (c n) -> p c n", c=4)
            VBT = ar(4 * 512).rearrange("p (b f) -> p b f", b=4)
            VGL = [ar(512, F32), ar(512, F32)]
            TMPG = ar(512, F32).rearrange("p (c i) -> p c i", c=4)
            WS = ar(8 * 128).rearrange("p (g i) -> p g i", g=8)
            WSF = ar(8 * 128, F32)
            BT = ar(4 * 128, F32).rearrange("p (c i) -> p c i", c=4)
            VBGt = ar(512, F32)
            VBBt = ar(512, F32)
            ST6L = [ar(8, F32), ar(8, F32)]
            MVL = [ar(4, F32), ar(4, F32)]
            vgc = [0]
            VST4 = ar(4 * NS, F32).rearrange("p (c n) -> p c n", c=4)
            P.dma("sp", WSF[:, :], wsT_d[jl], (), ["WSF"])
            P.dma("sp", BT[:, :, :], bT_d[jl].rearrange("p (c i) -> p c i", c=4), (), ["BT"])
            P.dma("sp", VBGt[:, :], vbg_d[jl], (), ["VBG"])
            P.dma("sp", VBBt[:, :], vbb_d[jl], (), ["VBB"])
            for q4 in range(2):
                tt("dve", WS[:, q4 * 4:(q4 + 1) * 4, :].rearrange("p g i -> p (g i)"), WSF[:, q4 * 512:(q4 + 1) * 512],
                   CBS[:, B_MULT:B_MULT + 512], ALU.mult, ["WSF", "CBS"], ["WS"])
            def gate_vb(ti, t0, n):
                nblk = 4 if ti < 4 else 1
                for bl in range(nblk):
                    m = 128 if ti < 4 else NS
                    c0 = t0 + bl * 128
                    vi = vgc[0] % 2
                    vgc[0] += 1
                    VG, ST6, MV = VGL[vi], ST6L[vi], MVL[vi]
                    kvg, kst, kmv = f"VG{vi}", f"ST6{vi}", f"MV{vi}"
                    ps, pk_ = bank()
                    for k in range(8):
                        mm(ps[0:m, :], XN[:, k, c0:c0 + m], WVB[:, k, :], k == 0, k == 7, ["W1", f"XN{ti}"], [pk_])
                    act(VG[0:m, :], ps[0:m, :], AF.Gelu_apprx_tanh, [pk_], [kvg])
                    P.op("dve", lambda e, m=m, ST6=ST6, VG=VG: e.bn_stats(out=ST6[0:m, 0:6], in_=VG[0:m, :]), [kvg], [kst])
                    P.op("dve", lambda e, m=m, ST6=ST6, MV=MV: e.bn_aggr(out=MV[0:m, 0:2], in_=ST6[0:m, 0:6]), [kst], [kmv])
                    rstd_from(MV[0:m, 1:2], MV[0:m, 2:3], [kmv], kmv)
                    tsc("dve", VG[0:m, :], VG[0:m, :], MV[0:m, 0:1], MV[0:m, 2:3], ALU.subtract, ALU.mult, [kvg, kmv], [kvg])
                    tt("dve", VG[0:m, :], VG[0:m, :], VBGt[0:m, :], ALU.mult, [kvg, "VBG"], [kvg])
                    tt("dve", VG[0:m, :], VG[0:m, :], VBBt[0:m, :], ALU.add, [kvg, "VBB"], [kvg])
                    if ti < 4:
                        cp("act", VBT[:, bl, :], VG[:, :], [kvg], [f"VBT{bl}"])
                        if ti == 3 and bl == 3:
                            P.dma("sp", bv_p[jl], VG[:, :], [kvg], [])
                    else:
                        P.dma("sp", bv_s[jl], VG[0:NS, :], [kvg], [])
                        ps2, pk2 = bank()
                        for c in range(4):
                            tr(ps2[:, c * NS:(c + 1) * NS], VG[0:NS, c * 128:(c + 1) * 128], IDF[0:NS, 0:NS], [kvg, "CFS"], [pk2])
                        for c in range(4):
                            tsc("dve", VST4[:, c, :], ps2[:, c * NS:(c + 1) * NS], pk(PK_WS00 + 4 * jl + c), pk(PK_BS0 + 4 * jl + c),
                                ALU.mult, ALU.add, [pk2, "CFS"], ["VST4"])
            def gate_u(ti, t0, n):
                for c in range(4):
                    ps, pk_ = bank()
                    for k in range(8):
                        mm(ps[:, 0:n], WU[:, k, c * 128:(c + 1) * 128], XN[:, k, t0:t0 + n], k == 0, k == 7, ["W0", f"XN{ti}"], [pk_])
                    act(UT[:, c, 0:n], ps[:, 0:n], AF.Gelu_apprx_tanh, [pk_], ["UT"])
                if ti == 4:
                    tt("dve", UT[:, :, 0:NS], UT[:, :, 0:NS], VST4[:, :, :], ALU.mult, ["UT", "VST4"], ["UT"])
            def gate_sp(ti, t0, n):
                if ti < 4:
                    for bl in range(4):
                        ps, pk_ = bank()
                        for g in range(8):
                            hp = (g % 2) * 64
                            cc = g // 2
                            mm(ps[hp:hp + 64, cc * 128:(cc + 1) * 128], VBT[:, bl, g * 64:(g + 1) * 64], WS[:, g, :], True, True,
                               [f"VBT{bl}", "WS"], [pk_])
                        tt("dve", TMPG[:, :, :], ps[:, :].rearrange("p (c i) -> p c i", c=4), BT[:, :, :], ALU.add, [pk_, "BT"], ["TMPG"])
                        tt("dve", UT[:, :, bl * 128:(bl + 1) * 128], UT[:, :, bl * 128:(bl + 1) * 128], TMPG[:, :, :], ALU.mult,
                           ["UT", "TMPG"], ["UT"])
            def gate_op(ti, t0, n):
                for j in range(8):
                    ps, pk_ = bank()
                    for k in range(4):
                        mm(ps[:, 0:n], WOG[:, k, j * 128:(j + 1) * 128], UT[:, k, 0:n], k == 0, k == 3, ["W2", "UT"], [pk_])
                    tt("dve", XT[:, j, t0:t0 + n], XT[:, j, t0:t0 + n], ps[:, 0:n], ALU.add, [pk_, f"XT{ti}_{j}"], [f"XT{ti}_{j}"])

            gt = TILES if 'gate' not in SKIP else []
            for ti, (t0, n) in enumerate(gt):
                gate_vb(ti, t0, n)
                next(ngab, None)
                if ti > 0:
                    gate_op(ti - 1, *gt[ti - 1])
                gate_u(ti, t0, n)
                gate_sp(ti, t0, n)
            if gt:
                gate_op(len(gt) - 1, *gt[-1])
            for _ in ngab:
                pass
            P.barrier()
            ar_reset(base)
            QTZ = [ar(T), ar(T)]
            KT = ar(T)
            A2 = [ar(512, F32), ar(512, F32)]
            Bq = ar(512)
            C2 = [ar(512, F32), ar(512, F32)]
            Dd = ar(512, F32)
            RDEN = ar(512, F32)
            KST = RDEN
            VSTG = RDEN
            Kc = ar(4 * 128, F32).rearrange("p (s f) -> p s f", s=4)
            Vc = ar(4 * 128, F32).rearrange("p (s f) -> p s f", s=4)
            ROW = ar(384, F32)
            PROD = Dd[:, :].rearrange("p (s f) -> p s f", s=4)
            SS = ar(8, F32)
            PM = ar(8, F32)
            RD = ar(2, F32)
            DEN2 = ar(2, F32)
            OROW = ar(128, F32)
            QSF = ar(NS, F32)
            KSF = ar(NS, F32)
            PT = [RING[:, 3, 2048 + 512 * i:2048 + 512 * (i + 1)] for i in range(4)]
            ptc = [0]
            itc = [0]

            def vtm(d, b):
                if d < 2:
                    return RING[:, 2, d * 2048 + b * 128:d * 2048 + (b + 1) * 128]
                return RING[:, 3, b * 128:(b + 1) * 128]

            def vtm_grp(d, grp):
                if d < 2:
                    return RING[:, 2, d * 2048 + grp * 512:d * 2048 + (grp + 1) * 512]
                return RING[:, 3, grp * 512:(grp + 1) * 512]

            memset("dve", Kc[:, :, :], 0.0, ["Kc"])
            memset("dve", Vc[:, :, :], 0.0, ["Vc"])
            memset("dve", QTZ[0][64:128, :], 0.0, ["QT"])
            memset("dve", QTZ[1][0:64, :], 0.0, ["QT"])
            ACCS = [(PS[5], "ps5"), (PS[6], "ps6")]
            DEN, kd = PS[7], "ps7"
            ZRHS = CBS[:, B_MASK:B_MASK + 512]
            def pair_w(c):
                sl = c % 2
                return (wslot(sl)[:, 0:1024].rearrange("p (k f) -> p k f", k=8),
                        wslot(sl)[:, 1024:2048].rearrange("p (k f) -> p k f", k=8),
                        wslot(sl)[:, 2048:3072].rearrange("p (k f) -> p k f", k=8), f"W{sl}")

            def load_pair_w(c):
                WQ_, WK_, WV_, wk_ = pair_w(c)
                load_w(WQ_, win[:, :, c * 128:(c + 1) * 128], wk_)
                load_w(WK_, win[:, :, 512 + c * 128:512 + (c + 1) * 128], wk_)
                load_w(WV_, win[:, :, 1024 + c * 128:1024 + (c + 1) * 128], wk_)

            for c in range(4 if 'attn' not in SKIP else 0):
                set_banks(range(8))
                WQ, WK, WV, wkey = pair_w(c)
                if c == 0:
                    load_pair_w(0)
                if c + 1 < 4:
                    load_pair_w(c + 1)
                def vperm_gen(c=c, WV=WV, wkey=wkey):
                    for d in range(3 if 'vperm' not in SKIP else 0):
                        dil = DILS[d]
                        for grp in range(4):
                            ps, pk_ = bank()
                            for bi in range(4):
                                b = grp * 4 + bi
                                if d == 0:
                                    start = 128 * b
                                elif d == 1:
                                    start = 512 * (b // 4) + (b % 4)
                                else:
                                    start = b
                                for k in range(8):
                                    mm(ps[:, bi * 128:(bi + 1) * 128], XN[:, k, ss(start, 128, dil)], WV[:, k, :], k == 0, k == 7,
                                       [wkey, "XN0", "XN1", "XN2", "XN3"], [pk_])
                            cp("act", vtm_grp(d, grp), ps[:, :], [pk_], ["VTM"])
                            if d == 0:
                                cp("act", VSTG[:, :], ps[:, :], [pk_], ["RDEN"])
                                P.dma("sp", av_p[jl, grp * 512:(grp + 1) * 512, c * 128:(c + 1) * 128].rearrange("(b p) f -> p b f", p=128),
                                      VSTG[:, :].rearrange("p (b f) -> p b f", b=4), ["RDEN"], [])
                            yield

                vg = vperm_gen()

                def vfill():
                    try:
                        next(vg)
                        return True
                    except StopIteration:
                        return False

                def emit_tr(A, kA, t0, c=c):
                    ps4, pk4 = bank()
                    for bl in range(4):
                        tr(ps4[:, bl * 128:(bl + 1) * 128], A[:, bl * 128:(bl + 1) * 128], IDF, [kA, "CFS"], [pk4])
                    cp("act", KST[:, :], ps4[:, :], [pk4], ["RDEN"])
                    P.dma("sp", ak_p[jl, t0:t0 + 512, c * 128:(c + 1) * 128].rearrange("(b p) f -> p b f", p=128),
                          KST[:, :].rearrange("p (b f) -> p b f", b=4), ["RDEN"], [])

                pending = None
                for ti, (t0, n) in enumerate(TILES if 'qk' not in SKIP else []):
                    for which in range(2):
                        ib = itc[0] % 2
                        itc[0] += 1
                        A, kA = A2[ib], f"A{ib}"
                        Cc, kC = C2[ib], f"C{ib}"
                        W_ = WQ if which == 0 else WK
                        gcol = pk((PK_QG if which == 0 else PK_KG) + jl)
                        ps, pk_ = bank()
                        for k in range(8):
                            mm(ps[:, 0:n], W_[:, k, :], XN[:, k, t0:t0 + n], k == 0, k == 7, [wkey, f"XN{ti}"], [pk_])
                        act(A[:, 0:n], ps[:, 0:n], AF.Identity, [pk_, "CFS"], [kA], scale=gcol)
                        act(Bq[:, 0:n], ps[:, 0:n], AF.Square, [pk_], ["B"])
                        vfill()
                        ps2, pk2 = bank()
                        mm(ps2[:, 0:n], HM, Bq[:, 0:n], True, True, ["B", "CBS"], [pk2])
                        ps3, pk3 = bank()
                        mm(ps3[:, 0:n], ROT, A[:, 0:n], True, True, [kA, "CFS"], [pk3])
                        rstd_from(ps2[:, 0:n], Cc[:, 0:n], [pk2], kC)
                        tt("dve", Dd[:, 0:n], ps3[:, 0:n], SIN[:, t0:t0 + n], ALU.mult, [pk3, "SIN"], ["D"])
                        tt("dve", A[:, 0:n], A[:, 0:n], COS[:, t0:t0 + n], ALU.mult, [kA, "COS"], [kA])
                        tt("dve", A[:, 0:n], A[:, 0:n], Dd[:, 0:n], ALU.add, [kA, "D"], [kA])
                        tt("dve", A[:, 0:n], A[:, 0:n], Cc[:, 0:n], ALU.mult, [kA, kC], [kA])
                        if which == 0:
                            cp("act", QTZ[0][0:64, t0:t0 + n], A[0:64, 0:n], [kA], ["QT"])
                            cp("act", QTZ[1][64:128, t0:t0 + n], A[64:128, 0:n], [kA], ["QT"])
                        else:
                            cp("act", KT[:, t0:t0 + n], A[:, 0:n], [kA], ["KT"])
                        if ti == 4:
                            cp("dve", (QSF if which == 0 else KSF)[:, 0:NS], A[:, 0:NS], [kA], ["QSF" if which == 0 else "KSF"])
                        if pending is not None:
                            emit_tr(*pending)
                            pending = None
                        if which == 1 and ti < 4:
                            pending = (A, kA, t0)
                if pending is not None:
                    emit_tr(*pending)
                while vfill():
                    pass

                def sample_chain(c=c, WV=WV, wkey=wkey):
                    for nq in range(NS if 'sattn' not in SKIP else 0):
                        ps, pk_ = bank()
                        tr(ps[0:1, 0:128], QSF[:, nq:nq + 1], IDF, ["QSF", "CFS"], [pk_])
                        tr(ps[0:1, 128:256], KSF[:, nq:nq + 1], IDF, ["KSF", "CFS"], [pk_])
                        for k in range(8):
                            mm(ps[0:1, 256:384], XN[:, k, SEQ + nq:SEQ + nq + 1], WV[:, k, :], k == 0, k == 7, [wkey, "XN4"], [pk_])
                        for d in range(3):
                            dil = DILS[d]
                            r0 = WINBUF - 128 * dil
                            P.dma("sp", Kc[:, d, :], ck[jl, nq, ss(r0, 128, dil), c * 128:(c + 1) * 128], (), ["Kc"])
                            P.dma("sp", Vc[:, d, :], cv[jl, nq, ss(r0, 128, dil), c * 128:(c + 1) * 128], (), ["Vc"])
                        yield
                        cp("act", ROW[0:1, 0:384], ps[0:1, 0:384], [pk_], ["ROW"])
                        yield
                        cp("dve", Kc[0:1, 3, :], ROW[0:1, 128:256], ["ROW"], ["Kc"])
                        cp("dve", Vc[0:1, 3, :], ROW[0:1, 256:384], ["ROW"], ["Vc"])
                        P.dma("sp", ak_s[jl, nq:nq + 1, c * 128:(c + 1) * 128], ROW[0:1, 128:256], ["ROW"], [])
                        P.dma("sp", av_s[jl, nq:nq + 1, c * 128:(c + 1) * 128], ROW[0:1, 256:384], ["ROW"], [])
                        ps2, pk2 = bank()
                        mm(ps2[:, 0:128], ONEF[0:1, :], ROW[0:1, 0:128], True, True, ["ROW", "CFS"], [pk2])
                        yield
                        for sl4 in range(4):
                            tt("dve", PROD[:, sl4, :], Kc[:, sl4, :], ps2[:, 0:128], ALU.mult, ["Kc", pk2], ["PROD"])
                        P.op("dve", lambda e: e.tensor_reduce(out=SS[:, 0:8], in_=PROD[:, :, :].rearrange("p s (h d) -> p (s h) d", h=2),
                                                              axis=AX.X, op=ALU.add), ["PROD"], ["SS"])
                        yield
                        act(PM[:, 0:8], SS[:, 0:8], AF.Exp, ["SS"], ["PM"], scale=0.125)
                        yield
                        tt("dve", PM[:, 0:8], PM[:, 0:8], PMASK, ALU.mult, ["PM", "CFS"], ["PM"])
                        yield
                        ps3, pk3 = bank()
                        for h in range(2):
                            for sl4 in range(4):
                                mm(ps3[0:1, h * 64:(h + 1) * 64], PM[:, sl4 * 2 + h:sl4 * 2 + h + 1], Vc[:, sl4, h * 64:(h + 1) * 64],
                                   sl4 == 0, sl4 == 3, ["PM", "Vc"], [pk3])
                        mm(ps3[0:1, 128:136], ONEF[:, 0:1], PM[:, 0:8], True, True, ["PM", "CFS"], [pk3])
                        yield
                        P.op("dve", lambda e, ps3=ps3: e.tensor_reduce(out=DEN2[0:1, 0:2], in_=ps3[0:1, 128:136].rearrange("p (s h) -> p h s", h=2),
                                                                        axis=AX.X, op=ALU.add), [pk3], ["DEN2"])
                        P.op("dve", lambda e: e.reciprocal(out=RD[0:1, 0:2], in_=DEN2[0:1, 0:2]), ["DEN2"], ["RD"])
                        for h in range(2):
                            tsc("dve", OROW[0:1, h * 64:(h + 1) * 64], ps3[0:1, h * 64:(h + 1) * 64], RD[0:1, h:h + 1], None, ALU.mult, None,
                                [pk3, "RD"], ["OROW"])
                        yield
                        ps4, pk4 = bank()
                        tr(ps4[:, 0:1], OROW[0:1, 0:128], IDF[0:1, 0:1], ["OROW", "CFS"], [pk4])
                        yield
                        cp("act", ATT[:, c, SEQ + nq:SEQ + nq + 1], ps4[:, 0:1], [pk4], ["ATT"])
                        yield

                gen = sample_chain()
                set_banks(range(5))

                def advance(k=1):
                    for _ in range(k):
                        try:
                            next(gen)
                        except StopIteration:
                            return False
                    return True

                items = []
                for s in range(4 if 'pattn' not in SKIP else 0):
                    for h in range(2):
                        bl_ = []
                        b1 = [(4 * s + i, 512 * s + 128 * i, 1, 128 * i, 1) for i in range(4)]
                        bl_.append((0, b1, 128, 0))
                        b2 = [(4 * s + i - 1, 512 * s + 128 * i, 1, 128 * i, 1) for i in range(4) if 4 * s + i >= 1]
                        bl_.append((0, b2, 128, 1))
                        b3 = [(4 * s + r, 512 * s + r, 4, r, 4) for r in range(4)]
                        bl_.append((1, b3, 128, 0))
                        if s >= 1:
                            b4 = [(4 * (s - 1) + r, 512 * s + r, 4, r, 4) for r in range(4)]
                            bl_.append((1, b4, 128, 1))
                        b5 = [(r, r + 16 * 32 * s, 16, r, 16) for r in range(16)]
                        bl_.append((2, b5, 32, 2 + s))
                        for bi_, bt in enumerate(bl_):
                            items.append((s, h, bt, h == 0 and bi_ == 0, h == 1 and bi_ == len(bl_) - 1))

                def issue_qk(it):
                    s, h, (d, tl, w, mi), _, _ = it
                    dil = DILS[d]
                    ps, pk_ = bank()
                    off = 512 - w * len(tl)
                    for i, (kb, q0, qst, c0, cst) in enumerate(tl):
                        if d == 0:
                            k0 = 128 * kb
                        elif d == 1:
                            k0 = 512 * (kb // 4) + (kb % 4)
                        else:
                            k0 = kb
                        mm(ps[:, off + i * w:off + (i + 1) * w], KT[:, ss(k0, 128, dil)], QTZ[h][:, ss(q0, w, qst)],
                           i == 0, False, ["KT", "QT"], [pk_])
                    mm(ps[:, off:512], IDB, MASK(mi)[:, off:512], False, True, ["CBS", pk_], [pk_])
                    pt = PT[ptc[0] % 4]
                    ptk = f"PT{ptc[0] % 4}"
                    ptc[0] += 1
                    act(pt[:, off:512], ps[:, off:512], AF.Exp, [pk_], [ptk], scale=0.125)
                    return pt, ptk, off

                def issue_pv(it, info):
                    s, h, (d, tl, w, mi), first, last = it
                    pt, ptk, off = info
                    ACC, ka = ACCS[h]
                    if first:
                        mm(ACCS[0][0][:, :], ZER, ZRHS, True, False, ["CBS"], [ACCS[0][1]])
                        mm(ACCS[1][0][:, :], ZER, ZRHS, True, False, ["CBS"], [ACCS[1][1]])
                        mm(DEN[:, :], ZER, ZRHS, True, False, ["CBS"], [kd])
                    for i, (kb, q0, qst, c0, cst) in enumerate(tl):
                        mm(ACC[:, ss(c0, w, cst)], vtm(d, kb), pt[:, off + i * w:off + (i + 1) * w],
                           False, False, [ptk, "VTM", ka], [ka])
                    nt = len(tl)
                    if d == 0:
                        mm(DEN[:, off:512], EH(h), pt[:, off:512], False, False, [ptk, "CBS", kd], [kd])
                    else:
                        for i, (kb, q0, qst, c0, cst) in enumerate(tl):
                            mm(DEN[:, ss(c0, w, cst)], EH(h), pt[:, off + i * w:off + (i + 1) * w],
                               False, False, [ptk, "CBS", kd], [kd])
                    if last:
                        mm(ACCS[0][0][:, :], ZER, ZRHS, False, True, ["CBS", ACCS[0][1]], [ACCS[0][1]])
                        mm(ACCS[1][0][:, :], ZER, ZRHS, False, True, ["CBS", ACCS[1][1]], [ACCS[1][1]])
                        mm(DEN[:, :], ZER, ZRHS, False, True, ["CBS", kd], [kd])
                        act(RDEN[:, :], DEN[:, :], AF.Ln, [kd], ["RDEN"])
                        act(RDEN[:, :], RDEN[:, :], AF.Exp, ["RDEN"], ["RDEN"], scale=-1.0)
                        tt("dve", ATT[0:64, c, 512 * s:512 * (s + 1)], ACCS[0][0][0:64, :], RDEN[0:64, :], ALU.mult,
                           [ACCS[0][1], "RDEN"], ["ATT"])
                        tt("dve", ATT[64:128, c, 512 * s:512 * (s + 1)], ACCS[1][0][64:128, :], RDEN[64:128, :], ALU.mult,
                           [ACCS[1][1], "RDEN"], ["ATT"])

                prev = None
                for idx in range(len(items) + 1):
                    cur = items[idx] if idx < len(items) else None
                    info = issue_qk(cur) if cur is not None else None
                    if prev is not None:
                        issue_pv(prev[0], prev[1])
                    prev = (cur, info) if cur is not None else None
                    advance(1)
                while advance(1):
                    pass
            P.barrier()
            if 'ffn' not in SKIP:
                ffn_loadG(layer, 0, 2)
            set_banks(range(8))
            WOA = wslot(0).rearrange("p (k j) -> p k j", k=4)
            load_w(WOA, wo[:, 0:4, :], "W0")
            for ti, (t0, n) in enumerate(TILES if 'oproj' not in SKIP else []):
                for j in range(8):
                    ps, pk_ = bank()
                    for k in range(4):
                        mm(ps[:, 0:n], WOA[:, k, j * 128:(j + 1) * 128], ATT[:, k, t0:t0 + n], k == 0, k == 3, ["W0", "ATT"], [pk_])
                    tt("dve", XT[:, j, t0:t0 + n], XT[:, j, t0:t0 + n], ps[:, 0:n], ALU.add, [pk_, f"XT{ti}_{j}"], [f"XT{ti}_{j}"])

        def c_layer(layer):
            jl = layer // 2
            P.barrier()
            ar_reset()
            GW = 30 + SEQ + 8
            GLU = ar(8 * GW).rearrange("p (c n) -> p c n", c=8)
            GTAIL = ar(8 * 30, F32).rearrange("p (c n) -> p c n", c=8)
            GSF = ar(8 * NS, F32).rearrange("p (c n) -> p c n", c=8)
            XC = ar(NS * 8 * 31, F32).rearrange("p (n c j) -> p n c j", n=NS, c=8)
            base = arpos[0]
            SQ = ar(8 * 512).rearrange("p (c n) -> p c n", c=8)
            RS = ar(512, F32)
            SG = ar(512, F32)
            STG4 = [ar(D, F32) for _ in range(NS)]
            set_banks(range(8))
            wci = w_ci[jl].rearrange("(k p) f -> p k f", p=128)
            wco = w_co[jl].rearrange("(k p) j -> p k j", p=128)

            def load_ci(c):
                sl = c % 2
                WA_ = wslot(sl)[:, 0:1024].rearrange("p (k f) -> p k f", k=8)
                WG_ = wslot(sl)[:, 1024:2048].rearrange("p (k f) -> p k f", k=8)
                load_w(WA_, wci[:, :, c * 128:(c + 1) * 128], f"W{sl}")
                load_w(WG_, wci[:, :, D + c * 128:D + (c + 1) * 128], f"W{sl}")
                return WA_, WG_, f"W{sl}"

            nxt = load_ci(0)
            for nq in range(NS):
                P.dma("sp", STG4[nq][0:30, :], stc[jl, nq], (), [f"STG4{nq}"])
                P.dma("sp", cc_s[jl, nq, 0:29, :], STG4[nq][1:30, :], [f"STG4{nq}"], [])
            ngc = rmsnorm_gen(PK_MIXG + 8 * layer, SQ, RS)
            next(ngc)
            memset("dve", GLU[:, :, 0:30], 0.0, ["GLU"])
            for c in range(8):
                WA_, WG_, wkey = nxt
                if c + 1 < 8:
                    nxt = load_ci(c + 1)
                for ti, (t0, n) in enumerate(TILES):
                    psA, ka = bank()
                    for k in range(8):
                        mm(psA[:, 0:n], WA_[:, k, :], XN[:, k, t0:t0 + n], k == 0, k == 7, [wkey, f"XN{ti}"], [ka])
                    psG, kg = bank()
                    for k in range(8):
                        mm(psG[:, 0:n], WG_[:, k, :], XN[:, k, t0:t0 + n], k == 0, k == 7, [wkey, f"XN{ti}"], [kg])
                    act(SG[:, 0:n], psG[:, 0:n], AF.Sigmoid, [kg], ["SG"])
                    if ti < 4:
                        tt("dve", GLU[:, c, 30 + t0:30 + t0 + n], psA[:, 0:n], SG[:, 0:n], ALU.mult, [ka, "SG"], ["GLU"])
                        if ti == 3:
                            tt("dve", GTAIL[:, c, :], psA[:, 482:512], SG[:, 482:512], ALU.mult, [ka, "SG"], ["GTAIL"])
                    else:
                        tt("dve", GSF[:, c, :], psA[:, 0:NS], SG[:, 0:NS], ALU.mult, [ka, "SG"], ["GSF"])
                    if c == 0:
                        next(ngc, None)
                if c == 1:
                    for nq in range(NS):
                        ps, pk_ = bank()
                        for c2 in range(8):
                            tr(ps[:, c2 * 30:(c2 + 1) * 30], STG4[nq][0:30, c2 * 128:(c2 + 1) * 128], IDF[0:30, 0:30],
                               [f"STG4{nq}", "CFS"], [pk_])
                        cp("act", XC[:, nq, :, 0:30], ps[:, 0:240].rearrange("p (c j) -> p c j", c=8), [pk_], ["XC"])
            P.barrier()
            if 'ffn' not in SKIP:
                ffn_loadG(layer, 0, 0)
            xnf = XN[:, :, :].rearrange("p c n -> p (c n)")
            Y = xnf[:, 0:8192].bitcast(F32).rearrange("p (c n) -> p c n", c=8)
            STt = xnf[:, 8192:12288].rearrange("p (c n) -> p c n", c=8)
            DG0 = xnf[:, 12288:12288 + 31 * 128].rearrange("p (j f) -> p j f", j=31)
            ar_reset(base)
            DGS = [DG0, ar(31 * 128).rearrange("p (j f) -> p j f", j=31)]
            MEAN = ar(512, F32)
            VAR = ar(512, F32)
            RS2 = ar(512, F32)
            DtL = [ar(512, F32), ar(512, F32)]
            YBL = [ar(512), ar(512)]
            YSL = [ar(512), ar(512)]
            STG = ar(D, F32)
            PRC = ar(8 * 31, F32).rearrange("p (c j) -> p c j", c=8)
            WCO0 = wslot(2).rearrange("p (k j) -> p k j", k=4)
            WCO1 = wslot(3).rearrange("p (k j) -> p k j", k=4)
            load_w(WCO0, wco[:, 0:4, :], "W2")
            load_w(WCO1, wco[:, 4:8, :], "W3")
            WDW = pk(PK_WDW + jl * 248, 248).rearrange("p (c j) -> p c j", c=8)
            psM, km = PS[6], "ps6"
            psE, ke = PS[7], "ps7"
            set_banks(range(6))

            def tail_outputs():
                for half in range(2):
                    ps, pk_ = bank()
                    for cc in range(4):
                        c = half * 4 + cc
                        tr(ps[0:30, cc * 128:(cc + 1) * 128], GTAIL[:, c, :], IDF, ["GTAIL", "CFS"], [pk_])
                    cp("act", STG[0:30, half * 512:(half + 1) * 512], ps[0:30, :], [pk_], ["STG"])
                P.dma("sp", cc_p[jl], STG[0:30, :], ["STG"], [])
                for nq in range(NS):
                    cp("dve", XC[:, nq, :, 30:31], GSF[:, :, nq:nq + 1], ["GSF"], ["XC"])
                for half in range(2):
                    ps2, pk2 = bank()
                    for cc in range(4):
                        c = half * 4 + cc
                        tr(ps2[0:NS, cc * 128:(cc + 1) * 128], GSF[:, c, 0:NS], IDF, ["GSF", "CFS"], [pk2])
                    cp("act", STG[0:NS, half * 512:(half + 1) * 512], ps2[0:NS, :], [pk2], ["STG"])
                P.dma("sp", cc_s[jl, :, 29, :], STG[0:NS, :], ["STG"], [])

            def build_dg(c):
                DG = DGS[c % 2]
                dgk = f"DG{c % 2}"
                for j in range(31):
                    if j % 4 == 0:
                        act(DG[:, j, :], IDB, AF.Identity, ["CBS", "CFS"], [f"{dgk}_{j}"], scale=pk(PK_WDW + jl * 248 + c * 31 + j))
                    else:
                        tsc("dve", DG[:, j, :], IDB, pk(PK_WDW + jl * 248 + c * 31 + j), None, ALU.mult, None, ["CBS", "CFS"], [f"{dgk}_{j}"])

            def conv_stats(ti, t0, n, lnprev=None):
                for c in range(8):
                    YB, YS = YBL[c % 2], YSL[c % 2]
                    kyb, kys = f"YB{c % 2}", f"YS{c % 2}"
                    bias = pk(PK_BDW + 8 * jl + c)
                    if ti < 4:
                        DG = DGS[c % 2]
                        dgk = f"DG{c % 2}"
                        ps, pk_ = bank()
                        for j in range(31):
                            mm(ps[:, 0:n], DG[:, j, :], GLU[:, c, t0 + j:t0 + j + n], j == 0, j == 30, [f"{dgk}_{j}", "GLU"], [pk_])
                        if not (ti == 3 and c >= 6):
                            build_dg((c + 2) % 8)
                        if lnprev is not None:
                            ln_chunk(c, lnprev[2])
                        src = ps[:, 0:n]
                        act(Y[:, c, 0:n], src, AF.Identity, [pk_, "CFS"], [f"Y{c}"], bias=bias)
                        act(YB[:, 0:n], src, AF.Identity, [pk_, "CFS"], [kyb], bias=bias)
                        act(YS[:, 0:n], src, AF.Square, [pk_, "CFS"], [kys], bias=bias)
                    else:
                        if lnprev is not None:
                            ln_chunk(c, lnprev[2])
                        for nq in range(NS):
                            tt("dve", PRC[:, c, :], XC[:, nq, c, :], WDW[:, c, :], ALU.mult, ["XC", "CFS"], ["PRC"])
                            P.op("dve", lambda e, c=c, nq=nq: e.tensor_reduce(out=Y[:, c, nq:nq + 1], in_=PRC[:, c, :], axis=AX.X, op=ALU.add),
                                 ["PRC"], [f"Y{c}"])
                        act(Y[:, c, 0:n], Y[:, c, 0:n], AF.Identity, [f"Y{c}", "CFS"], [f"Y{c}"], bias=bias)
                        act(YB[:, 0:n], Y[:, c, 0:n], AF.Copy, [f"Y{c}"], [kyb])
                        act(YS[:, 0:n], Y[:, c, 0:n], AF.Square, [f"Y{c}"], [kys])
                    mm(psM[:, 0:n], ODM, YB[:, 0:n], c == 0, c == 7, [kyb, "CBS"], [km])
                    mm(psE[:, 0:n], ODM, YS[:, 0:n], c == 0, c == 7, [kys, "CBS"], [ke])

            def ln_head(ti, t0, n):
                cp("act", MEAN[:, 0:n], psM[:, 0:n], [km], ["MEAN"])
                tt("dve", VAR[:, 0:n], MEAN[:, 0:n], MEAN[:, 0:n], ALU.mult, ["MEAN"], ["VAR"])
                tt("dve", VAR[:, 0:n], psE[:, 0:n], VAR[:, 0:n], ALU.subtract, [ke, "VAR"], ["VAR"])
                tsc("dve", VAR[:, 0:n], VAR[:, 0:n], 0.0, None, ALU.max, None, ["VAR"], ["VAR"])
                rstd_from(VAR[:, 0:n], RS2[:, 0:n], ["VAR"], "RS2")

            def ln_chunk(c, n):
                Dt, kdt = DtL[c % 2], f"Dt{c % 2}"
                tt("dve", Dt[:, 0:n], Y[:, c, 0:n], MEAN[:, 0:n], ALU.subtract, [f"Y{c}", "MEAN"], [kdt])
                tt("dve", Dt[:, 0:n], Dt[:, 0:n], RS2[:, 0:n], ALU.mult, [kdt, "RS2"], [kdt])
                act(STt[:, c, 0:n], Dt[:, 0:n], AF.Silu, [kdt, "CFS"], [f"STt{c}"], scale=pk(PK_CNG + 8 * jl + c), bias=pk(PK_CNB + 8 * jl + c))

            def outproj(ti, t0, n):
                for j in range(8):
                    ps, pk_ = bank()
                    for k in range(8):
                        Wc = WCO0 if k < 4 else WCO1
                        mm(ps[:, 0:n], Wc[:, k % 4, j * 128:(j + 1) * 128], STt[:, k, 0:n], k == 0, k == 7, ["W2", "W3", f"STt{k}"], [pk_])
                    tt("dve", XT[:, j, t0:t0 + n], XT[:, j, t0:t0 + n], ps[:, 0:n], ALU.add, [pk_, f"XT{ti}_{j}"], [f"XT{ti}_{j}"])

            build_dg(0)
            build_dg(1)
            prev = None
            for ti, (t0, n) in enumerate(TILES):
                if ti == 1:
                    tail_outputs()
                if prev is not None:
                    ln_head(*prev)
                conv_stats(ti, t0, n, prev)
                if prev is not None:
                    outproj(*prev)
                prev = (ti, t0, n)
            ln_head(*prev)
            for c in range(8):
                ln_chunk(c, prev[2])
            outproj(*prev)

        for layer in range(LAYERS):
            if layer % 2 == 0:
                ab_layer(layer)
            else:
                c_layer(layer)
            if 'ffn' not in SKIP:
                if layer % 2 == 0:
                    ffn(layer, 2, AB_BASE[0])
                else:
                    ffn(layer, 0)

        P.barrier()
        ar_reset()
        YO = [ar(D, F32), ar(D, F32)]
        set_banks(range(8))
        for b in range(16):
            yo = YO[b % 2]
            for half in range(2):
                ps, pk_ = bank()
                for cc in range(4):
                    c = half * 4 + cc
                    tr(ps[:, cc * 128:(cc + 1) * 128], XT[:, c, b * 128:(b + 1) * 128], IDF, [f"XT{b // 4}_{c}", "CFS"], [pk_])
                cp("act" if half == 0 else "dve", yo[:, half * 512:(half + 1) * 512], ps[:, :], [pk_], [f"YO{b % 2}"])
            P.dma("sp", y_p[b * 128:(b + 1) * 128, :], yo[:, :], [f"YO{b % 2}"], [])
        YS_ = ar(D, F32)
        for half in range(2):
            ps, pk_ = bank()
            for cc in range(4):
                c = half * 4 + cc
                tr(ps[0:NS, cc * 128:(cc + 1) * 128], XT[:, c, SEQ:T], IDF, [f"XT4_{c}", "CFS"], [pk_])
            cp("act", YS_[0:NS, half * 512:(half + 1) * 512], ps[0:NS, :], [pk_], ["YS_"])
        P.dma("sp", y_s[:, :], YS_[0:NS, :], ["YS_"], [])

        P.emit(block)
    return nc


_CACHE = {}


def _consts():
    cf = np.zeros((128, NCF), np.float32)
    cf[:, C_ID:C_ID + 128] = np.eye(128, dtype=np.float32)
    rot = np.zeros((128, 128), np.float32)
    for i in range(128):
        if (i % 64) < 32:
            rot[i + 32, i] = -1.0
        else:
            rot[i - 32, i] = 1.0
    cf[:, C_ROT:C_ROT + 128] = rot
    cf[:, C_ONE:C_ONE + 128] = 1.0
    pm = np.zeros((128, 8), np.float32)
    pm[:, 0:6] = 1.0
    pm[0, 6:8] = 3.0
    cf[:, C_PM:C_PM + 8] = pm
    cb = np.zeros((128, NCB), np.float32)
    cb[:, B_ID:B_ID + 128] = np.eye(128)
    hm = np.zeros((128, 128), np.float32)
    hm[0:64, 0:64] = 1.0 / 64
    hm[64:128, 64:128] = 1.0 / 64
    cb[:, B_HM:B_HM + 128] = hm
    cb[:, B_ODM:B_ODM + 128] = 1.0 / 1024
    cb[:, B_ONE64:B_ONE64 + 64] = 1.0
    j = np.arange(128)[:, None]
    i = np.arange(128)[None, :]
    cur = (j <= i).astype(np.float32)
    prev = (j >= i).astype(np.float32)
    NEG = -30000.0
    cb[:, B_MASK:B_MASK + 512] = np.tile((1.0 - cur) * NEG, (1, 4))
    cb[:, B_MASK + 512:B_MASK + 1024] = np.tile((1.0 - prev) * NEG, (1, 4))
    for s in range(4):
        cb[:, B_MASK + 512 * (2 + s):B_MASK + 512 * (3 + s)] = np.tile((1.0 - cur[:, 32 * s:32 * (s + 1)]) * NEG, (1, 16))
    cb[:, B_MULT:B_MULT + 512] = np.tile(cur, (1, 4))
    cb[:, B_EH:B_EH + 64] = 1.0
    cb[:, B_EH + 128 + 64:B_EH + 256] = 1.0
    inv = (10000.0 ** (-np.arange(0, 64, 2, dtype=np.float32) / 64)).astype(np.float32)
    pos = np.concatenate([np.arange(SEQ, dtype=np.float32), np.full((NS,), float(PAST), np.float32)])
    ang = (pos[None, :] * inv[:, None]).astype(np.float32)
    cosT = np.tile(np.cos(ang).astype(np.float32), (4, 1))
    sinT = np.tile(np.sin(ang).astype(np.float32), (4, 1))
    return cf, cb, np.ascontiguousarray(cosT), np.ascontiguousarray(sinT)


def kernel(x_prompt, x_sample, cache_a_k, cache_a_v, state_c_conv, norm_mix_g, norm_ffn_g, w_ffn_up, w_ffn_down,
           w_in_ab, q_norm_g, k_norm_g, vb_norm_g, vb_norm_b, w_spatial, b_spatial, w_out_ab,
           w_c_in, w_c_dw, b_c_dw, c_norm_g, c_norm_b, w_c_out):
    f = lambda a: np.ascontiguousarray(np.asarray(a, dtype=np.float32))
    x_prompt, x_sample = f(x_prompt), f(x_sample)
    cache_a_k, cache_a_v, state_c_conv = f(cache_a_k), f(cache_a_v), f(state_c_conv)
    if "nc" not in _CACHE:
        _CACHE["nc"] = build_program()
    nc = _CACHE["nc"]
    cf, cb, cosT, sinT = _consts()
    def fm(v):
        v = f(v)
        n = v.shape[0]
        return v.reshape(n, 8, 128).transpose(2, 0, 1).reshape(128, n * 8)
    pkc = np.zeros((128, NPK), np.float32)
    pkc[:, PK_MIXG:PK_MIXG + 32] = fm(norm_mix_g)
    pkc[:, PK_FFNG:PK_FFNG + 32] = fm(norm_ffn_g)
    pkc[:, PK_QG:PK_QG + 2] = np.tile(f(q_norm_g).T, (2, 1))
    pkc[:, PK_KG:PK_KG + 2] = np.tile(f(k_norm_g).T, (2, 1))
    pkc[:, PK_BDW:PK_BDW + 16] = fm(b_c_dw)
    pkc[:, PK_CNG:PK_CNG + 16] = fm(c_norm_g)
    pkc[:, PK_CNB:PK_CNB + 16] = fm(c_norm_b)
    wdw = f(w_c_dw)
    pkc[:, PK_WDW:PK_WDW + 496] = wdw.reshape(2, 31, 8, 128).transpose(3, 0, 2, 1).reshape(128, 496)
    wsp = f(w_spatial)
    bsp = f(b_spatial)
    gidx = (np.arange(128) // 64)[:, None] + 2 * np.arange(4)[None, :]
    for jl in range(2):
        pkc[:, PK_WS00 + 4 * jl:PK_WS00 + 4 * jl + 4] = wsp[jl, :, 0, 0][gidx]
        pkc[:, PK_BS0 + 4 * jl:PK_BS0 + 4 * jl + 4] = bsp[jl, :, 0][gidx]
    cf[:, C_PK:C_PK + NPK] = pkc
    wsT = np.ascontiguousarray(wsp.transpose(0, 3, 1, 2).reshape(2, 128, 8 * 128))
    bT = np.ascontiguousarray(bsp[:, gidx, :].reshape(2, 128, 4 * 128))
    vbg = np.ascontiguousarray(np.broadcast_to(f(vb_norm_g)[:, None, :], (2, 128, 512)))
    vbb = np.ascontiguousarray(np.broadcast_to(f(vb_norm_b)[:, None, :], (2, 128, 512)))
    shared = {"w_up": f(w_ffn_up), "w_dn": f(w_ffn_down), "w_in": f(w_in_ab), "w_out": f(w_out_ab), "w_ci": f(w_c_in),
              "w_co": f(w_c_out), "cf": cf, "cb": cb, "cosT": cosT, "sinT": sinT, "wsT": wsT, "bT": bT, "vbg": vbg, "vbb": vbb}
    in_maps = []
    for i in range(NCORES):
        m = dict(shared)
        m["xp"] = x_prompt[i]
        m["xs"] = np.ascontiguousarray(x_sample[NS * i:NS * (i + 1), 0, :])
        m["ck"] = np.ascontiguousarray(cache_a_k[:, NS * i:NS * (i + 1)].reshape(2, NS, WINBUF, 512))
        m["cv"] = np.ascontiguousarray(cache_a_v[:, NS * i:NS * (i + 1)].reshape(2, NS, WINBUF, 512))
        m["stc"] = np.ascontiguousarray(state_c_conv[:, NS * i:NS * (i + 1)])
        in_maps.append(m)
    res = run_bass_kernel_spmd(nc, in_maps, core_ids=list(range(NCORES)))
    R = res.results
    g = lambda name: [np.asarray(R[i][name], dtype=np.float32) for i in range(NCORES)]
    y_p = np.stack(g("y_p"), 0)
    y_s = np.concatenate(g("y_s"), 0)[:, None, :]
    ak_p = np.stack(g("ak_p"), 1).reshape(2, NCORES, SEQ, 8, 64)
    av_p = np.stack(g("av_p"), 1).reshape(2, NCORES, SEQ, 8, 64)
    ak_s = np.concatenate(g("ak_s"), 1).reshape(2, NS * NCORES, 1, 8, 64)
    av_s = np.concatenate(g("av_s"), 1).reshape(2, NS * NCORES, 1, 8, 64)
    bv_p = np.stack(g("bv_p"), 1)
    bv_s = np.concatenate(g("bv_s"), 1)[:, :, None, :]
    cc_p = np.stack(g("cc_p"), 1)
    cc_s = np.concatenate(g("cc_s"), 1)
    return (y_p, y_s, ak_p, av_p, ak_s, av_s, bv_p, bv_s, cc_p, cc_s)
```

```python
from contextlib import ExitStack
import numpy as np
import concourse.bass as bass
import concourse.mybir as mybir
from concourse.bass_utils import run_bass_kernel_spmd

F32 = mybir.dt.float32
BF16 = mybir.dt.bfloat16
AF = mybir.ActivationFunctionType
ALU = mybir.AluOpType
AX = mybir.AxisListType

NCORES = 8
LAYERS = 4
STRICT = True
SKIP = set()
D = 1024
SEQ = 2048
NS = 4
T = SEQ + NS
TILES = [(0, 512), (512, 512), (1024, 512), (1536, 512), (2048, NS)]
EPS = 1e-6
PAST = 8192
WINBUF = 2048
DILS = (1, 4, 16)

C_ID, C_ROT, C_ONE, C_PM = 0, 128, 256, 384
C_PK = 392
PK_MIXG, PK_FFNG, PK_QG, PK_KG, PK_BDW, PK_CNG, PK_CNB, PK_WDW, PK_WS00, PK_BS0 = 0, 32, 64, 66, 68, 84, 100, 116, 612, 620
NPK = 628
NCF = C_PK + NPK
B_ID, B_HM, B_ODM, B_ONE64, B_ZER, B_MASK = 0, 128, 256, 384, 448, 576
B_EH = B_MASK + 6 * 512
B_MULT = B_EH + 256
NCB = B_MULT + 512
ARN = 33600


def ss(start, n, step=1):
    return slice(start, start + step * (n - 1) + 1, step)


class Op:
    __slots__ = ("eng", "fn", "deps", "marked", "val", "isdma", "sem", "prev")

    def __init__(self, eng, fn):
        self.eng = eng
        self.fn = fn
        self.deps = []
        self.marked = False
        self.val = 0
        self.isdma = False
        self.sem = None
        self.prev = 0


class Prog:
    ENGS = ("pe", "act", "dve", "pool", "sp")

    def __init__(self, nc, esems, dsems):
        self.nc = nc
        self.esem = esems
        self.dsems = dsems
        self.dcnt = {q: [0] * len(v) for q, v in dsems.items()}
        self.drr = {q: 0 for q in dsems}
        self.dlast = {q: [None] * len(v) for q, v in dsems.items()}
        self.ops = {e: [] for e in self.ENGS}
        self.last_w = {}
        self.readers = {}
        self.bar = []
        self.bar_seen = {e: True for e in self.ENGS}
        self.nops = 0

    def _deps(self, op, reads, writes):
        deps = []
        for r in reads:
            w = self.last_w.get(r)
            if w is not None:
                deps.append(w)
        for w in writes:
            lw = self.last_w.get(w)
            if lw is not None and (STRICT or lw.isdma or op.isdma or lw.eng != op.eng):
                deps.append(lw)
            for rd in self.readers.get(w, ()):
                if STRICT or rd.isdma or op.isdma or rd.eng != op.eng:
                    deps.append(rd)
        if not self.bar_seen[op.eng]:
            deps.extend(self.bar)
            self.bar_seen[op.eng] = True
        seen = set()
        for d in deps:
            if d is op or id(d) in seen:
                continue
            seen.add(id(d))
            if (not d.isdma) and d.eng == "pe" and op.eng == "pe" and not op.isdma:
                continue
            op.deps.append(d)
            d.marked = True
        for r in reads:
            self.readers.setdefault(r, []).append(op)
        for w in writes:
            self.last_w[w] = op
            self.readers[w] = []

    def op(self, eng, fn, reads=(), writes=()):
        o = Op(eng, fn)
        self._deps(o, reads, writes)
        self.ops[eng].append(o)
        self.nops += 1
        return o

    def dma(self, q, out, in_, reads=(), writes=()):
        o = Op(q, lambda e, out=out, in_=in_: e.dma_start(out=out, in_=in_))
        o.isdma = True
        i = self.drr[q]
        self.drr[q] = (i + 1) % len(self.dsems[q])
        o.sem = self.dsems[q][i]
        o.prev = self.dcnt[q][i]
        self.dcnt[q][i] += 16
        o.val = self.dcnt[q][i]
        self.dlast[q][i] = o
        self._deps(o, reads, writes)
        self.ops[q].append(o)
        self.nops += 1
        return o

    def barrier(self):
        b = []
        for e in self.ENGS:
            for o in reversed(self.ops[e]):
                if not o.isdma:
                    b.append(o)
                    o.marked = True
                    break
        for q in self.dlast:
            for o in self.dlast[q]:
                if o is not None:
                    b.append(o)
        self.bar = b
        self.bar_seen = {e: False for e in self.ENGS}

    def emit(self, block):
        for e in self.ENGS:
            c = 0
            for o in self.ops[e]:
                if not o.isdma and o.marked:
                    c += 1
                    o.val = c
                    o.sem = self.esem[e]
        binder = {"pe": block.tensor, "act": block.scalar, "dve": block.vector, "pool": block.gpsimd, "sp": block.sync}
        for e in self.ENGS:
            ops = self.ops[e]
            dsems = self.dsems
            dcnt = self.dcnt

            def body(eng, ops=ops, e=e):
                waited = {}
                for o in ops:
                    for d in o.deps:
                        k = id(d.sem)
                        if waited.get(k, 0) >= d.val:
                            continue
                        eng.wait_ge(d.sem, d.val)
                        waited[k] = d.val
                    if o.isdma:
                        k = id(o.sem)
                        if o.prev > 0 and waited.get(k, 0) < o.prev:
                            eng.wait_ge(o.sem, o.prev)
                            waited[k] = o.prev
                        o.fn(eng).then_inc(o.sem, 16)
                    else:
                        ins = o.fn(eng)
                        if o.marked:
                            ins.then_inc(o.sem, 1)
                if e in ("sp", "pool"):
                    for q in dsems:
                        for i, s in enumerate(dsems[q]):
                            if dcnt[q][i] > 0:
                                eng.wait_ge(s, dcnt[q][i])

            binder[e](body)


def build_program():
    nc = bass.Bass("TRN2", target_bir_lowering=False)

    def din(name, shape):
        return nc.dram_tensor(name, list(shape), F32, kind="ExternalInput").ap()

    def dout(name, shape):
        return nc.dram_tensor(name, list(shape), F32, kind="ExternalOutput").ap()

    xp = din("xp", [SEQ, D])
    xs = din("xs", [NS, D])
    ck = din("ck", [2, NS, WINBUF, 512])
    cv = din("cv", [2, NS, WINBUF, 512])
    stc = din("stc", [2, NS, 30, D])
    w_up = din("w_up", [4, D, 4 * D])
    w_dn = din("w_dn", [4, 4 * D, D])
    w_in = din("w_in", [2, D, 2560])
    w_out = din("w_out", [2, D, D])
    w_ci = din("w_ci", [2, D, 2 * D])
    w_co = din("w_co", [2, D, D])
    cf_d = din("cf", [128, NCF])
    cb_d = din("cb", [128, NCB])
    cos_d = din("cosT", [128, T])
    sin_d = din("sinT", [128, T])
    wsT_d = din("wsT", [2, 128, 8 * 128])
    bT_d = din("bT", [2, 128, 4 * 128])
    vbg_d = din("vbg", [2, 128, 512])
    vbb_d = din("vbb", [2, 128, 512])

    y_p = dout("y_p", [SEQ, D])
    y_s = dout("y_s", [NS, D])
    ak_p = dout("ak_p", [2, SEQ, 512])
    av_p = dout("av_p", [2, SEQ, 512])
    ak_s = dout("ak_s", [2, NS, 512])
    av_s = dout("av_s", [2, NS, 512])
    bv_p = dout("bv_p", [2, 128, 512])
    bv_s = dout("bv_s", [2, NS, 512])
    cc_p = dout("cc_p", [2, 30, D])
    cc_s = dout("cc_s", [2, NS, 30, D])

    with ExitStack() as es:
        def sb(name, shape, dt):
            return es.enter_context(nc.sbuf_tensor(name, list(shape), dt))

        XT = sb("XT", [128, 8, T], F32)
        XN = sb("XN", [128, 8, T], BF16)
        RING = sb("RING", [128, 4, 4096], BF16)
        CFS = sb("CFS", [128, NCF], F32)
        CBS = sb("CBS", [128, NCB], BF16)
        AR = sb("AR", [128, ARN], BF16)
        PS = [es.enter_context(nc.psum_tensor(f"ps{i}", [128, 512], F32)) for i in range(8)]
        esems = {e: es.enter_context(nc.semaphore(f"se_{e}")) for e in Prog.ENGS}
        dsems = {"sp": [es.enter_context(nc.semaphore(f"sd_sp{i}")) for i in range(12)],
                 "pool": [es.enter_context(nc.semaphore(f"sd_pl{i}")) for i in range(8)]}
        block = es.enter_context(nc.Block())
        P = Prog(nc, esems, dsems)

        IDF = CFS[:, C_ID:C_ID + 128]
        ROT = CFS[:, C_ROT:C_ROT + 128]
        ONEF = CFS[:, C_ONE:C_ONE + 128]
        PMASK = CFS[:, C_PM:C_PM + 8]
        IDB = CBS[:, B_ID:B_ID + 128]
        HM = CBS[:, B_HM:B_HM + 128]
        ODM = CBS[:, B_ODM:B_ODM + 128]
        ONE64 = CBS[:, B_ONE64:B_ONE64 + 64]
        ZER = CBS[:, B_ZER:B_ZER + 128]

        def EH(h):
            return CBS[:, B_EH + 128 * h:B_EH + 128 * (h + 1)]

        def MASK(i):
            return CBS[:, B_MASK + 512 * i:B_MASK + 512 * (i + 1)]

        def pk(col, n=1):
            return CFS[:, C_PK + col:C_PK + col + n]

        arpos = [0]

        def ar_reset(pos=0):
            arpos[0] = pos

        def ar(n_el, dt=BF16):
            nb = n_el * (2 if dt == F32 else 1)
            nb = (nb + 7) // 8 * 8
            a = arpos[0]
            assert a + nb <= ARN, ("arena overflow", a, nb)
            arpos[0] = a + nb
            v = AR[:, a:a + nb]
            if dt == F32:
                v = v.bitcast(F32)
            return v[:, 0:n_el]

        bank_rot = {"list": list(range(8)), "i": 0}

        def bank():
            l = bank_rot["list"]
            i = l[bank_rot["i"] % len(l)]
            bank_rot["i"] += 1
            return PS[i], f"ps{i}"

        def set_banks(l):
            bank_rot["list"] = list(l)
            bank_rot["i"] = 0

        def mm(out, lhsT, rhs, start, stop, reads, writes):
            return P.op("pe", lambda e: e.matmul(out, lhsT=lhsT, rhs=rhs, start=start, stop=stop), reads, writes)

        def tr(out, in_, ident, reads, writes):
            return P.op("pe", lambda e: e.transpose(out=out, in_=in_, identity=ident), reads, writes)

        def act(out, in_, func, reads, writes, scale=None, bias=None):
            kw = {}
            if scale is not None:
                kw["scale"] = scale
            if bias is not None:
                kw["bias"] = bias
            return P.op("act", lambda e: e.activation(out=out, in_=in_, func=func, **kw), reads, writes)

        def tt(eng, out, in0, in1, op, reads, writes):
            return P.op(eng, lambda e: e.tensor_tensor(out=out, in0=in0, in1=in1, op=op), reads, writes)

        def stt(out, in0, scalar, in1, op0, op1, reads, writes):
            return P.op("dve", lambda e: e.scalar_tensor_tensor(out=out, in0=in0, scalar=scalar, in1=in1, op0=op0, op1=op1), reads, writes)

        def tsc(eng, out, in0, s1, s2, op0, op1, reads, writes):
            if op1 is None:
                return P.op(eng, lambda e: e.tensor_scalar(out=out, in0=in0, scalar1=s1, scalar2=None, op0=op0), reads, writes)
            return P.op(eng, lambda e: e.tensor_scalar(out=out, in0=in0, scalar1=s1, scalar2=s2, op0=op0, op1=op1), reads, writes)

        def cp(eng, out, in_, reads, writes):
            if eng == "act":
                return act(out, in_, AF.Copy, reads, writes)
            return P.op(eng, lambda e: e.tensor_copy(out=out, in_=in_), reads, writes)

        def memset(eng, ap, val, writes):
            return P.op(eng, lambda e: e.memset(ap, val), (), writes)

        def rstd_from(ps_ap, out_ap, reads, wkey):
            act(out_ap, ps_ap, AF.Ln, reads, [wkey], bias=EPS)
            act(out_ap, out_ap, AF.Exp, [wkey], [wkey], scale=-0.5)

        def wslot(i):
            return RING[:, i, :]

        P.dma("sp", CFS[:, :], cf_d[:, :], (), ["CFS"])
        for i0 in range(0, NCB, 1472):
            P.dma("pool", CBS[:, i0:i0 + 1472], cb_d[:, i0:i0 + 1472], (), ["CBS"])
        ar_reset()
        XSI = ar(D, F32)
        XIN = [ar(D, F32) for _ in range(15)]
        for b in range(16):
            xin = XIN[b % 15]
            P.dma("sp", xin[:, :], xp[b * 128:(b + 1) * 128, :], (), [f"XIN{b % 15}"])
            for half in range(2):
                ps, pk_ = bank()
                for cc in range(4):
                    c = half * 4 + cc
                    tr(ps[:, cc * 128:(cc + 1) * 128], xin[:, c * 128:(c + 1) * 128], IDF, [f"XIN{b % 15}", "CFS"], [pk_])
                cp("act" if half == 0 else "dve", XT[:, half * 4:half * 4 + 4, b * 128:(b + 1) * 128],
                   ps[:, :].rearrange("p (c n) -> p c n", c=4), [pk_], [f"XT{b // 4}_{half * 4 + q}" for q in range(4)])
        P.dma("sp", XSI[0:NS, :], xs[:, :], (), ["XSI"])
        ps, pk_ = bank()
        for c in range(8):
            tr(ps[:, c * NS:(c + 1) * NS], XSI[0:NS, c * 128:(c + 1) * 128], IDF[0:NS, 0:NS], ["XSI", "CFS"], [pk_])
        cp("act", XT[:, :, SEQ:T], ps[:, 0:8 * NS].rearrange("p (c n) -> p c n", c=8), [pk_], [f"XT4_{q}" for q in range(8)])

        TILES_F = [(i * 342, 342) for i in range(6)]

        def rmsnorm(gbase, SQ, RS, tiles=TILES, xk="XT", nk="XN"):
            for ti, (t0, n) in enumerate(tiles):
                act(SQ[:, :, 0:n], XT[:, :, t0:t0 + n], AF.Square, [f"{xk}{ti}_{c}" for c in range(8)], ["SQ"])
                ps, pk_ = bank()
                for c in range(8):
                    mm(ps[:, 0:n], ODM, SQ[:, c, 0:n], c == 0, c == 7, ["SQ", "CBS"], [pk_])
                if isinstance(RS, list):
                    RSt, krs = RS[ti % len(RS)], f"RS{ti % len(RS)}"
                else:
                    RSt, krs = RS, "RS"
                rstd_from(ps[:, 0:n], RSt[:, 0:n], [pk_], krs)
                for c in range(8):
                    stt(XN[:, c, t0:t0 + n], XT[:, c, t0:t0 + n], pk(gbase + c), RSt[:, 0:n], ALU.mult, ALU.mult,
                        [f"{xk}{ti}_{c}", krs, "CFS"], [f"{nk}{ti}"])

        def load_w(slot_ap, dram_ap, key):
            P.dma("pool", slot_ap, dram_ap, (), [key])

        def ffn_loadG(layer, G, first):
            wu = w_up[layer].rearrange("(k p) f -> p k f", p=128)
            wd = w_dn[layer].rearrange("(f p) j -> p f j", p=128)
            su, sd = (first + 2 * G) % 4, (first + 2 * G + 1) % 4
            load_w(wslot(su).rearrange("p (k f) -> p k f", k=8), wu[:, :, G * 512:(G + 1) * 512], f"W{su}")
            load_w(wslot(sd).rearrange("p (f j) -> p f j", f=4), wd[:, G * 4:(G + 1) * 4, :], f"W{sd}")

        def ffn(layer, first):
            P.barrier()
            ar_reset()
            SQ = ar(8 * 512).rearrange("p (c n) -> p c n", c=8)
            RS = [ar(512, F32) for _ in range(3)]
            R = [ar(512, F32), ar(512, F32)]
            H = [ar(4 * 512).rearrange("p (f n) -> p f n", f=4) for _ in range(2)]
            set_banks(range(8))
            rmsnorm(PK_FFNG + 8 * layer, SQ, RS, TILES_F, "XF", "XNF")
            hb = 0
            prevd = None
            for G in range(8):
                su, sd = (first + 2 * G) % 4, (first + 2 * G + 1) % 4
                WU = wslot(su).rearrange("p (k f) -> p k f", k=8)
                WD = wslot(sd).rearrange("p (f j) -> p f j", f=4)

                def up_part(ti, t0, n, Hc, hk, WU=WU, su=su):
                    for f in range(4):
                        ps, pk_ = bank()
                        for k in range(8):
                            mm(ps[:, 0:n], WU[:, k, f * 128:(f + 1) * 128], XN[:, k, t0:t0 + n], k == 0, k == 7,
                               [f"W{su}", f"XNF{ti}"], [pk_])
                        r = R[f % 2]
                        act(r[:, 0:n], ps[:, 0:n], AF.Relu, [pk_], [f"R{f % 2}"])
                        act(Hc[:, f, 0:n], r[:, 0:n], AF.Square, [f"R{f % 2}"], [f"{hk}_{f}"])

                def down_part(ti, t0, n, Hc, hk, WD=WD, sd=sd):
                    for j in range(8):
                        ps, pk_ = bank()
                        for f in range(4):
                            mm(ps[:, 0:n], WD[:, f, j * 128:(j + 1) * 128], Hc[:, f, 0:n], f == 0, f == 3,
                               [f"W{sd}", f"{hk}_{f}"], [pk_])
                        tt("dve", XT[:, j, t0:t0 + n], XT[:, j, t0:t0 + n], ps[:, 0:n], ALU.add, [pk_, f"XF{ti}_{j}"], [f"XF{ti}_{j}"])

                for ti, (t0, n) in enumerate(TILES_F):
                    Hc, hk = H[hb], f"H{hb}"
                    hb ^= 1
                    up_part(ti, t0, n, Hc, hk)
                    if prevd is not None:
                        prevd[0](*prevd[1])
                    prevd = (down_part, (ti, t0, n, Hc, hk))
                    if ti == 0 and G + 1 < 8:
                        ffn_loadG(layer, G + 1, first)
            prevd[0](*prevd[1])

        def ab_layer(layer):
            jl = layer // 2
            P.barrier()
            ar_reset()
            COS = ar(T, F32)
            SIN = ar(T, F32)
            ATT = ar(4 * T).rearrange("p (c n) -> p c n", c=4)
            RS = ar(512, F32)
            base = arpos[0]
            SQ = RING[:, 3, :].rearrange("p (c n) -> p c n", c=8)
            set_banks(range(8))
            P.dma("sp", COS[:, :], cos_d[:, :], (), ["COS"])
            P.dma("sp", SIN[:, :], sin_d[:, :], (), ["SIN"])
            win = w_in[jl].rearrange("(k p) f -> p k f", p=128)
            wo = w_out[jl].rearrange("(k p) j -> p k j", p=128)
            WU = wslot(0).rearrange("p (k f) -> p k f", k=8)
            WVB = wslot(1).rearrange("p (k f) -> p k f", k=8)
            WOG = wslot(2).rearrange("p (k j) -> p k j", k=4)
            load_w(WU, win[:, :, 1536:2048], "W0")
            load_w(WVB, win[:, :, 2048:2560], "W1")
            load_w(WOG, wo[:, 4:8, :], "W2")
            rmsnorm(PK_MIXG + 8 * layer, SQ, RS)
            ar_reset(base)
            UT = ar(4 * 512).rearrange("p (c n) -> p c n", c=4)
            VBT = ar(4 * 512).rearrange("p (b f) -> p b f", b=4)
            VGL = [ar(512, F32), ar(512, F32)]
            TMPG = ar(512, F32).rearrange("p (c i) -> p c i", c=4)
            WS = ar(8 * 128).rearrange("p (g i) -> p g i", g=8)
            WSF = ar(8 * 128, F32)
            BT = ar(4 * 128, F32).rearrange("p (c i) -> p c i", c=4)
            VBGt = ar(512, F32)
            VBBt = ar(512, F32)
            ST6L = [ar(8, F32), ar(8, F32)]
            MVL = [ar(4, F32), ar(4, F32)]
            vgc = [0]
            VST4 = ar(4 * NS, F32).rearrange("p (c n) -> p c n", c=4)
            P.dma("sp", WSF[:, :], wsT_d[jl], (), ["WSF"])
            P.dma("sp", BT[:, :, :], bT_d[jl].rearrange("p (c i) -> p c i", c=4), (), ["BT"])
            P.dma("sp", VBGt[:, :], vbg_d[jl], (), ["VBG"])
            P.dma("sp", VBBt[:, :], vbb_d[jl], (), ["VBB"])
            for q4 in range(2):
                tt("dve", WS[:, q4 * 4:(q4 + 1) * 4, :].rearrange("p g i -> p (g i)"), WSF[:, q4 * 512:(q4 + 1) * 512],
                   CBS[:, B_MULT:B_MULT + 512], ALU.mult, ["WSF", "CBS"], ["WS"])
            def gate_vb(ti, t0, n):
                nblk = 4 if ti < 4 else 1
                for bl in range(nblk):
                    m = 128 if ti < 4 else NS
                    c0 = t0 + bl * 128
                    vi = vgc[0] % 2
                    vgc[0] += 1
                    VG, ST6, MV = VGL[vi], ST6L[vi], MVL[vi]
                    kvg, kst, kmv = f"VG{vi}", f"ST6{vi}", f"MV{vi}"
                    ps, pk_ = bank()
                    for k in range(8):
                        mm(ps[0:m, :], XN[:, k, c0:c0 + m], WVB[:, k, :], k == 0, k == 7, ["W1", f"XN{ti}"], [pk_])
                    act(VG[0:m, :], ps[0:m, :], AF.Gelu_apprx_tanh, [pk_], [kvg])
                    P.op("dve", lambda e, m=m, ST6=ST6, VG=VG: e.bn_stats(out=ST6[0:m, 0:6], in_=VG[0:m, :]), [kvg], [kst])
                    P.op("dve", lambda e, m=m, ST6=ST6, MV=MV: e.bn_aggr(out=MV[0:m, 0:2], in_=ST6[0:m, 0:6]), [kst], [kmv])
                    rstd_from(MV[0:m, 1:2], MV[0:m, 2:3], [kmv], kmv)
                    tsc("dve", VG[0:m, :], VG[0:m, :], MV[0:m, 0:1], MV[0:m, 2:3], ALU.subtract, ALU.mult, [kvg, kmv], [kvg])
                    tt("dve", VG[0:m, :], VG[0:m, :], VBGt[0:m, :], ALU.mult, [kvg, "VBG"], [kvg])
                    tt("dve", VG[0:m, :], VG[0:m, :], VBBt[0:m, :], ALU.add, [kvg, "VBB"], [kvg])
                    if ti < 4:
                        cp("act", VBT[:, bl, :], VG[:, :], [kvg], [f"VBT{bl}"])
                        if ti == 3 and bl == 3:
                            P.dma("sp", bv_p[jl], VG[:, :], [kvg], [])
                    else:
                        P.dma("sp", bv_s[jl], VG[0:NS, :], [kvg], [])
                        ps2, pk2 = bank()
                        for c in range(4):
                            tr(ps2[:, c * NS:(c + 1) * NS], VG[0:NS, c * 128:(c + 1) * 128], IDF[0:NS, 0:NS], [kvg, "CFS"], [pk2])
                        for c in range(4):
                            tsc("dve", VST4[:, c, :], ps2[:, c * NS:(c + 1) * NS], pk(PK_WS00 + 4 * jl + c), pk(PK_BS0 + 4 * jl + c),
                                ALU.mult, ALU.add, [pk2, "CFS"], ["VST4"])
            def gate_u(ti, t0, n):
                for c in range(4):
                    ps, pk_ = bank()
                    for k in range(8):
                        mm(ps[:, 0:n], WU[:, k, c * 128:(c + 1) * 128], XN[:, k, t0:t0 + n], k == 0, k == 7, ["W0", f"XN{ti}"], [pk_])
                    act(UT[:, c, 0:n], ps[:, 0:n], AF.Gelu_apprx_tanh, [pk_], ["UT"])
                if ti == 4:
                    tt("dve", UT[:, :, 0:NS], UT[:, :, 0:NS], VST4[:, :, :], ALU.mult, ["UT", "VST4"], ["UT"])
            def gate_sp(ti, t0, n):
                if ti < 4:
                    for bl in range(4):
                        ps, pk_ = bank()
                        for g in range(8):
                            hp = (g % 2) * 64
                            cc = g // 2
                            mm(ps[hp:hp + 64, cc * 128:(cc + 1) * 128], VBT[:, bl, g * 64:(g + 1) * 64], WS[:, g, :], True, True,
                               [f"VBT{bl}", "WS"], [pk_])
                        tt("dve", TMPG[:, :, :], ps[:, :].rearrange("p (c i) -> p c i", c=4), BT[:, :, :], ALU.add, [pk_, "BT"], ["TMPG"])
                        tt("dve", UT[:, :, bl * 128:(bl + 1) * 128], UT[:, :, bl * 128:(bl + 1) * 128], TMPG[:, :, :], ALU.mult,
                           ["UT", "TMPG"], ["UT"])
            def gate_op(ti, t0, n):
                for j in range(8):
                    ps, pk_ = bank()
                    for k in range(4):
                        mm(ps[:, 0:n], WOG[:, k, j * 128:(j + 1) * 128], UT[:, k, 0:n], k == 0, k == 3, ["W2", "UT"], [pk_])
                    tt("dve", XT[:, j, t0:t0 + n], XT[:, j, t0:t0 + n], ps[:, 0:n], ALU.add, [pk_, f"XT{ti}_{j}"], [f"XT{ti}_{j}"])

            gt = TILES if 'gate' not in SKIP else []
            for ti, (t0, n) in enumerate(gt):
                gate_vb(ti, t0, n)
                if ti > 0:
                    gate_op(ti - 1, *gt[ti - 1])
                gate_u(ti, t0, n)
                gate_sp(ti, t0, n)
            if gt:
                gate_op(len(gt) - 1, *gt[-1])
            P.barrier()
            ar_reset(base)
            QTZ = [ar(T), ar(T)]
            KT = ar(T)
            A2 = [ar(512, F32), ar(512, F32)]
            Bq = ar(512)
            C2 = [ar(512, F32), ar(512, F32)]
            Dd = ar(512, F32)
            RDEN = ar(512, F32)
            KST = RDEN
            VSTG = RDEN
            Kc = ar(4 * 128, F32).rearrange("p (s f) -> p s f", s=4)
            Vc = ar(4 * 128, F32).rearrange("p (s f) -> p s f", s=4)
            ROW = ar(384, F32)
            PROD = Dd[:, :].rearrange("p (s f) -> p s f", s=4)
            SS = ar(8, F32)
            PM = ar(8, F32)
            RD = ar(2, F32)
            DEN2 = ar(2, F32)
            OROW = ar(128, F32)
            QSF = ar(NS, F32)
            KSF = ar(NS, F32)
            PT = [RING[:, 3, 2048 + 512 * i:2048 + 512 * (i + 1)] for i in range(4)]
            ptc = [0]
            itc = [0]

            def vtm(d, b):
                if d < 2:
                    return RING[:, 2, d * 2048 + b * 128:d * 2048 + (b + 1) * 128]
                return RING[:, 3, b * 128:(b + 1) * 128]

            def vtm_grp(d, grp):
                if d < 2:
                    return RING[:, 2, d * 2048 + grp * 512:d * 2048 + (grp + 1) * 512]
                return RING[:, 3, grp * 512:(grp + 1) * 512]

            memset("dve", Kc[:, :, :], 0.0, ["Kc"])
            memset("dve", Vc[:, :, :], 0.0, ["Vc"])
            memset("dve", QTZ[0][64:128, :], 0.0, ["QT"])
            memset("dve", QTZ[1][0:64, :], 0.0, ["QT"])
            ACCS = [(PS[5], "ps5"), (PS[6], "ps6")]
            DEN, kd = PS[7], "ps7"
            ZRHS = CBS[:, B_MASK:B_MASK + 512]
            def pair_w(c):
                sl = c % 2
                return (wslot(sl)[:, 0:1024].rearrange("p (k f) -> p k f", k=8),
                        wslot(sl)[:, 1024:2048].rearrange("p (k f) -> p k f", k=8),
                        wslot(sl)[:, 2048:3072].rearrange("p (k f) -> p k f", k=8), f"W{sl}")

            def load_pair_w(c):
                WQ_, WK_, WV_, wk_ = pair_w(c)
                load_w(WQ_, win[:, :, c * 128:(c + 1) * 128], wk_)
                load_w(WK_, win[:, :, 512 + c * 128:512 + (c + 1) * 128], wk_)
                load_w(WV_, win[:, :, 1024 + c * 128:1024 + (c + 1) * 128], wk_)

            for c in range(4 if 'attn' not in SKIP else 0):
                set_banks(range(8))
                WQ, WK, WV, wkey = pair_w(c)
                if c == 0:
                    load_pair_w(0)
                if c + 1 < 4:
                    load_pair_w(c + 1)
                def vperm_gen(c=c, WV=WV, wkey=wkey):
                    for d in range(3 if 'vperm' not in SKIP else 0):
                        dil = DILS[d]
                        for grp in range(4):
                            ps, pk_ = bank()
                            for bi in range(4):
                                b = grp * 4 + bi
                                if d == 0:
                                    start = 128 * b
                                elif d == 1:
                                    start = 512 * (b // 4) + (b % 4)
                                else:
                                    start = b
                                for k in range(8):
                                    mm(ps[:, bi * 128:(bi + 1) * 128], XN[:, k, ss(start, 128, dil)], WV[:, k, :], k == 0, k == 7,
                                       [wkey, "XN0", "XN1", "XN2", "XN3"], [pk_])
                            cp("act", vtm_grp(d, grp), ps[:, :], [pk_], ["VTM"])
                            if d == 0:
                                cp("act", VSTG[:, :], ps[:, :], [pk_], ["RDEN"])
                                P.dma("sp", av_p[jl, grp * 512:(grp + 1) * 512, c * 128:(c + 1) * 128].rearrange("(b p) f -> p b f", p=128),
                                      VSTG[:, :].rearrange("p (b f) -> p b f", b=4), ["RDEN"], [])
                            yield

                vg = vperm_gen()

                def vfill():
                    try:
                        next(vg)
                        return True
                    except StopIteration:
                        return False

                def emit_tr(A, kA, t0, c=c):
                    ps4, pk4 = bank()
                    for bl in range(4):
                        tr(ps4[:, bl * 128:(bl + 1) * 128], A[:, bl * 128:(bl + 1) * 128], IDF, [kA, "CFS"], [pk4])
                    cp("act", KST[:, :], ps4[:, :], [pk4], ["RDEN"])
                    P.dma("sp", ak_p[jl, t0:t0 + 512, c * 128:(c + 1) * 128].rearrange("(b p) f -> p b f", p=128),
                          KST[:, :].rearrange("p (b f) -> p b f", b=4), ["RDEN"], [])

                pending = None
                for ti, (t0, n) in enumerate(TILES if 'qk' not in SKIP else []):
                    for which in range(2):
                        ib = itc[0] % 2
                        itc[0] += 1
                        A, kA = A2[ib], f"A{ib}"
                        Cc, kC = C2[ib], f"C{ib}"
                        W_ = WQ if which == 0 else WK
                        gcol = pk((PK_QG if which == 0 else PK_KG) + jl)
                        ps, pk_ = bank()
                        for k in range(8):
                            mm(ps[:, 0:n], W_[:, k, :], XN[:, k, t0:t0 + n], k == 0, k == 7, [wkey, f"XN{ti}"], [pk_])
                        act(A[:, 0:n], ps[:, 0:n], AF.Identity, [pk_, "CFS"], [kA], scale=gcol)
                        act(Bq[:, 0:n], ps[:, 0:n], AF.Square, [pk_], ["B"])
                        vfill()
                        ps2, pk2 = bank()
                        mm(ps2[:, 0:n], HM, Bq[:, 0:n], True, True, ["B", "CBS"], [pk2])
                        ps3, pk3 = bank()
                        mm(ps3[:, 0:n], ROT, A[:, 0:n], True, True, [kA, "CFS"], [pk3])
                        rstd_from(ps2[:, 0:n], Cc[:, 0:n], [pk2], kC)
                        tt("dve", Dd[:, 0:n], ps3[:, 0:n], SIN[:, t0:t0 + n], ALU.mult, [pk3, "SIN"], ["D"])
                        tt("dve", A[:, 0:n], A[:, 0:n], COS[:, t0:t0 + n], ALU.mult, [kA, "COS"], [kA])
                        tt("dve", A[:, 0:n], A[:, 0:n], Dd[:, 0:n], ALU.add, [kA, "D"], [kA])
                        tt("dve", A[:, 0:n], A[:, 0:n], Cc[:, 0:n], ALU.mult, [kA, kC], [kA])
                        if which == 0:
                            cp("act", QTZ[0][0:64, t0:t0 + n], A[0:64, 0:n], [kA], ["QT"])
                            cp("act", QTZ[1][64:128, t0:t0 + n], A[64:128, 0:n], [kA], ["QT"])
                        else:
                            cp("act", KT[:, t0:t0 + n], A[:, 0:n], [kA], ["KT"])
                        if ti == 4:
                            cp("dve", (QSF if which == 0 else KSF)[:, 0:NS], A[:, 0:NS], [kA], ["QSF" if which == 0 else "KSF"])
                        if pending is not None:
                            emit_tr(*pending)
                            pending = None
                        if which == 1 and ti < 4:
                            pending = (A, kA, t0)
                if pending is not None:
                    emit_tr(*pending)
                while vfill():
                    pass

                def sample_chain(c=c, WV=WV, wkey=wkey):
                    for nq in range(NS if 'sattn' not in SKIP else 0):
                        ps, pk_ = bank()
                        tr(ps[0:1, 0:128], QSF[:, nq:nq + 1], IDF, ["QSF", "CFS"], [pk_])
                        tr(ps[0:1, 128:256], KSF[:, nq:nq + 1], IDF, ["KSF", "CFS"], [pk_])
                        for k in range(8):
                            mm(ps[0:1, 256:384], XN[:, k, SEQ + nq:SEQ + nq + 1], WV[:, k, :], k == 0, k == 7, [wkey, "XN4"], [pk_])
                        for d in range(3):
                            dil = DILS[d]
                            r0 = WINBUF - 128 * dil
                            P.dma("sp", Kc[:, d, :], ck[jl, nq, ss(r0, 128, dil), c * 128:(c + 1) * 128], (), ["Kc"])
                            P.dma("sp", Vc[:, d, :], cv[jl, nq, ss(r0, 128, dil), c * 128:(c + 1) * 128], (), ["Vc"])
                        yield
                        cp("act", ROW[0:1, 0:384], ps[0:1, 0:384], [pk_], ["ROW"])
                        yield
                        cp("dve", Kc[0:1, 3, :], ROW[0:1, 128:256], ["ROW"], ["Kc"])
                        cp("dve", Vc[0:1, 3, :], ROW[0:1, 256:384], ["ROW"], ["Vc"])
                        P.dma("sp", ak_s[jl, nq:nq + 1, c * 128:(c + 1) * 128], ROW[0:1, 128:256], ["ROW"], [])
                        P.dma("sp", av_s[jl, nq:nq + 1, c * 128:(c + 1) * 128], ROW[0:1, 256:384], ["ROW"], [])
                        ps2, pk2 = bank()
                        mm(ps2[:, 0:128], ONEF[0:1, :], ROW[0:1, 0:128], True, True, ["ROW", "CFS"], [pk2])
                        yield
                        for sl4 in range(4):
                            tt("dve", PROD[:, sl4, :], Kc[:, sl4, :], ps2[:, 0:128], ALU.mult, ["Kc", pk2], ["PROD"])
                        P.op("dve", lambda e: e.tensor_reduce(out=SS[:, 0:8], in_=PROD[:, :, :].rearrange("p s (h d) -> p (s h) d", h=2),
                                                              axis=AX.X, op=ALU.add), ["PROD"], ["SS"])
                        yield
                        act(PM[:, 0:8], SS[:, 0:8], AF.Exp, ["SS"], ["PM"], scale=0.125)
                        yield
                        tt("dve", PM[:, 0:8], PM[:, 0:8], PMASK, ALU.mult, ["PM", "CFS"], ["PM"])
                        yield
                        ps3, pk3 = bank()
                        for h in range(2):
                            for sl4 in range(4):
                                mm(ps3[0:1, h * 64:(h + 1) * 64], PM[:, sl4 * 2 + h:sl4 * 2 + h + 1], Vc[:, sl4, h * 64:(h + 1) * 64],
                                   sl4 == 0, sl4 == 3, ["PM", "Vc"], [pk3])
                        mm(ps3[0:1, 128:136], ONEF[:, 0:1], PM[:, 0:8], True, True, ["PM", "CFS"], [pk3])
                        yield
                        P.op("dve", lambda e, ps3=ps3: e.tensor_reduce(out=DEN2[0:1, 0:2], in_=ps3[0:1, 128:136].rearrange("p (s h) -> p h s", h=2),
                                                                        axis=AX.X, op=ALU.add), [pk3], ["DEN2"])
                        P.op("dve", lambda e: e.reciprocal(out=RD[0:1, 0:2], in_=DEN2[0:1, 0:2]), ["DEN2"], ["RD"])
                        for h in range(2):
                            tsc("dve", OROW[0:1, h * 64:(h + 1) * 64], ps3[0:1, h * 64:(h + 1) * 64], RD[0:1, h:h + 1], None, ALU.mult, None,
                                [pk3, "RD"], ["OROW"])
                        yield
                        ps4, pk4 = bank()
                        tr(ps4[:, 0:1], OROW[0:1, 0:128], IDF[0:1, 0:1], ["OROW", "CFS"], [pk4])
                        yield
                        cp("act", ATT[:, c, SEQ + nq:SEQ + nq + 1], ps4[:, 0:1], [pk4], ["ATT"])
                        yield

                gen = sample_chain()
                set_banks(range(5))

                def advance(k=1):
                    for _ in range(k):
                        try:
                            next(gen)
                        except StopIteration:
                            return False
                    return True

                items = []
                for s in range(4 if 'pattn' not in SKIP else 0):
                    for h in range(2):
                        bl_ = []
                        b1 = [(4 * s + i, 512 * s + 128 * i, 1, 128 * i, 1) for i in range(4)]
                        bl_.append((0, b1, 128, 0))
                        b2 = [(4 * s + i - 1, 512 * s + 128 * i, 1, 128 * i, 1) for i in range(4) if 4 * s + i >= 1]
                        bl_.append((0, b2, 128, 1))
                        b3 = [(4 * s + r, 512 * s + r, 4, r, 4) for r in range(4)]
                        bl_.append((1, b3, 128, 0))
                        if s >= 1:
                            b4 = [(4 * (s - 1) + r, 512 * s + r, 4, r, 4) for r in range(4)]
                            bl_.append((1, b4, 128, 1))
                        b5 = [(r, r + 16 * 32 * s, 16, r, 16) for r in range(16)]
                        bl_.append((2, b5, 32, 2 + s))
                        for bi_, bt in enumerate(bl_):
                            items.append((s, h, bt, h == 0 and bi_ == 0, h == 1 and bi_ == len(bl_) - 1))

                def issue_qk(it):
                    s, h, (d, tl, w, mi), _, _ = it
                    dil = DILS[d]
                    ps, pk_ = bank()
                    off = 512 - w * len(tl)
                    for i, (kb, q0, qst, c0, cst) in enumerate(tl):
                        if d == 0:
                            k0 = 128 * kb
                        elif d == 1:
                            k0 = 512 * (kb // 4) + (kb % 4)
                        else:
                            k0 = kb
                        mm(ps[:, off + i * w:off + (i + 1) * w], KT[:, ss(k0, 128, dil)], QTZ[h][:, ss(q0, w, qst)],
                           i == 0, False, ["KT", "QT"], [pk_])
                    mm(ps[:, off:512], IDB, MASK(mi)[:, off:512], False, True, ["CBS", pk_], [pk_])
                    pt = PT[ptc[0] % 4]
                    ptk = f"PT{ptc[0] % 4}"
                    ptc[0] += 1
                    act(pt[:, off:512], ps[:, off:512], AF.Exp, [pk_], [ptk], scale=0.125)
                    return pt, ptk, off

                def issue_pv(it, info):
                    s, h, (d, tl, w, mi), first, last = it
                    pt, ptk, off = info
                    ACC, ka = ACCS[h]
                    if first:
                        mm(ACCS[0][0][:, :], ZER, ZRHS, True, False, ["CBS"], [ACCS[0][1]])
                        mm(ACCS[1][0][:, :], ZER, ZRHS, True, False, ["CBS"], [ACCS[1][1]])
                        mm(DEN[:, :], ZER, ZRHS, True, False, ["CBS"], [kd])
                    for i, (kb, q0, qst, c0, cst) in enumerate(tl):
                        mm(ACC[:, ss(c0, w, cst)], vtm(d, kb), pt[:, off + i * w:off + (i + 1) * w],
                           False, False, [ptk, "VTM", ka], [ka])
                    nt = len(tl)
                    if d == 0:
                        mm(DEN[:, off:512], EH(h), pt[:, off:512], False, False, [ptk, "CBS", kd], [kd])
                    else:
                        for i, (kb, q0, qst, c0, cst) in enumerate(tl):
                            mm(DEN[:, ss(c0, w, cst)], EH(h), pt[:, off + i * w:off + (i + 1) * w],
                               False, False, [ptk, "CBS", kd], [kd])
                    if last:
                        mm(ACCS[0][0][:, :], ZER, ZRHS, False, True, ["CBS", ACCS[0][1]], [ACCS[0][1]])
                        mm(ACCS[1][0][:, :], ZER, ZRHS, False, True, ["CBS", ACCS[1][1]], [ACCS[1][1]])
                        mm(DEN[:, :], ZER, ZRHS, False, True, ["CBS", kd], [kd])
                        act(RDEN[:, :], DEN[:, :], AF.Ln, [kd], ["RDEN"])
                        act(RDEN[:, :], RDEN[:, :], AF.Exp, ["RDEN"], ["RDEN"], scale=-1.0)
                        tt("dve", ATT[0:64, c, 512 * s:512 * (s + 1)], ACCS[0][0][0:64, :], RDEN[0:64, :], ALU.mult,
                           [ACCS[0][1], "RDEN"], ["ATT"])
                        tt("dve", ATT[64:128, c, 512 * s:512 * (s + 1)], ACCS[1][0][64:128, :], RDEN[64:128, :], ALU.mult,
                           [ACCS[1][1], "RDEN"], ["ATT"])

                prev = None
                for idx in range(len(items) + 1):
                    cur = items[idx] if idx < len(items) else None
                    info = issue_qk(cur) if cur is not None else None
                    if prev is not None:
                        issue_pv(prev[0], prev[1])
                    prev = (cur, info) if cur is not None else None
                    advance(1)
                while advance(1):
                    pass
            P.barrier()
            if 'ffn' not in SKIP:
                ffn_loadG(layer, 0, 2)
            set_banks(range(8))
            WOA = wslot(0).rearrange("p (k j) -> p k j", k=4)
            load_w(WOA, wo[:, 0:4, :], "W0")
            for ti, (t0, n) in enumerate(TILES if 'oproj' not in SKIP else []):
                for j in range(8):
                    ps, pk_ = bank()
                    for k in range(4):
                        mm(ps[:, 0:n], WOA[:, k, j * 128:(j + 1) * 128], ATT[:, k, t0:t0 + n], k == 0, k == 3, ["W0", "ATT"], [pk_])
                    tt("dve", XT[:, j, t0:t0 + n], XT[:, j, t0:t0 + n], ps[:, 0:n], ALU.add, [pk_, f"XT{ti}_{j}"], [f"XT{ti}_{j}"])

        def c_layer(layer):
            jl = layer // 2
            P.barrier()
            ar_reset()
            GW = 30 + SEQ + 8
            GLU = ar(8 * GW).rearrange("p (c n) -> p c n", c=8)
            GTAIL = ar(8 * 30, F32).rearrange("p (c n) -> p c n", c=8)
            GSF = ar(8 * NS, F32).rearrange("p (c n) -> p c n", c=8)
            XC = ar(NS * 8 * 31, F32).rearrange("p (n c j) -> p n c j", n=NS, c=8)
            base = arpos[0]
            SQ = ar(8 * 512).rearrange("p (c n) -> p c n", c=8)
            RS = ar(512, F32)
            SG = ar(512, F32)
            STG4 = [ar(D, F32) for _ in range(NS)]
            set_banks(range(8))
            wci = w_ci[jl].rearrange("(k p) f -> p k f", p=128)
            wco = w_co[jl].rearrange("(k p) j -> p k j", p=128)

            def load_ci(c):
                sl = c % 2
                WA_ = wslot(sl)[:, 0:1024].rearrange("p (k f) -> p k f", k=8)
                WG_ = wslot(sl)[:, 1024:2048].rearrange("p (k f) -> p k f", k=8)
                load_w(WA_, wci[:, :, c * 128:(c + 1) * 128], f"W{sl}")
                load_w(WG_, wci[:, :, D + c * 128:D + (c + 1) * 128], f"W{sl}")
                return WA_, WG_, f"W{sl}"

            nxt = load_ci(0)
            for nq in range(NS):
                P.dma("sp", STG4[nq][0:30, :], stc[jl, nq], (), [f"STG4{nq}"])
                P.dma("sp", cc_s[jl, nq, 0:29, :], STG4[nq][1:30, :], [f"STG4{nq}"], [])
            rmsnorm(PK_MIXG + 8 * layer, SQ, RS)
            memset("dve", GLU[:, :, 0:30], 0.0, ["GLU"])
            for c in range(8):
                WA_, WG_, wkey = nxt
                if c + 1 < 8:
                    nxt = load_ci(c + 1)
                for ti, (t0, n) in enumerate(TILES):
                    psA, ka = bank()
                    for k in range(8):
                        mm(psA[:, 0:n], WA_[:, k, :], XN[:, k, t0:t0 + n], k == 0, k == 7, [wkey, f"XN{ti}"], [ka])
                    psG, kg = bank()
                    for k in range(8):
                        mm(psG[:, 0:n], WG_[:, k, :], XN[:, k, t0:t0 + n], k == 0, k == 7, [wkey, f"XN{ti}"], [kg])
                    act(SG[:, 0:n], psG[:, 0:n], AF.Sigmoid, [kg], ["SG"])
                    if ti < 4:
                        tt("dve", GLU[:, c, 30 + t0:30 + t0 + n], psA[:, 0:n], SG[:, 0:n], ALU.mult, [ka, "SG"], ["GLU"])
                        if ti == 3:
                            tt("dve", GTAIL[:, c, :], psA[:, 482:512], SG[:, 482:512], ALU.mult, [ka, "SG"], ["GTAIL"])
                    else:
                        tt("dve", GSF[:, c, :], psA[:, 0:NS], SG[:, 0:NS], ALU.mult, [ka, "SG"], ["GSF"])
                if c == 1:
                    for nq in range(NS):
                        ps, pk_ = bank()
                        for c2 in range(8):
                            tr(ps[:, c2 * 30:(c2 + 1) * 30], STG4[nq][0:30, c2 * 128:(c2 + 1) * 128], IDF[0:30, 0:30],
                               [f"STG4{nq}", "CFS"], [pk_])
                        cp("act", XC[:, nq, :, 0:30], ps[:, 0:240].rearrange("p (c j) -> p c j", c=8), [pk_], ["XC"])
            P.barrier()
            if 'ffn' not in SKIP:
                ffn_loadG(layer, 0, 0)
            xnf = XN[:, :, :].rearrange("p c n -> p (c n)")
            Y = xnf[:, 0:8192].bitcast(F32).rearrange("p (c n) -> p c n", c=8)
            STt = xnf[:, 8192:12288].rearrange("p (c n) -> p c n", c=8)
            DG0 = xnf[:, 12288:12288 + 31 * 128].rearrange("p (j f) -> p j f", j=31)
            ar_reset(base)
            DGS = [DG0, ar(31 * 128).rearrange("p (j f) -> p j f", j=31)]
            MEAN = ar(512, F32)
            VAR = ar(512, F32)
            RS2 = ar(512, F32)
            DtL = [ar(512, F32), ar(512, F32)]
            YBL = [ar(512), ar(512)]
            YSL = [ar(512), ar(512)]
            STG = ar(D, F32)
            PRC = ar(8 * 31, F32).rearrange("p (c j) -> p c j", c=8)
            WCO0 = wslot(2).rearrange("p (k j) -> p k j", k=4)
            WCO1 = wslot(3).rearrange("p (k j) -> p k j", k=4)
            load_w(WCO0, wco[:, 0:4, :], "W2")
            load_w(WCO1, wco[:, 4:8, :], "W3")
            WDW = pk(PK_WDW + jl * 248, 248).rearrange("p (c j) -> p c j", c=8)
            psM, km = PS[6], "ps6"
            psE, ke = PS[7], "ps7"
            set_banks(range(6))

            def tail_outputs():
                for half in range(2):
                    ps, pk_ = bank()
                    for cc in range(4):
                        c = half * 4 + cc
                        tr(ps[0:30, cc * 128:(cc + 1) * 128], GTAIL[:, c, :], IDF, ["GTAIL", "CFS"], [pk_])
                    cp("act", STG[0:30, half * 512:(half + 1) * 512], ps[0:30, :], [pk_], ["STG"])
                P.dma("sp", cc_p[jl], STG[0:30, :], ["STG"], [])
                for nq in range(NS):
                    cp("dve", XC[:, nq, :, 30:31], GSF[:, :, nq:nq + 1], ["GSF"], ["XC"])
                for half in range(2):
                    ps2, pk2 = bank()
                    for cc in range(4):
                        c = half * 4 + cc
                        tr(ps2[0:NS, cc * 128:(cc + 1) * 128], GSF[:, c, 0:NS], IDF, ["GSF", "CFS"], [pk2])
                    cp("act", STG[0:NS, half * 512:(half + 1) * 512], ps2[0:NS, :], [pk2], ["STG"])
                P.dma("sp", cc_s[jl, :, 29, :], STG[0:NS, :], ["STG"], [])

            def build_dg(c):
                DG = DGS[c % 2]
                dgk = f"DG{c % 2}"
                for j in range(31):
                    if j % 4 == 0:
                        act(DG[:, j, :], IDB, AF.Identity, ["CBS", "CFS"], [f"{dgk}_{j}"], scale=pk(PK_WDW + jl * 248 + c * 31 + j))
                    else:
                        tsc("dve", DG[:, j, :], IDB, pk(PK_WDW + jl * 248 + c * 31 + j), None, ALU.mult, None, ["CBS", "CFS"], [f"{dgk}_{j}"])

            def conv_stats(ti, t0, n):
                for c in range(8):
                    YB, YS = YBL[c % 2], YSL[c % 2]
                    kyb, kys = f"YB{c % 2}", f"YS{c % 2}"
                    bias = pk(PK_BDW + 8 * jl + c)
                    if ti < 4:
                        DG = DGS[c % 2]
                        dgk = f"DG{c % 2}"
                        ps, pk_ = bank()
                        for j in range(31):
                            mm(ps[:, 0:n], DG[:, j, :], GLU[:, c, t0 + j:t0 + j + n], j == 0, j == 30, [f"{dgk}_{j}", "GLU"], [pk_])
                        if not (ti == 3 and c >= 6):
                            build_dg((c + 2) % 8)
                        src = ps[:, 0:n]
                        act(Y[:, c, 0:n], src, AF.Identity, [pk_, "CFS"], [f"Y{c}"], bias=bias)
                        act(YB[:, 0:n], src, AF.Identity, [pk_, "CFS"], [kyb], bias=bias)
                        act(YS[:, 0:n], src, AF.Square, [pk_, "CFS"], [kys], bias=bias)
                    else:
                        for nq in range(NS):
                            tt("dve", PRC[:, c, :], XC[:, nq, c, :], WDW[:, c, :], ALU.mult, ["XC", "CFS"], ["PRC"])
                            P.op("dve", lambda e, c=c, nq=nq: e.tensor_reduce(out=Y[:, c, nq:nq + 1], in_=PRC[:, c, :], axis=AX.X, op=ALU.add),
                                 ["PRC"], [f"Y{c}"])
                        act(Y[:, c, 0:n], Y[:, c, 0:n], AF.Identity, [f"Y{c}", "CFS"], [f"Y{c}"], bias=bias)
                        act(YB[:, 0:n], Y[:, c, 0:n], AF.Copy, [f"Y{c}"], [kyb])
                        act(YS[:, 0:n], Y[:, c, 0:n], AF.Square, [f"Y{c}"], [kys])
                    mm(psM[:, 0:n], ODM, YB[:, 0:n], c == 0, c == 7, [kyb, "CBS"], [km])
                    mm(psE[:, 0:n], ODM, YS[:, 0:n], c == 0, c == 7, [kys, "CBS"], [ke])

            def ln_apply(ti, t0, n):
                cp("act", MEAN[:, 0:n], psM[:, 0:n], [km], ["MEAN"])
                tt("dve", VAR[:, 0:n], MEAN[:, 0:n], MEAN[:, 0:n], ALU.mult, ["MEAN"], ["VAR"])
                tt("dve", VAR[:, 0:n], psE[:, 0:n], VAR[:, 0:n], ALU.subtract, [ke, "VAR"], ["VAR"])
                tsc("dve", VAR[:, 0:n], VAR[:, 0:n], 0.0, None, ALU.max, None, ["VAR"], ["VAR"])
                rstd_from(VAR[:, 0:n], RS2[:, 0:n], ["VAR"], "RS2")
                for c in range(8):
                    Dt, kdt = DtL[c % 2], f"Dt{c % 2}"
                    tt("dve", Dt[:, 0:n], Y[:, c, 0:n], MEAN[:, 0:n], ALU.subtract, [f"Y{c}", "MEAN"], [kdt])
                    tt("dve", Dt[:, 0:n], Dt[:, 0:n], RS2[:, 0:n], ALU.mult, [kdt, "RS2"], [kdt])
                    act(STt[:, c, 0:n], Dt[:, 0:n], AF.Silu, [kdt, "CFS"], [f"STt{c}"], scale=pk(PK_CNG + 8 * jl + c), bias=pk(PK_CNB + 8 * jl + c))

            def outproj(ti, t0, n):
                for j in range(8):
                    ps, pk_ = bank()
                    for k in range(8):
                        Wc = WCO0 if k < 4 else WCO1
                        mm(ps[:, 0:n], Wc[:, k % 4, j * 128:(j + 1) * 128], STt[:, k, 0:n], k == 0, k == 7, ["W2", "W3", f"STt{k}"], [pk_])
                    tt("dve", XT[:, j, t0:t0 + n], XT[:, j, t0:t0 + n], ps[:, 0:n], ALU.add, [pk_, f"XT{ti}_{j}"], [f"XT{ti}_{j}"])

            build_dg(0)
            build_dg(1)
            prev = None
            for ti, (t0, n) in enumerate(TILES):
                if ti == 1:
                    tail_outputs()
                conv_stats(ti, t0, n)
                if prev is not None:
                    outproj(*prev)
                ln_apply(ti, t0, n)
                prev = (ti, t0, n)
            outproj(*prev)

        for layer in range(LAYERS):
            if layer % 2 == 0:
                ab_layer(layer)
            else:
                c_layer(layer)
            if 'ffn' not in SKIP:
                ffn(layer, 2 if layer % 2 == 0 else 0)

        P.barrier()
        ar_reset()
        YO = [ar(D, F32), ar(D, F32)]
        set_banks(range(8))
        for b in range(16):
            yo = YO[b % 2]
            for half in range(2):
                ps, pk_ = bank()
                for cc in range(4):
                    c = half * 4 + cc
                    tr(ps[:, cc * 128:(cc + 1) * 128], XT[:, c, b * 128:(b + 1) * 128], IDF, [f"XT{b // 4}_{c}", "CFS"], [pk_])
                cp("act" if half == 0 else "dve", yo[:, half * 512:(half + 1) * 512], ps[:, :], [pk_], [f"YO{b % 2}"])
            P.dma("sp", y_p[b * 128:(b + 1) * 128, :], yo[:, :], [f"YO{b % 2}"], [])
        YS_ = ar(D, F32)
        for half in range(2):
            ps, pk_ = bank()
            for cc in range(4):
                c = half * 4 + cc
                tr(ps[0:NS, cc * 128:(cc + 1) * 128], XT[:, c, SEQ:T], IDF, [f"XT4_{c}", "CFS"], [pk_])
            cp("act", YS_[0:NS, half * 512:(half + 1) * 512], ps[0:NS, :], [pk_], ["YS_"])
        P.dma("sp", y_s[:, :], YS_[0:NS, :], ["YS_"], [])

        P.emit(block)
    return nc


_CACHE = {}


def _consts():
    cf = np.zeros((128, NCF), np.float32)
    cf[:, C_ID:C_ID + 128] = np.eye(128, dtype=np.float32)
    rot = np.zeros((128, 128), np.float32)
    for i in range(128):
        if (i % 64) < 32:
            rot[i + 32, i] = -1.0
        else:
            rot[i - 32, i] = 1.0
    cf[:, C_ROT:C_ROT + 128] = rot
    cf[:, C_ONE:C_ONE + 128] = 1.0
    pm = np.zeros((128, 8), np.float32)
    pm[:, 0:6] = 1.0
    pm[0, 6:8] = 3.0
    cf[:, C_PM:C_PM + 8] = pm
    cb = np.zeros((128, NCB), np.float32)
    cb[:, B_ID:B_ID + 128] = np.eye(128)
    hm = np.zeros((128, 128), np.float32)
    hm[0:64, 0:64] = 1.0 / 64
    hm[64:128, 64:128] = 1.0 / 64
    cb[:, B_HM:B_HM + 128] = hm
    cb[:, B_ODM:B_ODM + 128] = 1.0 / 1024
    cb[:, B_ONE64:B_ONE64 + 64] = 1.0
    j = np.arange(128)[:, None]
    i = np.arange(128)[None, :]
    cur = (j <= i).astype(np.float32)
    prev = (j >= i).astype(np.float32)
    NEG = -30000.0
    cb[:, B_MASK:B_MASK + 512] = np.tile((1.0 - cur) * NEG, (1, 4))
    cb[:, B_MASK + 512:B_MASK + 1024] = np.tile((1.0 - prev) * NEG, (1, 4))
    for s in range(4):
        cb[:, B_MASK + 512 * (2 + s):B_MASK + 512 * (3 + s)] = np.tile((1.0 - cur[:, 32 * s:32 * (s + 1)]) * NEG, (1, 16))
    cb[:, B_MULT:B_MULT + 512] = np.tile(cur, (1, 4))
    cb[:, B_EH:B_EH + 64] = 1.0
    cb[:, B_EH + 128 + 64:B_EH + 256] = 1.0
    inv = (10000.0 ** (-np.arange(0, 64, 2, dtype=np.float32) / 64)).astype(np.float32)
    pos = np.concatenate([np.arange(SEQ, dtype=np.float32), np.full((NS,), float(PAST), np.float32)])
    ang = (pos[None, :] * inv[:, None]).astype(np.float32)
    cosT = np.tile(np.cos(ang).astype(np.float32), (4, 1))
    sinT = np.tile(np.sin(ang).astype(np.float32), (4, 1))
    return cf, cb, np.ascontiguousarray(cosT), np.ascontiguousarray(sinT)


def kernel(x_prompt, x_sample, cache_a_k, cache_a_v, state_c_conv, norm_mix_g, norm_ffn_g, w_ffn_up, w_ffn_down,
           w_in_ab, q_norm_g, k_norm_g, vb_norm_g, vb_norm_b, w_spatial, b_spatial, w_out_ab,
           w_c_in, w_c_dw, b_c_dw, c_norm_g, c_norm_b, w_c_out):
    f = lambda a: np.ascontiguousarray(np.asarray(a, dtype=np.float32))
    x_prompt, x_sample = f(x_prompt), f(x_sample)
    cache_a_k, cache_a_v, state_c_conv = f(cache_a_k), f(cache_a_v), f(state_c_conv)
    if "nc" not in _CACHE:
        _CACHE["nc"] = build_program()
    nc = _CACHE["nc"]
    cf, cb, cosT, sinT = _consts()
    def fm(v):
        v = f(v)
        n = v.shape[0]
        return v.reshape(n, 8, 128).transpose(2, 0, 1).reshape(128, n * 8)
    pkc = np.zeros((128, NPK), np.float32)
    pkc[:, PK_MIXG:PK_MIXG + 32] = fm(norm_mix_g)
    pkc[:, PK_FFNG:PK_FFNG + 32] = fm(norm_ffn_g)
    pkc[:, PK_QG:PK_QG + 2] = np.tile(f(q_norm_g).T, (2, 1))
    pkc[:, PK_KG:PK_KG + 2] = np.tile(f(k_norm_g).T, (2, 1))
    pkc[:, PK_BDW:PK_BDW + 16] = fm(b_c_dw)
    pkc[:, PK_CNG:PK_CNG + 16] = fm(c_norm_g)
    pkc[:, PK_CNB:PK_CNB + 16] = fm(c_norm_b)
    wdw = f(w_c_dw)
    pkc[:, PK_WDW:PK_WDW + 496] = wdw.reshape(2, 31, 8, 128).transpose(3, 0, 2, 1).reshape(128, 496)
    wsp = f(w_spatial)
    bsp = f(b_spatial)
    gidx = (np.arange(128) // 64)[:, None] + 2 * np.arange(4)[None, :]
    for jl in range(2):
        pkc[:, PK_WS00 + 4 * jl:PK_WS00 + 4 * jl + 4] = wsp[jl, :, 0, 0][gidx]
        pkc[:, PK_BS0 + 4 * jl:PK_BS0 + 4 * jl + 4] = bsp[jl, :, 0][gidx]
    cf[:, C_PK:C_PK + NPK] = pkc
    wsT = np.ascontiguousarray(wsp.transpose(0, 3, 1, 2).reshape(2, 128, 8 * 128))
    bT = np.ascontiguousarray(bsp[:, gidx, :].reshape(2, 128, 4 * 128))
    vbg = np.ascontiguousarray(np.broadcast_to(f(vb_norm_g)[:, None, :], (2, 128, 512)))
    vbb = np.ascontiguousarray(np.broadcast_to(f(vb_norm_b)[:, None, :], (2, 128, 512)))
    shared = {"w_up": f(w_ffn_up), "w_dn": f(w_ffn_down), "w_in": f(w_in_ab), "w_out": f(w_out_ab), "w_ci": f(w_c_in),
              "w_co": f(w_c_out), "cf": cf, "cb": cb, "cosT": cosT, "sinT": sinT, "wsT": wsT, "bT": bT, "vbg": vbg, "vbb": vbb}
    in_maps = []
    for i in range(NCORES):
        m = dict(shared)
        m["xp"] = x_prompt[i]
        m["xs"] = np.ascontiguousarray(x_sample[NS * i:NS * (i + 1), 0, :])
        m["ck"] = np.ascontiguousarray(cache_a_k[:, NS * i:NS * (i + 1)].reshape(2, NS, WINBUF, 512))
        m["cv"] = np.ascontiguousarray(cache_a_v[:, NS * i:NS * (i + 1)].reshape(2, NS, WINBUF, 512))
        m["stc"] = np.ascontiguousarray(state_c_conv[:, NS * i:NS * (i + 1)])
        in_maps.append(m)
    res = run_bass_kernel_spmd(nc, in_maps, core_ids=list(range(NCORES)))
    R = res.results
    g = lambda name: [np.asarray(R[i][name], dtype=np.float32) for i in range(NCORES)]
    y_p = np.stack(g("y_p"), 0)
    y_s = np.concatenate(g("y_s"), 0)[:, None, :]
    ak_p = np.stack(g("ak_p"), 1).reshape(2, NCORES, SEQ, 8, 64)
    av_p = np.stack(g("av_p"), 1).reshape(2, NCORES, SEQ, 8, 64)
    ak_s = np.concatenate(g("ak_s"), 1).reshape(2, NS * NCORES, 1, 8, 64)
    av_s = np.concatenate(g("av_s"), 1).reshape(2, NS * NCORES, 1, 8, 64)
    bv_p = np.stack(g("bv_p"), 1)
    bv_s = np.concatenate(g("bv_s"), 1)[:, :, None, :]
    cc_p = np.stack(g("cc_p"), 1)
    cc_s = np.concatenate(g("cc_s"), 1)
    return (y_p, y_s, ak_p, av_p, ak_s, av_s, bv_p, bv_s, cc_p, cc_s)
```

```python
from contextlib import ExitStack
import numpy as np
import concourse.bass as bass
import concourse.mybir as mybir
from concourse.bass_utils import run_bass_kernel_spmd

F32 = mybir.dt.float32
BF16 = mybir.dt.bfloat16
AF = mybir.ActivationFunctionType
ALU = mybir.AluOpType
AX = mybir.AxisListType

NCORES = 8
LAYERS = 4
STRICT = True
SKIP = set()
D = 1024
SEQ = 2048
NS = 4
T = SEQ + NS
TILES = [(0, 512), (512, 512), (1024, 512), (1536, 512), (2048, NS)]
EPS = 1e-6
PAST = 8192
WINBUF = 2048
DILS = (1, 4, 16)

C_ID, C_ROT, C_ONE, C_PM = 0, 128, 256, 384
C_PK = 392
PK_MIXG, PK_FFNG, PK_QG, PK_KG, PK_BDW, PK_CNG, PK_CNB, PK_WDW, PK_WS00, PK_BS0 = 0, 32, 64, 66, 68, 84, 100, 116, 612, 620
NPK = 628
NCF = C_PK + NPK
B_ID, B_HM, B_ODM, B_ONE64, B_ZER, B_MASK = 0, 128, 256, 384, 448, 576
B_EH = B_MASK + 6 * 512
B_MULT = B_EH + 256
NCB = B_MULT + 512
ARN = 33600


def ss(start, n, step=1):
    return slice(start, start + step * (n - 1) + 1, step)


class Op:
    __slots__ = ("eng", "fn", "deps", "marked", "val", "isdma", "sem", "prev")

    def __init__(self, eng, fn):
        self.eng = eng
        self.fn = fn
        self.deps = []
        self.marked = False
        self.val = 0
        self.isdma = False
        self.sem = None
        self.prev = 0


class Prog:
    ENGS = ("pe", "act", "dve", "pool", "sp")

    def __init__(self, nc, esems, dsems):
        self.nc = nc
        self.esem = esems
        self.dsems = dsems
        self.dcnt = {q: [0] * len(v) for q, v in dsems.items()}
        self.drr = {q: 0 for q in dsems}
        self.dlast = {q: [None] * len(v) for q, v in dsems.items()}
        self.ops = {e: [] for e in self.ENGS}
        self.last_w = {}
        self.readers = {}
        self.bar = []
        self.bar_seen = {e: True for e in self.ENGS}
        self.nops = 0

    def _deps(self, op, reads, writes):
        deps = []
        for r in reads:
            w = self.last_w.get(r)
            if w is not None:
                deps.append(w)
        for w in writes:
            lw = self.last_w.get(w)
            if lw is not None and (STRICT or lw.isdma or op.isdma or lw.eng != op.eng):
                deps.append(lw)
            for rd in self.readers.get(w, ()):
                if STRICT or rd.isdma or op.isdma or rd.eng != op.eng:
                    deps.append(rd)
        if not self.bar_seen[op.eng]:
            deps.extend(self.bar)
            self.bar_seen[op.eng] = True
        seen = set()
        for d in deps:
            if d is op or id(d) in seen:
                continue
            seen.add(id(d))
            if (not d.isdma) and d.eng == "pe" and op.eng == "pe" and not op.isdma:
                continue
            op.deps.append(d)
            d.marked = True
        for r in reads:
            self.readers.setdefault(r, []).append(op)
        for w in writes:
            self.last_w[w] = op
            self.readers[w] = []

    def op(self, eng, fn, reads=(), writes=()):
        o = Op(eng, fn)
        self._deps(o, reads, writes)
        self.ops[eng].append(o)
        self.nops += 1
        return o

    def dma(self, q, out, in_, reads=(), writes=()):
        o = Op(q, lambda e, out=out, in_=in_: e.dma_start(out=out, in_=in_))
        o.isdma = True
        i = self.drr[q]
        self.drr[q] = (i + 1) % len(self.dsems[q])
        o.sem = self.dsems[q][i]
        o.prev = self.dcnt[q][i]
        self.dcnt[q][i] += 16
        o.val = self.dcnt[q][i]
        self.dlast[q][i] = o
        self._deps(o, reads, writes)
        self.ops[q].append(o)
        self.nops += 1
        return o

    def barrier(self):
        b = []
        for e in self.ENGS:
            for o in reversed(self.ops[e]):
                if not o.isdma:
                    b.append(o)
                    o.marked = True
                    break
        for q in self.dlast:
            for o in self.dlast[q]:
                if o is not None:
                    b.append(o)
        self.bar = b
        self.bar_seen = {e: False for e in self.ENGS}

    def emit(self, block):
        for e in self.ENGS:
            c = 0
            for o in self.ops[e]:
                if not o.isdma and o.marked:
                    c += 1
                    o.val = c
                    o.sem = self.esem[e]
        binder = {"pe": block.tensor, "act": block.scalar, "dve": block.vector, "pool": block.gpsimd, "sp": block.sync}
        for e in self.ENGS:
            ops = self.ops[e]
            dsems = self.dsems
            dcnt = self.dcnt

            def body(eng, ops=ops, e=e):
                waited = {}
                for o in ops:
                    for d in o.deps:
                        k = id(d.sem)
                        if waited.get(k, 0) >= d.val:
                            continue
                        eng.wait_ge(d.sem, d.val)
                        waited[k] = d.val
                    if o.isdma:
                        k = id(o.sem)
                        if o.prev > 0 and waited.get(k, 0) < o.prev:
                            eng.wait_ge(o.sem, o.prev)
                            waited[k] = o.prev
                        o.fn(eng).then_inc(o.sem, 16)
                    else:
                        ins = o.fn(eng)
                        if o.marked:
                            ins.then_inc(o.sem, 1)
                if e in ("sp", "pool"):
                    for q in dsems:
                        for i, s in enumerate(dsems[q]):
                            if dcnt[q][i] > 0:
                                eng.wait_ge(s, dcnt[q][i])

            binder[e](body)


def build_program():
    nc = bass.Bass("TRN2", target_bir_lowering=False)

    def din(name, shape):
        return nc.dram_tensor(name, list(shape), F32, kind="ExternalInput").ap()

    def dout(name, shape):
        return nc.dram_tensor(name, list(shape), F32, kind="ExternalOutput").ap()

    xp = din("xp", [SEQ, D])
    xs = din("xs", [NS, D])
    ck = din("ck", [2, NS, WINBUF, 512])
    cv = din("cv", [2, NS, WINBUF, 512])
    stc = din("stc", [2, NS, 30, D])
    w_up = din("w_up", [4, D, 4 * D])
    w_dn = din("w_dn", [4, 4 * D, D])
    w_in = din("w_in", [2, D, 2560])
    w_out = din("w_out", [2, D, D])
    w_ci = din("w_ci", [2, D, 2 * D])
    w_co = din("w_co", [2, D, D])
    cf_d = din("cf", [128, NCF])
    cb_d = din("cb", [128, NCB])
    cos_d = din("cosT", [128, T])
    sin_d = din("sinT", [128, T])
    wsT_d = din("wsT", [2, 128, 8 * 128])
    bT_d = din("bT", [2, 128, 4 * 128])
    vbg_d = din("vbg", [2, 128, 512])
    vbb_d = din("vbb", [2, 128, 512])

    y_p = dout("y_p", [SEQ, D])
    y_s = dout("y_s", [NS, D])
    ak_p = dout("ak_p", [2, SEQ, 512])
    av_p = dout("av_p", [2, SEQ, 512])
    ak_s = dout("ak_s", [2, NS, 512])
    av_s = dout("av_s", [2, NS, 512])
    bv_p = dout("bv_p", [2, 128, 512])
    bv_s = dout("bv_s", [2, NS, 512])
    cc_p = dout("cc_p", [2, 30, D])
    cc_s = dout("cc_s", [2, NS, 30, D])

    with ExitStack() as es:
        def sb(name, shape, dt):
            return es.enter_context(nc.sbuf_tensor(name, list(shape), dt))

        XT = sb("XT", [128, 8, T], F32)
        XN = sb("XN", [128, 8, T], BF16)
        RING = sb("RING", [128, 4, 4096], BF16)
        CFS = sb("CFS", [128, NCF], F32)
        CBS = sb("CBS", [128, NCB], BF16)
        AR = sb("AR", [128, ARN], BF16)
        PS = [es.enter_context(nc.psum_tensor(f"ps{i}", [128, 512], F32)) for i in range(8)]
        esems = {e: es.enter_context(nc.semaphore(f"se_{e}")) for e in Prog.ENGS}
        dsems = {"sp": [es.enter_context(nc.semaphore(f"sd_sp{i}")) for i in range(12)],
                 "pool": [es.enter_context(nc.semaphore(f"sd_pl{i}")) for i in range(8)]}
        block = es.enter_context(nc.Block())
        P = Prog(nc, esems, dsems)

        IDF = CFS[:, C_ID:C_ID + 128]
        ROT = CFS[:, C_ROT:C_ROT + 128]
        ONEF = CFS[:, C_ONE:C_ONE + 128]
        PMASK = CFS[:, C_PM:C_PM + 8]
        IDB = CBS[:, B_ID:B_ID + 128]
        HM = CBS[:, B_HM:B_HM + 128]
        ODM = CBS[:, B_ODM:B_ODM + 128]
        ONE64 = CBS[:, B_ONE64:B_ONE64 + 64]
        ZER = CBS[:, B_ZER:B_ZER + 128]

        def EH(h):
            return CBS[:, B_EH + 128 * h:B_EH + 128 * (h + 1)]

        def MASK(i):
            return CBS[:, B_MASK + 512 * i:B_MASK + 512 * (i + 1)]

        def pk(col, n=1):
            return CFS[:, C_PK + col:C_PK + col + n]

        arpos = [0]

        def ar_reset(pos=0):
            arpos[0] = pos

        def ar(n_el, dt=BF16):
            nb = n_el * (2 if dt == F32 else 1)
            nb = (nb + 7) // 8 * 8
            a = arpos[0]
            assert a + nb <= ARN, ("arena overflow", a, nb)
            arpos[0] = a + nb
            v = AR[:, a:a + nb]
            if dt == F32:
                v = v.bitcast(F32)
            return v[:, 0:n_el]

        bank_rot = {"list": list(range(8)), "i": 0}

        def bank():
            l = bank_rot["list"]
            i = l[bank_rot["i"] % len(l)]
            bank_rot["i"] += 1
            return PS[i], f"ps{i}"

        def set_banks(l):
            bank_rot["list"] = list(l)
            bank_rot["i"] = 0

        def mm(out, lhsT, rhs, start, stop, reads, writes):
            return P.op("pe", lambda e: e.matmul(out, lhsT=lhsT, rhs=rhs, start=start, stop=stop), reads, writes)

        def tr(out, in_, ident, reads, writes):
            return P.op("pe", lambda e: e.transpose(out=out, in_=in_, identity=ident), reads, writes)

        def act(out, in_, func, reads, writes, scale=None, bias=None):
            kw = {}
            if scale is not None:
                kw["scale"] = scale
            if bias is not None:
                kw["bias"] = bias
            return P.op("act", lambda e: e.activation(out=out, in_=in_, func=func, **kw), reads, writes)

        def tt(eng, out, in0, in1, op, reads, writes):
            return P.op(eng, lambda e: e.tensor_tensor(out=out, in0=in0, in1=in1, op=op), reads, writes)

        def stt(out, in0, scalar, in1, op0, op1, reads, writes):
            return P.op("dve", lambda e: e.scalar_tensor_tensor(out=out, in0=in0, scalar=scalar, in1=in1, op0=op0, op1=op1), reads, writes)

        def tsc(eng, out, in0, s1, s2, op0, op1, reads, writes):
            if op1 is None:
                return P.op(eng, lambda e: e.tensor_scalar(out=out, in0=in0, scalar1=s1, scalar2=None, op0=op0), reads, writes)
            return P.op(eng, lambda e: e.tensor_scalar(out=out, in0=in0, scalar1=s1, scalar2=s2, op0=op0, op1=op1), reads, writes)

        def cp(eng, out, in_, reads, writes):
            if eng == "act":
                return act(out, in_, AF.Copy, reads, writes)
            return P.op(eng, lambda e: e.tensor_copy(out=out, in_=in_), reads, writes)

        def memset(eng, ap, val, writes):
            return P.op(eng, lambda e: e.memset(ap, val), (), writes)

        def rstd_from(ps_ap, out_ap, reads, wkey):
            act(out_ap, ps_ap, AF.Ln, reads, [wkey], bias=EPS)
            act(out_ap, out_ap, AF.Exp, [wkey], [wkey], scale=-0.5)

        def wslot(i):
            return RING[:, i, :]

        P.dma("sp", CFS[:, :], cf_d[:, :], (), ["CFS"])
        for i0 in range(0, NCB, 1472):
            P.dma("pool", CBS[:, i0:i0 + 1472], cb_d[:, i0:i0 + 1472], (), ["CBS"])
        ar_reset()
        XSI = ar(D, F32)
        XIN = [ar(D, F32) for _ in range(15)]
        for b in range(16):
            xin = XIN[b % 15]
            P.dma("sp", xin[:, :], xp[b * 128:(b + 1) * 128, :], (), [f"XIN{b % 15}"])
            for half in range(2):
                ps, pk_ = bank()
                for cc in range(4):
                    c = half * 4 + cc
                    tr(ps[:, cc * 128:(cc + 1) * 128], xin[:, c * 128:(c + 1) * 128], IDF, [f"XIN{b % 15}", "CFS"], [pk_])
                cp("act" if half == 0 else "dve", XT[:, half * 4:half * 4 + 4, b * 128:(b + 1) * 128],
                   ps[:, :].rearrange("p (c n) -> p c n", c=4), [pk_], [f"XT{b // 4}_{half * 4 + q}" for q in range(4)])
        P.dma("sp", XSI[0:NS, :], xs[:, :], (), ["XSI"])
        ps, pk_ = bank()
        for c in range(8):
            tr(ps[:, c * NS:(c + 1) * NS], XSI[0:NS, c * 128:(c + 1) * 128], IDF[0:NS, 0:NS], ["XSI", "CFS"], [pk_])
        cp("act", XT[:, :, SEQ:T], ps[:, 0:8 * NS].rearrange("p (c n) -> p c n", c=8), [pk_], [f"XT4_{q}" for q in range(8)])

        TILES_F = [(i * 342, 342) for i in range(6)]

        def rmsnorm(gbase, SQ, RS, tiles=TILES, xk="XT", nk="XN"):
            for ti, (t0, n) in enumerate(tiles):
                act(SQ[:, :, 0:n], XT[:, :, t0:t0 + n], AF.Square, [f"{xk}{ti}_{c}" for c in range(8)], ["SQ"])
                ps, pk_ = bank()
                for c in range(8):
                    mm(ps[:, 0:n], ODM, SQ[:, c, 0:n], c == 0, c == 7, ["SQ", "CBS"], [pk_])
                if isinstance(RS, list):
                    RSt, krs = RS[ti % len(RS)], f"RS{ti % len(RS)}"
                else:
                    RSt, krs = RS, "RS"
                rstd_from(ps[:, 0:n], RSt[:, 0:n], [pk_], krs)
                for c in range(8):
                    stt(XN[:, c, t0:t0 + n], XT[:, c, t0:t0 + n], pk(gbase + c), RSt[:, 0:n], ALU.mult, ALU.mult,
                        [f"{xk}{ti}_{c}", krs, "CFS"], [f"{nk}{ti}"])

        def load_w(slot_ap, dram_ap, key):
            P.dma("pool", slot_ap, dram_ap, (), [key])

        def ffn_loadG(layer, G, first):
            wu = w_up[layer].rearrange("(k p) f -> p k f", p=128)
            wd = w_dn[layer].rearrange("(f p) j -> p f j", p=128)
            su, sd = (first + 2 * G) % 4, (first + 2 * G + 1) % 4
            load_w(wslot(su).rearrange("p (k f) -> p k f", k=8), wu[:, :, G * 512:(G + 1) * 512], f"W{su}")
            load_w(wslot(sd).rearrange("p (f j) -> p f j", f=4), wd[:, G * 4:(G + 1) * 4, :], f"W{sd}")

        def ffn(layer, first):
            P.barrier()
            ar_reset()
            SQ = ar(8 * 512).rearrange("p (c n) -> p c n", c=8)
            RS = [ar(512, F32) for _ in range(3)]
            R = [ar(512, F32), ar(512, F32)]
            H = [ar(4 * 512).rearrange("p (f n) -> p f n", f=4) for _ in range(2)]
            set_banks(range(8))
            rmsnorm(PK_FFNG + 8 * layer, SQ, RS, TILES_F, "XF", "XNF")
            hb = 0
            prevd = None
            for G in range(8):
                su, sd = (first + 2 * G) % 4, (first + 2 * G + 1) % 4
                WU = wslot(su).rearrange("p (k f) -> p k f", k=8)
                WD = wslot(sd).rearrange("p (f j) -> p f j", f=4)

                def up_part(ti, t0, n, Hc, hk, WU=WU, su=su):
                    for f in range(4):
                        ps, pk_ = bank()
                        for k in range(8):
                            mm(ps[:, 0:n], WU[:, k, f * 128:(f + 1) * 128], XN[:, k, t0:t0 + n], k == 0, k == 7,
                               [f"W{su}", f"XNF{ti}"], [pk_])
                        r = R[f % 2]
                        act(r[:, 0:n], ps[:, 0:n], AF.Relu, [pk_], [f"R{f % 2}"])
                        act(Hc[:, f, 0:n], r[:, 0:n], AF.Square, [f"R{f % 2}"], [f"{hk}_{f}"])

                def down_part(ti, t0, n, Hc, hk, WD=WD, sd=sd):
                    for j in range(8):
                        ps, pk_ = bank()
                        for f in range(4):
                            mm(ps[:, 0:n], WD[:, f, j * 128:(j + 1) * 128], Hc[:, f, 0:n], f == 0, f == 3,
                               [f"W{sd}", f"{hk}_{f}"], [pk_])
                        tt("dve", XT[:, j, t0:t0 + n], XT[:, j, t0:t0 + n], ps[:, 0:n], ALU.add, [pk_, f"XF{ti}_{j}"], [f"XF{ti}_{j}"])

                for ti, (t0, n) in enumerate(TILES_F):
                    Hc, hk = H[hb], f"H{hb}"
                    hb ^= 1
                    up_part(ti, t0, n, Hc, hk)
                    if prevd is not None:
                        prevd[0](*prevd[1])
                    prevd = (down_part, (ti, t0, n, Hc, hk))
                    if ti == 0 and G + 1 < 8:
                        ffn_loadG(layer, G + 1, first)
            prevd[0](*prevd[1])

        def ab_layer(layer):
            jl = layer // 2
            P.barrier()
            ar_reset()
            COS = ar(T, F32)
            SIN = ar(T, F32)
            ATT = ar(4 * T).rearrange("p (c n) -> p c n", c=4)
            RS = ar(512, F32)
            base = arpos[0]
            SQ = RING[:, 3, :].rearrange("p (c n) -> p c n", c=8)
            set_banks(range(8))
            P.dma("sp", COS[:, :], cos_d[:, :], (), ["COS"])
            P.dma("sp", SIN[:, :], sin_d[:, :], (), ["SIN"])
            win = w_in[jl].rearrange("(k p) f -> p k f", p=128)
            wo = w_out[jl].rearrange("(k p) j -> p k j", p=128)
            WU = wslot(0).rearrange("p (k f) -> p k f", k=8)
            WVB = wslot(1).rearrange("p (k f) -> p k f", k=8)
            WOG = wslot(2).rearrange("p (k j) -> p k j", k=4)
            load_w(WU, win[:, :, 1536:2048], "W0")
            load_w(WVB, win[:, :, 2048:2560], "W1")
            load_w(WOG, wo[:, 4:8, :], "W2")
            rmsnorm(PK_MIXG + 8 * layer, SQ, RS)
            ar_reset(base)
            UT = ar(4 * 512).rearrange("p (c n) -> p c n", c=4)
            VBT = ar(4 * 512).rearrange("p (b f) -> p b f", b=4)
            VGL = [ar(512, F32), ar(512, F32)]
            TMPG = ar(512, F32).rearrange("p (c i) -> p c i", c=4)
            WS = ar(8 * 128).rearrange("p (g i) -> p g i", g=8)
            WSF = ar(8 * 128, F32)
            BT = ar(4 * 128, F32).rearrange("p (c i) -> p c i", c=4)
            VBGt = ar(512, F32)
            VBBt = ar(512, F32)
            ST6L = [ar(8, F32), ar(8, F32)]
            MVL = [ar(4, F32), ar(4, F32)]
            vgc = [0]
            VST4 = ar(4 * NS, F32).rearrange("p (c n) -> p c n", c=4)
            P.dma("sp", WSF[:, :], wsT_d[jl], (), ["WSF"])
            P.dma("sp", BT[:, :, :], bT_d[jl].rearrange("p (c i) -> p c i", c=4), (), ["BT"])
            P.dma("sp", VBGt[:, :], vbg_d[jl], (), ["VBG"])
            P.dma("sp", VBBt[:, :], vbb_d[jl], (), ["VBB"])
            for q4 in range(2):
                tt("dve", WS[:, q4 * 4:(q4 + 1) * 4, :].rearrange("p g i -> p (g i)"), WSF[:, q4 * 512:(q4 + 1) * 512],
                   CBS[:, B_MULT:B_MULT + 512], ALU.mult, ["WSF", "CBS"], ["WS"])
            def gate_vb(ti, t0, n):
                nblk = 4 if ti < 4 else 1
                for bl in range(nblk):
                    m = 128 if ti < 4 else NS
                    c0 = t0 + bl * 128
                    vi = vgc[0] % 2
                    vgc[0] += 1
                    VG, ST6, MV = VGL[vi], ST6L[vi], MVL[vi]
                    kvg, kst, kmv = f"VG{vi}", f"ST6{vi}", f"MV{vi}"
                    ps, pk_ = bank()
                    for k in range(8):
                        mm(ps[0:m, :], XN[:, k, c0:c0 + m], WVB[:, k, :], k == 0, k == 7, ["W1", f"XN{ti}"], [pk_])
                    act(VG[0:m, :], ps[0:m, :], AF.Gelu_apprx_tanh, [pk_], [kvg])
                    P.op("dve", lambda e, m=m, ST6=ST6, VG=VG: e.bn_stats(out=ST6[0:m, 0:6], in_=VG[0:m, :]), [kvg], [kst])
                    P.op("dve", lambda e, m=m, ST6=ST6, MV=MV: e.bn_aggr(out=MV[0:m, 0:2], in_=ST6[0:m, 0:6]), [kst], [kmv])
                    rstd_from(MV[0:m, 1:2], MV[0:m, 2:3], [kmv], kmv)
                    tsc("dve", VG[0:m, :], VG[0:m, :], MV[0:m, 0:1], MV[0:m, 2:3], ALU.subtract, ALU.mult, [kvg, kmv], [kvg])
                    tt("dve", VG[0:m, :], VG[0:m, :], VBGt[0:m, :], ALU.mult, [kvg, "VBG"], [kvg])
                    tt("dve", VG[0:m, :], VG[0:m, :], VBBt[0:m, :], ALU.add, [kvg, "VBB"], [kvg])
                    if ti < 4:
                        cp("act", VBT[:, bl, :], VG[:, :], [kvg], [f"VBT{bl}"])
                        if ti == 3 and bl == 3:
                            P.dma("sp", bv_p[jl], VG[:, :], [kvg], [])
                    else:
                        P.dma("sp", bv_s[jl], VG[0:NS, :], [kvg], [])
                        ps2, pk2 = bank()
                        for c in range(4):
                            tr(ps2[:, c * NS:(c + 1) * NS], VG[0:NS, c * 128:(c + 1) * 128], IDF[0:NS, 0:NS], [kvg, "CFS"], [pk2])
                        for c in range(4):
                            tsc("dve", VST4[:, c, :], ps2[:, c * NS:(c + 1) * NS], pk(PK_WS00 + 4 * jl + c), pk(PK_BS0 + 4 * jl + c),
                                ALU.mult, ALU.add, [pk2, "CFS"], ["VST4"])
            def gate_u(ti, t0, n):
                for c in range(4):
                    ps, pk_ = bank()
                    for k in range(8):
                        mm(ps[:, 0:n], WU[:, k, c * 128:(c + 1) * 128], XN[:, k, t0:t0 + n], k == 0, k == 7, ["W0", f"XN{ti}"], [pk_])
                    act(UT[:, c, 0:n], ps[:, 0:n], AF.Gelu_apprx_tanh, [pk_], ["UT"])
                if ti == 4:
                    tt("dve", UT[:, :, 0:NS], UT[:, :, 0:NS], VST4[:, :, :], ALU.mult, ["UT", "VST4"], ["UT"])
            def gate_sp(ti, t0, n):
                if ti < 4:
                    for bl in range(4):
                        ps, pk_ = bank()
                        for g in range(8):
                            hp = (g % 2) * 64
                            cc = g // 2
                            mm(ps[hp:hp + 64, cc * 128:(cc + 1) * 128], VBT[:, bl, g * 64:(g + 1) * 64], WS[:, g, :], True, True,
                               [f"VBT{bl}", "WS"], [pk_])
                        tt("dve", TMPG[:, :, :], ps[:, :].rearrange("p (c i) -> p c i", c=4), BT[:, :, :], ALU.add, [pk_, "BT"], ["TMPG"])
                        tt("dve", UT[:, :, bl * 128:(bl + 1) * 128], UT[:, :, bl * 128:(bl + 1) * 128], TMPG[:, :, :], ALU.mult,
                           ["UT", "TMPG"], ["UT"])
            def gate_op(ti, t0, n):
                for j in range(8):
                    ps, pk_ = bank()
                    for k in range(4):
                        mm(ps[:, 0:n], WOG[:, k, j * 128:(j + 1) * 128], UT[:, k, 0:n], k == 0, k == 3, ["W2", "UT"], [pk_])
                    tt("dve", XT[:, j, t0:t0 + n], XT[:, j, t0:t0 + n], ps[:, 0:n], ALU.add, [pk_, f"XT{ti}_{j}"], [f"XT{ti}_{j}"])

            gt = TILES if 'gate' not in SKIP else []
            for ti, (t0, n) in enumerate(gt):
                gate_vb(ti, t0, n)
                if ti > 0:
                    gate_op(ti - 1, *gt[ti - 1])
                gate_u(ti, t0, n)
                gate_sp(ti, t0, n)
            if gt:
                gate_op(len(gt) - 1, *gt[-1])
            def pair_w(c):
                sl = c % 2
                return (wslot(sl)[:, 0:1024].rearrange("p (k f) -> p k f", k=8),
                        wslot(sl)[:, 1024:2048].rearrange("p (k f) -> p k f", k=8),
                        wslot(sl)[:, 2048:3072].rearrange("p (k f) -> p k f", k=8), f"W{sl}")

            def load_pair_w(c):
                WQ_, WK_, WV_, wk_ = pair_w(c)
                load_w(WQ_, win[:, :, c * 128:(c + 1) * 128], wk_)
                load_w(WK_, win[:, :, 512 + c * 128:512 + (c + 1) * 128], wk_)
                load_w(WV_, win[:, :, 1024 + c * 128:1024 + (c + 1) * 128], wk_)

            if 'attn' not in SKIP:
                load_pair_w(0)
            P.barrier()
            ar_reset(base)
            QTZ = [ar(T), ar(T)]
            KT = ar(T)
            A2 = [ar(512, F32), ar(512, F32)]
            Bq = ar(512)
            C2 = [ar(512, F32), ar(512, F32)]
            Dd = ar(512, F32)
            RDEN = ar(512, F32)
            KST = RDEN
            VSTG = RDEN
            Kc = ar(4 * 128, F32).rearrange("p (s f) -> p s f", s=4)
            Vc = ar(4 * 128, F32).rearrange("p (s f) -> p s f", s=4)
            ROW = ar(384, F32)
            PROD = Dd[:, :].rearrange("p (s f) -> p s f", s=4)
            SS = ar(8, F32)
            PM = ar(8, F32)
            RD = ar(2, F32)
            DEN2 = ar(2, F32)
            OROW = ar(128, F32)
            QSF = ar(NS, F32)
            KSF = ar(NS, F32)
            PT = [RING[:, 3, 2048 + 512 * i:2048 + 512 * (i + 1)] for i in range(4)]
            ptc = [0]
            itc = [0]

            def vtm(d, b):
                if d < 2:
                    return RING[:, 2, d * 2048 + b * 128:d * 2048 + (b + 1) * 128]
                return RING[:, 3, b * 128:(b + 1) * 128]

            def vtm_grp(d, grp):
                if d < 2:
                    return RING[:, 2, d * 2048 + grp * 512:d * 2048 + (grp + 1) * 512]
                return RING[:, 3, grp * 512:(grp + 1) * 512]

            memset("dve", Kc[:, :, :], 0.0, ["Kc"])
            memset("dve", Vc[:, :, :], 0.0, ["Vc"])
            memset("dve", QTZ[0][64:128, :], 0.0, ["QT"])
            memset("dve", QTZ[1][0:64, :], 0.0, ["QT"])
            ACCS = [(PS[5], "ps5"), (PS[6], "ps6")]
            DEN, kd = PS[7], "ps7"
            ZRHS = CBS[:, B_MASK:B_MASK + 512]
            for c in range(4 if 'attn' not in SKIP else 0):
                set_banks(range(8))
                WQ, WK, WV, wkey = pair_w(c)
                if c + 1 < 4:
                    load_pair_w(c + 1)
                else:
                    load_w(wslot(0).rearrange("p (k j) -> p k j", k=4), wo[:, 0:4, :], "W0")
                def vperm_gen(c=c, WV=WV, wkey=wkey):
                    for d in range(3 if 'vperm' not in SKIP else 0):
                        dil = DILS[d]
                        for grp in range(4):
                            ps, pk_ = bank()
                            for bi in range(4):
                                b = grp * 4 + bi
                                if d == 0:
                                    start = 128 * b
                                elif d == 1:
                                    start = 512 * (b // 4) + (b % 4)
                                else:
                                    start = b
                                for k in range(8):
                                    mm(ps[:, bi * 128:(bi + 1) * 128], XN[:, k, ss(start, 128, dil)], WV[:, k, :], k == 0, k == 7,
                                       [wkey, "XN0", "XN1", "XN2", "XN3"], [pk_])
                            cp("act", vtm_grp(d, grp), ps[:, :], [pk_], ["VTM"])
                            if d == 0:
                                cp("act", VSTG[:, :], ps[:, :], [pk_], ["RDEN"])
                                P.dma("sp", av_p[jl, grp * 512:(grp + 1) * 512, c * 128:(c + 1) * 128].rearrange("(b p) f -> p b f", p=128),
                                      VSTG[:, :].rearrange("p (b f) -> p b f", b=4), ["RDEN"], [])
                            yield

                vg = vperm_gen()

                def vfill():
                    try:
                        next(vg)
                        return True
                    except StopIteration:
                        return False

                def emit_tr(A, kA, t0, c=c):
                    ps4, pk4 = bank()
                    for bl in range(4):
                        tr(ps4[:, bl * 128:(bl + 1) * 128], A[:, bl * 128:(bl + 1) * 128], IDF, [kA, "CFS"], [pk4])
                    cp("act", KST[:, :], ps4[:, :], [pk4], ["RDEN"])
                    P.dma("sp", ak_p[jl, t0:t0 + 512, c * 128:(c + 1) * 128].rearrange("(b p) f -> p b f", p=128),
                          KST[:, :].rearrange("p (b f) -> p b f", b=4), ["RDEN"], [])

                pending = None
                for ti, (t0, n) in enumerate(TILES if 'qk' not in SKIP else []):
                    for which in range(2):
                        ib = itc[0] % 2
                        itc[0] += 1
                        A, kA = A2[ib], f"A{ib}"
                        Cc, kC = C2[ib], f"C{ib}"
                        W_ = WQ if which == 0 else WK
                        gcol = pk((PK_QG if which == 0 else PK_KG) + jl)
                        ps, pk_ = bank()
                        for k in range(8):
                            mm(ps[:, 0:n], W_[:, k, :], XN[:, k, t0:t0 + n], k == 0, k == 7, [wkey, f"XN{ti}"], [pk_])
                        act(A[:, 0:n], ps[:, 0:n], AF.Identity, [pk_, "CFS"], [kA], scale=gcol)
                        act(Bq[:, 0:n], ps[:, 0:n], AF.Square, [pk_], ["B"])
                        vfill()
                        ps2, pk2 = bank()
                        mm(ps2[:, 0:n], HM, Bq[:, 0:n], True, True, ["B", "CBS"], [pk2])
                        ps3, pk3 = bank()
                        mm(ps3[:, 0:n], ROT, A[:, 0:n], True, True, [kA, "CFS"], [pk3])
                        rstd_from(ps2[:, 0:n], Cc[:, 0:n], [pk2], kC)
                        tt("dve", Dd[:, 0:n], ps3[:, 0:n], SIN[:, t0:t0 + n], ALU.mult, [pk3, "SIN"], ["D"])
                        tt("dve", A[:, 0:n], A[:, 0:n], COS[:, t0:t0 + n], ALU.mult, [kA, "COS"], [kA])
                        tt("dve", A[:, 0:n], A[:, 0:n], Dd[:, 0:n], ALU.add, [kA, "D"], [kA])
                        tt("dve", A[:, 0:n], A[:, 0:n], Cc[:, 0:n], ALU.mult, [kA, kC], [kA])
                        if which == 0:
                            cp("act", QTZ[0][0:64, t0:t0 + n], A[0:64, 0:n], [kA], ["QT"])
                            cp("act", QTZ[1][64:128, t0:t0 + n], A[64:128, 0:n], [kA], ["QT"])
                        else:
                            cp("act", KT[:, t0:t0 + n], A[:, 0:n], [kA], ["KT"])
                        if ti == 4:
                            cp("dve", (QSF if which == 0 else KSF)[:, 0:NS], A[:, 0:NS], [kA], ["QSF" if which == 0 else "KSF"])
                        if pending is not None:
                            emit_tr(*pending)
                            pending = None
                        if which == 1 and ti < 4:
                            pending = (A, kA, t0)
                if pending is not None:
                    emit_tr(*pending)
                while vfill():
                    pass

                def sample_chain(c=c, WV=WV, wkey=wkey):
                    for nq in range(NS if 'sattn' not in SKIP else 0):
                        ps, pk_ = bank()
                        tr(ps[0:1, 0:128], QSF[:, nq:nq + 1], IDF, ["QSF", "CFS"], [pk_])
                        tr(ps[0:1, 128:256], KSF[:, nq:nq + 1], IDF, ["KSF", "CFS"], [pk_])
                        for k in range(8):
                            mm(ps[0:1, 256:384], XN[:, k, SEQ + nq:SEQ + nq + 1], WV[:, k, :], k == 0, k == 7, [wkey, "XN4"], [pk_])
                        for d in range(3):
                            dil = DILS[d]
                            r0 = WINBUF - 128 * dil
                            P.dma("sp", Kc[:, d, :], ck[jl, nq, ss(r0, 128, dil), c * 128:(c + 1) * 128], (), ["Kc"])
                            P.dma("sp", Vc[:, d, :], cv[jl, nq, ss(r0, 128, dil), c * 128:(c + 1) * 128], (), ["Vc"])
                        yield
                        cp("act", ROW[0:1, 0:384], ps[0:1, 0:384], [pk_], ["ROW"])
                        yield
                        cp("dve", Kc[0:1, 3, :], ROW[0:1, 128:256], ["ROW"], ["Kc"])
                        cp("dve", Vc[0:1, 3, :], ROW[0:1, 256:384], ["ROW"], ["Vc"])
                        P.dma("sp", ak_s[jl, nq:nq + 1, c * 128:(c + 1) * 128], ROW[0:1, 128:256], ["ROW"], [])
                        P.dma("sp", av_s[jl, nq:nq + 1, c * 128:(c + 1) * 128], ROW[0:1, 256:384], ["ROW"], [])
                        ps2, pk2 = bank()
                        mm(ps2[:, 0:128], ONEF[0:1, :], ROW[0:1, 0:128], True, True, ["ROW", "CFS"], [pk2])
                        yield
                        for sl4 in range(4):
                            tt("dve", PROD[:, sl4, :], Kc[:, sl4, :], ps2[:, 0:128], ALU.mult, ["Kc", pk2], ["PROD"])
                        P.op("dve", lambda e: e.tensor_reduce(out=SS[:, 0:8], in_=PROD[:, :, :].rearrange("p s (h d) -> p (s h) d", h=2),
                                                              axis=AX.X, op=ALU.add), ["PROD"], ["SS"])
                        yield
                        act(PM[:, 0:8], SS[:, 0:8], AF.Exp, ["SS"], ["PM"], scale=0.125)
                        yield
                        tt("dve", PM[:, 0:8], PM[:, 0:8], PMASK, ALU.mult, ["PM", "CFS"], ["PM"])
                        yield
                        ps3, pk3 = bank()
                        for h in range(2):
                            for sl4 in range(4):
                                mm(ps3[0:1, h * 64:(h + 1) * 64], PM[:, sl4 * 2 + h:sl4 * 2 + h + 1], Vc[:, sl4, h * 64:(h + 1) * 64],
                                   sl4 == 0, sl4 == 3, ["PM", "Vc"], [pk3])
                        mm(ps3[0:1, 128:136], ONEF[:, 0:1], PM[:, 0:8], True, True, ["PM", "CFS"], [pk3])
                        yield
                        P.op("dve", lambda e, ps3=ps3: e.tensor_reduce(out=DEN2[0:1, 0:2], in_=ps3[0:1, 128:136].rearrange("p (s h) -> p h s", h=2),
                                                                        axis=AX.X, op=ALU.add), [pk3], ["DEN2"])
                        P.op("dve", lambda e: e.reciprocal(out=RD[0:1, 0:2], in_=DEN2[0:1, 0:2]), ["DEN2"], ["RD"])
                        for h in range(2):
                            tsc("dve", OROW[0:1, h * 64:(h + 1) * 64], ps3[0:1, h * 64:(h + 1) * 64], RD[0:1, h:h + 1], None, ALU.mult, None,
                                [pk3, "RD"], ["OROW"])
                        yield
                        ps4, pk4 = bank()
                        tr(ps4[:, 0:1], OROW[0:1, 0:128], IDF[0:1, 0:1], ["OROW", "CFS"], [pk4])
                        yield
                        cp("act", ATT[:, c, SEQ + nq:SEQ + nq + 1], ps4[:, 0:1], [pk4], ["ATT"])
                        yield

                gen = sample_chain()
                set_banks(range(5))

                def advance(k=1):
                    for _ in range(k):
                        try:
                            next(gen)
                        except StopIteration:
                            return False
                    return True

                items = []
                for s in range(4 if 'pattn' not in SKIP else 0):
                    for h in range(2):
                        bl_ = []
                        b1 = [(4 * s + i, 512 * s + 128 * i, 1, 128 * i, 1) for i in range(4)]
                        bl_.append((0, b1, 128, 0))
                        b2 = [(4 * s + i - 1, 512 * s + 128 * i, 1, 128 * i, 1) for i in range(4) if 4 * s + i >= 1]
                        bl_.append((0, b2, 128, 1))
                        b3 = [(4 * s + r, 512 * s + r, 4, r, 4) for r in range(4)]
                        bl_.append((1, b3, 128, 0))
                        if s >= 1:
                            b4 = [(4 * (s - 1) + r, 512 * s + r, 4, r, 4) for r in range(4)]
                            bl_.append((1, b4, 128, 1))
                        b5 = [(r, r + 16 * 32 * s, 16, r, 16) for r in range(16)]
                        bl_.append((2, b5, 32, 2 + s))
                        for bi_, bt in enumerate(bl_):
                            items.append((s, h, bt, h == 0 and bi_ == 0, h == 1 and bi_ == len(bl_) - 1))

                def issue_qk(it):
                    s, h, (d, tl, w, mi), _, _ = it
                    dil = DILS[d]
                    ps, pk_ = bank()
                    off = 512 - w * len(tl)
                    for i, (kb, q0, qst, c0, cst) in enumerate(tl):
                        if d == 0:
                            k0 = 128 * kb
                        elif d == 1:
                            k0 = 512 * (kb // 4) + (kb % 4)
                        else:
                            k0 = kb
                        mm(ps[:, off + i * w:off + (i + 1) * w], KT[:, ss(k0, 128, dil)], QTZ[h][:, ss(q0, w, qst)],
                           i == 0, False, ["KT", "QT"], [pk_])
                    mm(ps[:, off:512], IDB, MASK(mi)[:, off:512], False, True, ["CBS", pk_], [pk_])
                    pt = PT[ptc[0] % 4]
                    ptk = f"PT{ptc[0] % 4}"
                    ptc[0] += 1
                    act(pt[:, off:512], ps[:, off:512], AF.Exp, [pk_], [ptk], scale=0.125)
                    return pt, ptk, off

                def issue_pv(it, info):
                    s, h, (d, tl, w, mi), first, last = it
                    pt, ptk, off = info
                    ACC, ka = ACCS[h]
                    if first:
                        mm(ACCS[0][0][:, :], ZER, ZRHS, True, False, ["CBS"], [ACCS[0][1]])
                        mm(ACCS[1][0][:, :], ZER, ZRHS, True, False, ["CBS"], [ACCS[1][1]])
                        mm(DEN[:, :], ZER, ZRHS, True, False, ["CBS"], [kd])
                    for i, (kb, q0, qst, c0, cst) in enumerate(tl):
                        mm(ACC[:, ss(c0, w, cst)], vtm(d, kb), pt[:, off + i * w:off + (i + 1) * w],
                           False, False, [ptk, "VTM", ka], [ka])
                    nt = len(tl)
                    if d == 0:
                        mm(DEN[:, off:512], EH(h), pt[:, off:512], False, False, [ptk, "CBS", kd], [kd])
                    else:
                        for i, (kb, q0, qst, c0, cst) in enumerate(tl):
                            mm(DEN[:, ss(c0, w, cst)], EH(h), pt[:, off + i * w:off + (i + 1) * w],
                               False, False, [ptk, "CBS", kd], [kd])
                    if last:
                        mm(ACCS[0][0][:, :], ZER, ZRHS, False, True, ["CBS", ACCS[0][1]], [ACCS[0][1]])
                        mm(ACCS[1][0][:, :], ZER, ZRHS, False, True, ["CBS", ACCS[1][1]], [ACCS[1][1]])
                        mm(DEN[:, :], ZER, ZRHS, False, True, ["CBS", kd], [kd])
                        act(RDEN[:, :], DEN[:, :], AF.Ln, [kd], ["RDEN"])
                        act(RDEN[:, :], RDEN[:, :], AF.Exp, ["RDEN"], ["RDEN"], scale=-1.0)
                        tt("dve", ATT[0:64, c, 512 * s:512 * (s + 1)], ACCS[0][0][0:64, :], RDEN[0:64, :], ALU.mult,
                           [ACCS[0][1], "RDEN"], ["ATT"])
                        tt("dve", ATT[64:128, c, 512 * s:512 * (s + 1)], ACCS[1][0][64:128, :], RDEN[64:128, :], ALU.mult,
                           [ACCS[1][1], "RDEN"], ["ATT"])

                prev = None
                for idx in range(len(items) + 1):
                    cur = items[idx] if idx < len(items) else None
                    info = issue_qk(cur) if cur is not None else None
                    if prev is not None:
                        issue_pv(prev[0], prev[1])
                    prev = (cur, info) if cur is not None else None
                    advance(1)
                while advance(1):
                    pass
            P.barrier()
            if 'ffn' not in SKIP:
                ffn_loadG(layer, 0, 2)
            set_banks(range(8))
            WOA = wslot(0).rearrange("p (k j) -> p k j", k=4)
            if 'attn' in SKIP:
                load_w(WOA, wo[:, 0:4, :], "W0")
            for ti, (t0, n) in enumerate(TILES if 'oproj' not in SKIP else []):
                for j in range(8):
                    ps, pk_ = bank()
                    for k in range(4):
                        mm(ps[:, 0:n], WOA[:, k, j * 128:(j + 1) * 128], ATT[:, k, t0:t0 + n], k == 0, k == 3, ["W0", "ATT"], [pk_])
                    tt("dve", XT[:, j, t0:t0 + n], XT[:, j, t0:t0 + n], ps[:, 0:n], ALU.add, [pk_, f"XT{ti}_{j}"], [f"XT{ti}_{j}"])

        def c_layer(layer):
            jl = layer // 2
            P.barrier()
            ar_reset()
            GW = 30 + SEQ + 8
            GLU = ar(8 * GW).rearrange("p (c n) -> p c n", c=8)
            GTAIL = ar(8 * 30, F32).rearrange("p (c n) -> p c n", c=8)
            GSF = ar(8 * NS, F32).rearrange("p (c n) -> p c n", c=8)
            XC = ar(NS * 8 * 31, F32).rearrange("p (n c j) -> p n c j", n=NS, c=8)
            base = arpos[0]
            SQ = ar(8 * 512).rearrange("p (c n) -> p c n", c=8)
            RS = ar(512, F32)
            SG = ar(512, F32)
            STG4 = [ar(D, F32) for _ in range(NS)]
            set_banks(range(8))
            wci = w_ci[jl].rearrange("(k p) f -> p k f", p=128)
            wco = w_co[jl].rearrange("(k p) j -> p k j", p=128)

            def load_ci(c):
                sl = c % 2
                WA_ = wslot(sl)[:, 0:1024].rearrange("p (k f) -> p k f", k=8)
                WG_ = wslot(sl)[:, 1024:2048].rearrange("p (k f) -> p k f", k=8)
                load_w(WA_, wci[:, :, c * 128:(c + 1) * 128], f"W{sl}")
                load_w(WG_, wci[:, :, D + c * 128:D + (c + 1) * 128], f"W{sl}")
                return WA_, WG_, f"W{sl}"

            nxt = load_ci(0)
            for nq in range(NS):
                P.dma("sp", STG4[nq][0:30, :], stc[jl, nq], (), [f"STG4{nq}"])
                P.dma("sp", cc_s[jl, nq, 0:29, :], STG4[nq][1:30, :], [f"STG4{nq}"], [])
            rmsnorm(PK_MIXG + 8 * layer, SQ, RS)
            memset("dve", GLU[:, :, 0:30], 0.0, ["GLU"])
            for c in range(8):
                WA_, WG_, wkey = nxt
                if c + 1 < 8:
                    nxt = load_ci(c + 1)
                for ti, (t0, n) in enumerate(TILES):
                    psA, ka = bank()
                    for k in range(8):
                        mm(psA[:, 0:n], WA_[:, k, :], XN[:, k, t0:t0 + n], k == 0, k == 7, [wkey, f"XN{ti}"], [ka])
                    psG, kg = bank()
                    for k in range(8):
                        mm(psG[:, 0:n], WG_[:, k, :], XN[:, k, t0:t0 + n], k == 0, k == 7, [wkey, f"XN{ti}"], [kg])
                    act(SG[:, 0:n], psG[:, 0:n], AF.Sigmoid, [kg], ["SG"])
                    if ti < 4:
                        tt("dve", GLU[:, c, 30 + t0:30 + t0 + n], psA[:, 0:n], SG[:, 0:n], ALU.mult, [ka, "SG"], ["GLU"])
                        if ti == 3:
                            tt("dve", GTAIL[:, c, :], psA[:, 482:512], SG[:, 482:512], ALU.mult, [ka, "SG"], ["GTAIL"])
                    else:
                        tt("dve", GSF[:, c, :], psA[:, 0:NS], SG[:, 0:NS], ALU.mult, [ka, "SG"], ["GSF"])
                if c == 1:
                    for nq in range(NS):
                        ps, pk_ = bank()
                        for c2 in range(8):
                            tr(ps[:, c2 * 30:(c2 + 1) * 30], STG4[nq][0:30, c2 * 128:(c2 + 1) * 128], IDF[0:30, 0:30],
                               [f"STG4{nq}", "CFS"], [pk_])
                        cp("act", XC[:, nq, :, 0:30], ps[:, 0:240].rearrange("p (c j) -> p c j", c=8), [pk_], ["XC"])
            P.barrier()
            if 'ffn' not in SKIP:
                ffn_loadG(layer, 0, 0)
            xnf = XN[:, :, :].rearrange("p c n -> p (c n)")
            Y = xnf[:, 0:8192].bitcast(F32).rearrange("p (c n) -> p c n", c=8)
            STt = xnf[:, 8192:12288].rearrange("p (c n) -> p c n", c=8)
            DG0 = xnf[:, 12288:12288 + 31 * 128].rearrange("p (j f) -> p j f", j=31)
            ar_reset(base)
            DGS = [DG0, ar(31 * 128).rearrange("p (j f) -> p j f", j=31)]
            MEAN = ar(512, F32)
            VAR = ar(512, F32)
            RS2 = ar(512, F32)
            DtL = [ar(512, F32), ar(512, F32)]
            YBL = [ar(512), ar(512)]
            YSL = [ar(512), ar(512)]
            STG = ar(D, F32)
            PRC = ar(8 * 31, F32).rearrange("p (c j) -> p c j", c=8)
            WCO0 = wslot(2).rearrange("p (k j) -> p k j", k=4)
            WCO1 = wslot(3).rearrange("p (k j) -> p k j", k=4)
            load_w(WCO0, wco[:, 0:4, :], "W2")
            load_w(WCO1, wco[:, 4:8, :], "W3")
            WDW = pk(PK_WDW + jl * 248, 248).rearrange("p (c j) -> p c j", c=8)
            psM, km = PS[6], "ps6"
            psE, ke = PS[7], "ps7"
            set_banks(range(6))

            def tail_outputs():
                for half in range(2):
                    ps, pk_ = bank()
                    for cc in range(4):
                        c = half * 4 + cc
                        tr(ps[0:30, cc * 128:(cc + 1) * 128], GTAIL[:, c, :], IDF, ["GTAIL", "CFS"], [pk_])
                    cp("act", STG[0:30, half * 512:(half + 1) * 512], ps[0:30, :], [pk_], ["STG"])
                P.dma("sp", cc_p[jl], STG[0:30, :], ["STG"], [])
                for nq in range(NS):
                    cp("dve", XC[:, nq, :, 30:31], GSF[:, :, nq:nq + 1], ["GSF"], ["XC"])
                for half in range(2):
                    ps2, pk2 = bank()
                    for cc in range(4):
                        c = half * 4 + cc
                        tr(ps2[0:NS, cc * 128:(cc + 1) * 128], GSF[:, c, 0:NS], IDF, ["GSF", "CFS"], [pk2])
                    cp("act", STG[0:NS, half * 512:(half + 1) * 512], ps2[0:NS, :], [pk2], ["STG"])
                P.dma("sp", cc_s[jl, :, 29, :], STG[0:NS, :], ["STG"], [])

            def build_dg(c):
                DG = DGS[c % 2]
                dgk = f"DG{c % 2}"
                for j in range(31):
                    if j % 4 == 0:
                        act(DG[:, j, :], IDB, AF.Identity, ["CBS", "CFS"], [f"{dgk}_{j}"], scale=pk(PK_WDW + jl * 248 + c * 31 + j))
                    else:
                        tsc("dve", DG[:, j, :], IDB, pk(PK_WDW + jl * 248 + c * 31 + j), None, ALU.mult, None, ["CBS", "CFS"], [f"{dgk}_{j}"])

            def conv_stats(ti, t0, n):
                for c in range(8):
                    YB, YS = YBL[c % 2], YSL[c % 2]
                    kyb, kys = f"YB{c % 2}", f"YS{c % 2}"
                    bias = pk(PK_BDW + 8 * jl + c)
                    if ti < 4:
                        DG = DGS[c % 2]
                        dgk = f"DG{c % 2}"
                        ps, pk_ = bank()
                        for j in range(31):
                            mm(ps[:, 0:n], DG[:, j, :], GLU[:, c, t0 + j:t0 + j + n], j == 0, j == 30, [f"{dgk}_{j}", "GLU"], [pk_])
                        if not (ti == 3 and c >= 6):
                            build_dg((c + 2) % 8)
                        src = ps[:, 0:n]
                        act(Y[:, c, 0:n], src, AF.Identity, [pk_, "CFS"], [f"Y{c}"], bias=bias)
                        act(YB[:, 0:n], src, AF.Identity, [pk_, "CFS"], [kyb], bias=bias)
                        act(YS[:, 0:n], src, AF.Square, [pk_, "CFS"], [kys], bias=bias)
                    else:
                        for nq in range(NS):
                            tt("dve", PRC[:, c, :], XC[:, nq, c, :], WDW[:, c, :], ALU.mult, ["XC", "CFS"], ["PRC"])
                            P.op("dve", lambda e, c=c, nq=nq: e.tensor_reduce(out=Y[:, c, nq:nq + 1], in_=PRC[:, c, :], axis=AX.X, op=ALU.add),
                                 ["PRC"], [f"Y{c}"])
                        act(Y[:, c, 0:n], Y[:, c, 0:n], AF.Identity, [f"Y{c}", "CFS"], [f"Y{c}"], bias=bias)
                        act(YB[:, 0:n], Y[:, c, 0:n], AF.Copy, [f"Y{c}"], [kyb])
                        act(YS[:, 0:n], Y[:, c, 0:n], AF.Square, [f"Y{c}"], [kys])
                    mm(psM[:, 0:n], ODM, YB[:, 0:n], c == 0, c == 7, [kyb, "CBS"], [km])
                    mm(psE[:, 0:n], ODM, YS[:, 0:n], c == 0, c == 7, [kys, "CBS"], [ke])

            def ln_apply(ti, t0, n):
                cp("act", MEAN[:, 0:n], psM[:, 0:n], [km], ["MEAN"])
                tt("dve", VAR[:, 0:n], MEAN[:, 0:n], MEAN[:, 0:n], ALU.mult, ["MEAN"], ["VAR"])
                tt("dve", VAR[:, 0:n], psE[:, 0:n], VAR[:, 0:n], ALU.subtract, [ke, "VAR"], ["VAR"])
                tsc("dve", VAR[:, 0:n], VAR[:, 0:n], 0.0, None, ALU.max, None, ["VAR"], ["VAR"])
                rstd_from(VAR[:, 0:n], RS2[:, 0:n], ["VAR"], "RS2")
                for c in range(8):
                    Dt, kdt = DtL[c % 2], f"Dt{c % 2}"
                    tt("dve", Dt[:, 0:n], Y[:, c, 0:n], MEAN[:, 0:n], ALU.subtract, [f"Y{c}", "MEAN"], [kdt])
                    tt("dve", Dt[:, 0:n], Dt[:, 0:n], RS2[:, 0:n], ALU.mult, [kdt, "RS2"], [kdt])
                    act(STt[:, c, 0:n], Dt[:, 0:n], AF.Silu, [kdt, "CFS"], [f"STt{c}"], scale=pk(PK_CNG + 8 * jl + c), bias=pk(PK_CNB + 8 * jl + c))

            def outproj(ti, t0, n):
                for j in range(8):
                    ps, pk_ = bank()
                    for k in range(8):
                        Wc = WCO0 if k < 4 else WCO1
                        mm(ps[:, 0:n], Wc[:, k % 4, j * 128:(j + 1) * 128], STt[:, k, 0:n], k == 0, k == 7, ["W2", "W3", f"STt{k}"], [pk_])
                    tt("dve", XT[:, j, t0:t0 + n], XT[:, j, t0:t0 + n], ps[:, 0:n], ALU.add, [pk_, f"XT{ti}_{j}"], [f"XT{ti}_{j}"])

            build_dg(0)
            build_dg(1)
            prev = None
            for ti, (t0, n) in enumerate(TILES):
                if ti == 1:
                    tail_outputs()
                if ti == 4 and prev is not None:
                    outproj(*prev)
                    prev = None
                conv_stats(ti, t0, n)
                if prev is not None:
                    outproj(*prev)
                ln_apply(ti, t0, n)
                prev = (ti, t0, n)
            outproj(*prev)

        for layer in range(LAYERS):
            if layer % 2 == 0:
                ab_layer(layer)
            else:
                c_layer(layer)
            if 'ffn' not in SKIP:
                ffn(layer, 2 if layer % 2 == 0 else 0)

        P.barrier()
        ar_reset()
        YO = [ar(D, F32), ar(D, F32)]
        set_banks(range(8))
        for b in range(16):
            yo = YO[b % 2]
            for half in range(2):
                ps, pk_ = bank()
                for cc in range(4):
                    c = half * 4 + cc
                    tr(ps[:, cc * 128:(cc + 1) * 128], XT[:, c, b * 128:(b + 1) * 128], IDF, [f"XT{b // 4}_{c}", "CFS"], [pk_])
                cp("act" if half == 0 else "dve", yo[:, half * 512:(half + 1) * 512], ps[:, :], [pk_], [f"YO{b % 2}"])
            P.dma("sp", y_p[b * 128:(b + 1) * 128, :], yo[:, :], [f"YO{b % 2}"], [])
        YS_ = ar(D, F32)
        for half in range(2):
            ps, pk_ = bank()
            for cc in range(4):
                c = half * 4 + cc
                tr(ps[0:NS, cc * 128:(cc + 1) * 128], XT[:, c, SEQ:T], IDF, [f"XT4_{c}", "CFS"], [pk_])
            cp("act", YS_[0:NS, half * 512:(half + 1) * 512], ps[0:NS, :], [pk_], ["YS_"])
        P.dma("sp", y_s[:, :], YS_[0:NS, :], ["YS_"], [])

        P.emit(block)
    return nc


_CACHE = {}


def _consts():
    cf = np.zeros((128, NCF), np.float32)
    cf[:, C_ID:C_ID + 128] = np.eye(128, dtype=np.float32)
    rot = np.zeros((128, 128), np.float32)
    for i in range(128):
        if (i % 64) < 32:
            rot[i + 32, i] = -1.0
        else:
            rot[i - 32, i] = 1.0
    cf[:, C_ROT:C_ROT + 128] = rot
    cf[:, C_ONE:C_ONE + 128] = 1.0
    pm = np.zeros((128, 8), np.float32)
    pm[:, 0:6] = 1.0
    pm[0, 6:8] = 3.0
    cf[:, C_PM:C_PM + 8] = pm
    cb = np.zeros((128, NCB), np.float32)
    cb[:, B_ID:B_ID + 128] = np.eye(128)
    hm = np.zeros((128, 128), np.float32)
    hm[0:64, 0:64] = 1.0 / 64
    hm[64:128, 64:128] = 1.0 / 64
    cb[:, B_HM:B_HM + 128] = hm
    cb[:, B_ODM:B_ODM + 128] = 1.0 / 1024
    cb[:, B_ONE64:B_ONE64 + 64] = 1.0
    j = np.arange(128)[:, None]
    i = np.arange(128)[None, :]
    cur = (j <= i).astype(np.float32)
    prev = (j >= i).astype(np.float32)
    NEG = -30000.0
    cb[:, B_MASK:B_MASK + 512] = np.tile((1.0 - cur) * NEG, (1, 4))
    cb[:, B_MASK + 512:B_MASK + 1024] = np.tile((1.0 - prev) * NEG, (1, 4))
    for s in range(4):
        cb[:, B_MASK + 512 * (2 + s):B_MASK + 512 * (3 + s)] = np.tile((1.0 - cur[:, 32 * s:32 * (s + 1)]) * NEG, (1, 16))
    cb[:, B_MULT:B_MULT + 512] = np.tile(cur, (1, 4))
    cb[:, B_EH:B_EH + 64] = 1.0
    cb[:, B_EH + 128 + 64:B_EH + 256] = 1.0
    inv = (10000.0 ** (-np.arange(0, 64, 2, dtype=np.float32) / 64)).astype(np.float32)
    pos = np.concatenate([np.arange(SEQ, dtype=np.float32), np.full((NS,), float(PAST), np.float32)])
    ang = (pos[None, :] * inv[:, None]).astype(np.float32)
    cosT = np.tile(np.cos(ang).astype(np.float32), (4, 1))
    sinT = np.tile(np.sin(ang).astype(np.float32), (4, 1))
    return cf, cb, np.ascontiguousarray(cosT), np.ascontiguousarray(sinT)


def kernel(x_prompt, x_sample, cache_a_k, cache_a_v, state_c_conv, norm_mix_g, norm_ffn_g, w_ffn_up, w_ffn_down,
           w_in_ab, q_norm_g, k_norm_g, vb_norm_g, vb_norm_b, w_spatial, b_spatial, w_out_ab,
           w_c_in, w_c_dw, b_c_dw, c_norm_g, c_norm_b, w_c_out):
    f = lambda a: np.ascontiguousarray(np.asarray(a, dtype=np.float32))
    x_prompt, x_sample = f(x_prompt), f(x_sample)
    cache_a_k, cache_a_v, state_c_conv = f(cache_a_k), f(cache_a_v), f(state_c_conv)
    if "nc" not in _CACHE:
        _CACHE["nc"] = build_program()
    nc = _CACHE["nc"]
    cf, cb, cosT, sinT = _consts()
    def fm(v):
        v = f(v)
        n = v.shape[0]
        return v.reshape(n, 8, 128).transpose(2, 0, 1).reshape(128, n * 8)
    pkc = np.zeros((128, NPK), np.float32)
    pkc[:, PK_MIXG:PK_MIXG + 32] = fm(norm_mix_g)
    pkc[:, PK_FFNG:PK_FFNG + 32] = fm(norm_ffn_g)
    pkc[:, PK_QG:PK_QG + 2] = np.tile(f(q_norm_g).T, (2, 1))
    pkc[:, PK_KG:PK_KG + 2] = np.tile(f(k_norm_g).T, (2, 1))
    pkc[:, PK_BDW:PK_BDW + 16] = fm(b_c_dw)
    pkc[:, PK_CNG:PK_CNG + 16] = fm(c_norm_g)
    pkc[:, PK_CNB:PK_CNB + 16] = fm(c_norm_b)
    wdw = f(w_c_dw)
    pkc[:, PK_WDW:PK_WDW + 496] = wdw.reshape(2, 31, 8, 128).transpose(3, 0, 2, 1).reshape(128, 496)
    wsp = f(w_spatial)
    bsp = f(b_spatial)
    gidx = (np.arange(128) // 64)[:, None] + 2 * np.arange(4)[None, :]
    for jl in range(2):
        pkc[:, PK_WS00 + 4 * jl:PK_WS00 + 4 * jl + 4] = wsp[jl, :, 0, 0][gidx]
        pkc[:, PK_BS0 + 4 * jl:PK_BS0 + 4 * jl + 4] = bsp[jl, :, 0][gidx]
    cf[:, C_PK:C_PK + NPK] = pkc
    wsT = np.ascontiguousarray(wsp.transpose(0, 3, 1, 2).reshape(2, 128, 8 * 128))
    bT = np.ascontiguousarray(bsp[:, gidx, :].reshape(2, 128, 4 * 128))
    vbg = np.ascontiguousarray(np.broadcast_to(f(vb_norm_g)[:, None, :], (2, 128, 512)))
    vbb = np.ascontiguousarray(np.broadcast_to(f(vb_norm_b)[:, None, :], (2, 128, 512)))
    shared = {"w_up": f(w_ffn_up), "w_dn": f(w_ffn_down), "w_in": f(w_in_ab), "w_out": f(w_out_ab), "w_ci": f(w_c_in),
              "w_co": f(w_c_out), "cf": cf, "cb": cb, "cosT": cosT, "sinT": sinT, "wsT": wsT, "bT": bT, "vbg": vbg, "vbb": vbb}
    in_maps = []
    for i in range(NCORES):
        m = dict(shared)
        m["xp"] = x_prompt[i]
        m["xs"] = np.ascontiguousarray(x_sample[NS * i:NS * (i + 1), 0, :])
        m["ck"] = np.ascontiguousarray(cache_a_k[:, NS * i:NS * (i + 1)].reshape(2, NS, WINBUF, 512))
        m["cv"] = np.ascontiguousarray(cache_a_v[:, NS * i:NS * (i + 1)].reshape(2, NS, WINBUF, 512))
        m["stc"] = np.ascontiguousarray(state_c_conv[:, NS * i:NS * (i + 1)])
        in_maps.append(m)
    res = run_bass_kernel_spmd(nc, in_maps, core_ids=list(range(NCORES)))
    R = res.results
    g = lambda name: [np.asarray(R[i][name], dtype=np.float32) for i in range(NCORES)]
    y_p = np.stack(g("y_p"), 0)
    y_s = np.concatenate(g("y_s"), 0)[:, None, :]
    ak_p = np.stack(g("ak_p"), 1).reshape(2, NCORES, SEQ, 8, 64)
    av_p = np.stack(g("av_p"), 1).reshape(2, NCORES, SEQ, 8, 64)
    ak_s = np.concatenate(g("ak_s"), 1).reshape(2, NS * NCORES, 1, 8, 64)
    av_s = np.concatenate(g("av_s"), 1).reshape(2, NS * NCORES, 1, 8, 64)
    bv_p = np.stack(g("bv_p"), 1)
    bv_s = np.concatenate(g("bv_s"), 1)[:, :, None, :]
    cc_p = np.stack(g("cc_p"), 1)
    cc_s = np.concatenate(g("cc_s"), 1)
    return (y_p, y_s, ak_p, av_p, ak_s, av_s, bv_p, bv_s, cc_p, cc_s)
```
